# Optimizing a Trainium2 kernel written in Bass

```python
import jax, jax.numpy as jnp
from jax import lax
import numpy as np

D_MODEL = 1024
BATCH = 16
SEQ = 256
DEPTH = 2
DEC_BATCH = 8
DEC_SEQ = 2048
PAST_LEN = 256

GRID_W = 64
EPS = 1e-6
MLA_HEADS = 4
Q_LORA = 384
KV_LORA = 256
QK_NOPE = 128
QK_ROPE = 64
V_HEAD = 128
MLA_WIDTH = MLA_HEADS * V_HEAD
ROPE_BASE = 10000.0
Q_BLOCK = 128
LRU_WIDTH = 512
LRU_BLOCKS = 4
LRU_BLOCK = LRU_WIDTH // LRU_BLOCKS
CONV_W = 4
CONV_LEFT = 2
LRU_C = 8.0
IN_COLS = Q_LORA + KV_LORA + QK_ROPE + 2 * LRU_WIDTH
IN_SPLITS = (Q_LORA, Q_LORA + KV_LORA, Q_LORA + KV_LORA + QK_ROPE, Q_LORA + KV_LORA + QK_ROPE + LRU_WIDTH)
POOL_WINDOWS = (2, 4, 8, 16)
POOL_GROUP = D_MODEL // len(POOL_WINDOWS)
PEER_HEADS = 8
N_KEYS = 128
N_EXPERTS = N_KEYS * N_KEYS
PEER_DKEY = 256
PEER_TOPK = 16
PEER_CHUNK = 128

kernel_name = "hybrid_diffusion_mla_rglru_pool_peer_step"


def rmsnorm(x, g):
    xf = x.astype(jnp.float32)
    y = xf * lax.rsqrt(jnp.mean(xf * xf, axis=-1, keepdims=True) + EPS)
    return (y * g.astype(jnp.float32)).astype(x.dtype)


def ada_params(cvec, w_mod, b_mod):
    m = jax.nn.silu(cvec) @ w_mod + b_mod
    return jnp.split(m[:, None, :], 6, axis=-1)


def axial_rope(n_tok):
    n_rows = n_tok // GRID_W
    rows = jnp.repeat(jnp.arange(n_rows, dtype=jnp.float32), GRID_W)
    cols = jnp.tile(jnp.arange(GRID_W, dtype=jnp.float32), n_rows)
    axis_dim = QK_ROPE // 2
    inv_freq = ROPE_BASE ** (-jnp.arange(0, axis_dim, 2, dtype=jnp.float32) / axis_dim)
    ang = jnp.concatenate([rows[:, None] * inv_freq, cols[:, None] * inv_freq], axis=-1)
    return jnp.cos(ang), jnp.sin(ang)


def apply_rope(x, cos, sin):
    xf = x.astype(jnp.float32)
    x1, x2 = xf[..., 0::2], xf[..., 1::2]
    rot = jnp.stack([x1 * cos - x2 * sin, x1 * sin + x2 * cos], axis=-1)
    return rot.reshape(x.shape).astype(x.dtype)


def mla_attention(q_nope, q_rope, k_nope, k_rope, v):
    B, Sq, H, _ = q_nope.shape
    nb = Sq // Q_BLOCK
    scale = (QK_NOPE + QK_ROPE) ** -0.5

    def to_blocks(t):
        return jnp.moveaxis(t.reshape(B, nb, Q_BLOCK, *t.shape[2:]), 1, 0)

    def one_block(qs):
        qn, qr = qs
        s = (jnp.einsum('bqhd,bkhd->bhqk', qn, k_nope).astype(jnp.float32)
             + jnp.einsum('bqhd,bkd->bhqk', qr, k_rope).astype(jnp.float32)) * scale
        p = jax.nn.softmax(s, axis=-1).astype(v.dtype)
        return jnp.einsum('bhqk,bkhd->bqhd', p, v)

    out = lax.map(one_block, (to_blocks(q_nope), to_blocks(q_rope)))
    return jnp.moveaxis(out, 0, 1).reshape(B, Sq, H * V_HEAD)


def depthwise_conv(x, w, b):
    S = x.shape[1]
    xp = jnp.pad(x, ((0, 0), (CONV_LEFT, CONV_W - 1 - CONV_LEFT), (0, 0)))
    y = xp[:, 0:S] * w[0]
    for k in range(1, CONV_W):
        y = y + xp[:, k:k + S] * w[k]
    return y + b


def rglru_coeffs(x, w_r, b_r, w_i, b_i, lam):
    B, S, _ = x.shape
    xf = x.astype(jnp.float32)
    xb = xf.reshape(B, S, LRU_BLOCKS, LRU_BLOCK)
    r = jax.nn.sigmoid(jnp.einsum('bsnc,ncd->bsnd', xb, w_r.astype(jnp.float32)).reshape(B, S, LRU_WIDTH) + b_r)
    i = jax.nn.sigmoid(jnp.einsum('bsnc,ncd->bsnd', xb, w_i.astype(jnp.float32)).reshape(B, S, LRU_WIDTH) + b_i)
    log_a = -LRU_C * r * jax.nn.softplus(-lam.astype(jnp.float32))
    a = jnp.exp(log_a)
    bx = jnp.sqrt(-jnp.expm1(2.0 * log_a)) * (i * xf)
    return a, bx


def linear_scan(a, bx, h0, reverse):
    def combine(l, r):
        a_l, b_l = l
        a_r, b_r = r
        return a_l * a_r, a_r * b_l + b_r
    A, Bc = lax.associative_scan(combine, (a, bx), axis=1, reverse=reverse)
    return A * h0[:, None, :] + Bc


def attn_lru_mixer(h, w_in, g_q, w_uq, g_kv, w_ukv, conv_w, conv_b, w_rg, b_rg, w_ig, b_ig, lam, w_o, ctx=None):
    B, S, _ = h.shape
    cq, ckv, krope, ux, ug = jnp.split(h @ w_in, IN_SPLITS, axis=-1)
    q = (rmsnorm(cq, g_q) @ w_uq).reshape(B, S, MLA_HEADS, QK_NOPE + QK_ROPE)
    q_nope, q_rope = q[..., :QK_NOPE], q[..., QK_NOPE:]
    ckv = rmsnorm(ckv, g_kv)
    if ctx is None:
        ckv_keys, krope_keys = ckv, krope
    else:
        ctx_ckv, ctx_krope, ctx_lru = ctx
        cos, sin = axial_rope(S)
        q_rope = apply_rope(q_rope, cos[:, None, :], sin[:, None, :])
        ckv_keys = jnp.concatenate([ctx_ckv.astype(ckv.dtype), ckv], axis=1)
        krope_keys = jnp.concatenate([ctx_krope.astype(krope.dtype), apply_rope(krope, cos, sin)], axis=1)
    Sk = ckv_keys.shape[1]
    kv = (ckv_keys @ w_ukv).reshape(B, Sk, MLA_HEADS, QK_NOPE + V_HEAD)
    attn = mla_attention(q_nope, q_rope, kv[..., :QK_NOPE], krope_keys, kv[..., QK_NOPE:])
    xc = depthwise_conv(ux, conv_w, conv_b)
    a_f, b_f = rglru_coeffs(xc, w_rg[0], b_rg[0], w_ig[0], b_ig[0], lam[0])
    a_b, b_b = rglru_coeffs(xc, w_rg[1], b_rg[1], w_ig[1], b_ig[1], lam[1])
    if ctx is None:
        h0_f = jnp.zeros((B, LRU_WIDTH), jnp.float32)
        h0_b = jnp.zeros((B, LRU_WIDTH), jnp.float32)
    else:
        h0_f = ctx_lru[:, 0].astype(jnp.float32)
        h0_b = ctx_lru[:, 1].astype(jnp.float32)
    h_f = linear_scan(a_f, b_f, h0_f, reverse=False)
    h_b = linear_scan(a_b, b_b, h0_b, reverse=True)
    rec = ((h_f + h_b) * jax.nn.gelu(ug.astype(jnp.float32))).astype(attn.dtype)
    out = jnp.concatenate([attn, rec], axis=-1) @ w_o
    if ctx is None:
        return out, (ckv, krope, jnp.stack([h_f[:, -1], h_b[:, 0]], axis=1))
    return out, None


def pool_mixer(h, w_pool, s_pool):
    B, S, _ = h.shape
    hf = h.astype(jnp.float32)
    cs = jnp.pad(jnp.cumsum(hf, axis=1), ((0, 0), (1, 0), (0, 0)))
    t = jnp.arange(S)
    outs = []
    for g, w in enumerate(POOL_WINDOWS):
        lo = jnp.clip(t - w // 2, 0, S)
        hi = jnp.clip(t + w - w // 2, 0, S)
        csg = cs[..., g * POOL_GROUP:(g + 1) * POOL_GROUP]
        mean = (csg[:, hi] - csg[:, lo]) / (hi - lo).astype(jnp.float32)[None, :, None]
        outs.append(mean - hf[..., g * POOL_GROUP:(g + 1) * POOL_GROUP])
    d = jnp.stack(outs, axis=2)
    y = jnp.einsum('bsgc,gcd->bsgd', d, w_pool.astype(jnp.float32)).reshape(B, S, D_MODEL)
    return (y * s_pool).astype(h.dtype)


def peer_ffn(h, w_q, sub_keys, u, v):
    B, S, D = h.shape
    xs = h.reshape((B * S) // PEER_CHUNK, PEER_CHUNK, D)
    half = PEER_DKEY // 2

    def chunk(xc):
        q = (xc @ w_q).reshape(PEER_CHUNK, PEER_HEADS, 2, half)
        s = jnp.einsum('thpc,hpnc->thpn', q, sub_keys).astype(jnp.float32)
        sv, si = lax.top_k(s, PEER_TOPK)
        cand = sv[..., 0, :, None] + sv[..., 1, None, :]
        cidx = si[..., 0, :, None] * N_KEYS + si[..., 1, None, :]
        fv, fi = lax.top_k(cand.reshape(PEER_CHUNK, PEER_HEADS, PEER_TOPK * PEER_TOPK), PEER_TOPK)
        eidx = jnp.take_along_axis(cidx.reshape(PEER_CHUNK, PEER_HEADS, PEER_TOPK * PEER_TOPK), fi, axis=-1)
        gates = jax.nn.softmax(fv, axis=-1)
        ue = jnp.take(u, eidx, axis=0)
        ve = jnp.take(v, eidx, axis=0)
        act = jax.nn.gelu(jnp.einsum('thkd,td->thk', ue, xc).astype(jnp.float32))
        return jnp.einsum('thk,thkd->td', (gates * act).astype(xc.dtype), ve)

    return lax.map(chunk, xs).reshape(B, S, D)


def setup_inputs(seed: int = 0) -> dict:
    key = jax.random.key(seed)
    ks = iter(jax.random.split(key, 64))

    def nrm(shape, s):
        return jax.random.normal(next(ks), shape, jnp.float32) * s

    def gain(n):
        return 1.0 + nrm((n,), 0.05)

    def lam_init():
        a_c = jax.random.uniform(next(ks), (2, LRU_WIDTH), jnp.float32, 0.9, 0.999)
        a = a_c ** (1.0 / LRU_C)
        return jnp.log(a) - jnp.log1p(-a)

    d_in = D_MODEL ** -0.5
    inp = {}
    inp['x_prompt'] = nrm((BATCH, SEQ, D_MODEL), 1.0)
    inp['x_sample'] = nrm((DEC_BATCH, DEC_SEQ, D_MODEL), 1.0)
    inp['cache_ckv_l0'] = nrm((DEC_BATCH, PAST_LEN, KV_LORA), 1.0)
    inp['cache_krope_l0'] = nrm((DEC_BATCH, PAST_LEN, QK_ROPE), 1.0)
    inp['state_lru_l0'] = nrm((DEC_BATCH, 2, LRU_WIDTH), 0.5)
    inp['c'] = nrm((DEC_BATCH, D_MODEL), 1.0)
    inp['c_ctx'] = nrm((D_MODEL,), 1.0)
    inp['w_mod_l0'] = nrm((D_MODEL, 6 * D_MODEL), 0.5 * d_in)
    inp['b_mod_l0'] = nrm((6 * D_MODEL,), 0.01)
    inp['w_mod_l1'] = nrm((D_MODEL, 6 * D_MODEL), 0.5 * d_in)
    inp['b_mod_l1'] = nrm((6 * D_MODEL,), 0.01)
    inp['g_mix_l0'] = gain(D_MODEL)
    inp['g_ffn_l0'] = gain(D_MODEL)
    inp['g_mix_l1'] = gain(D_MODEL)
    inp['g_ffn_l1'] = gain(D_MODEL)
    inp['w_in_l0'] = nrm((D_MODEL, IN_COLS), d_in)
    inp['g_q_l0'] = gain(Q_LORA)
    inp['w_uq_l0'] = nrm((Q_LORA, MLA_HEADS * (QK_NOPE + QK_ROPE)), Q_LORA ** -0.5)
    inp['g_kv_l0'] = gain(KV_LORA)
    inp['w_ukv_l0'] = nrm((KV_LORA, MLA_HEADS * (QK_NOPE + V_HEAD)), KV_LORA ** -0.5)
    inp['conv_w_l0'] = nrm((CONV_W, LRU_WIDTH), CONV_W ** -0.5)
    inp['conv_b_l0'] = nrm((LRU_WIDTH,), 0.01)
    inp['w_rg_l0'] = nrm((2, LRU_BLOCKS, LRU_BLOCK, LRU_BLOCK), LRU_BLOCK ** -0.5)
    inp['b_rg_l0'] = nrm((2, LRU_WIDTH), 0.01)
    inp['w_ig_l0'] = nrm((2, LRU_BLOCKS, LRU_BLOCK, LRU_BLOCK), LRU_BLOCK ** -0.5)
    inp['b_ig_l0'] = nrm((2, LRU_WIDTH), 0.01)
    inp['lam_l0'] = lam_init()
    inp['w_o_l0'] = nrm((MLA_WIDTH + LRU_WIDTH, D_MODEL), (MLA_WIDTH + LRU_WIDTH) ** -0.5)
    inp['w_pool_l1'] = nrm((len(POOL_WINDOWS), POOL_GROUP, POOL_GROUP), POOL_GROUP ** -0.5)
    inp['s_pool_l1'] = 1.0 + nrm((D_MODEL,), 0.1)
    for l in range(DEPTH):
        inp['peer_wq_l%d' % l] = nrm((D_MODEL, PEER_HEADS * PEER_DKEY), d_in)
        inp['peer_keys_l%d' % l] = nrm((PEER_HEADS, 2, N_KEYS, PEER_DKEY // 2), (PEER_DKEY // 2) ** -0.5)
        inp['peer_u_l%d' % l] = nrm((N_EXPERTS, D_MODEL), d_in)
        inp['peer_v_l%d' % l] = nrm((N_EXPERTS, D_MODEL), PEER_HEADS ** -0.5)
    inp['g_final'] = gain(D_MODEL)
    return inp


def reference(x_prompt, x_sample, cache_ckv_l0, cache_krope_l0, state_lru_l0, c, c_ctx,
              w_mod_l0, b_mod_l0, w_mod_l1, b_mod_l1,
              g_mix_l0, g_ffn_l0, g_mix_l1, g_ffn_l1,
              w_in_l0, g_q_l0, w_uq_l0, g_kv_l0, w_ukv_l0, conv_w_l0, conv_b_l0,
              w_rg_l0, b_rg_l0, w_ig_l0, b_ig_l0, lam_l0, w_o_l0,
              w_pool_l1, s_pool_l1,
              peer_wq_l0, peer_keys_l0, peer_u_l0, peer_v_l0,
              peer_wq_l1, peer_keys_l1, peer_u_l1, peer_v_l1,
              g_final):
    w_mod = (w_mod_l0, w_mod_l1)
    b_mod = (b_mod_l0, b_mod_l1)
    g_mix = (g_mix_l0, g_mix_l1)
    g_ffn = (g_ffn_l0, g_ffn_l1)
    even_mixers = ((w_in_l0, g_q_l0, w_uq_l0, g_kv_l0, w_ukv_l0, conv_w_l0, conv_b_l0,
                    w_rg_l0, b_rg_l0, w_ig_l0, b_ig_l0, lam_l0, w_o_l0),)
    odd_mixers = ((w_pool_l1, s_pool_l1),)
    peers = ((peer_wq_l0, peer_keys_l0, peer_u_l0, peer_v_l0),
             (peer_wq_l1, peer_keys_l1, peer_u_l1, peer_v_l1))

    def trunk(x, cvec, ctx_caches):
        new_state = []
        for layer in range(DEPTH):
            sh_m, sc_m, gt_m, sh_f, sc_f, gt_f = ada_params(cvec, w_mod[layer], b_mod[layer])
            h = rmsnorm(x, g_mix[layer]) * (1.0 + sc_m) + sh_m
            if layer % 2 == 0:
                ctx = None if ctx_caches is None else ctx_caches[layer // 2]
                mix, st = attn_lru_mixer(h, *even_mixers[layer // 2], ctx=ctx)
                if ctx is None:
                    new_state.append(st)
            else:
                mix = pool_mixer(h, *odd_mixers[layer // 2])
            x = x + gt_m * mix
            h = rmsnorm(x, g_ffn[layer]) * (1.0 + sc_f) + sh_f
            x = x + gt_f * peer_ffn(h, *peers[layer])
        return rmsnorm(x, g_final), new_state

    y_prompt, prompt_state = trunk(x_prompt, c_ctx[None, :], None)
    y_sample, _ = trunk(x_sample, c, ((cache_ckv_l0, cache_krope_l0, state_lru_l0),))
    new_ckv_l0, new_krope_l0, new_lru_l0 = prompt_state[0]
    return (y_prompt, y_sample, new_ckv_l0, new_krope_l0, new_lru_l0)
```

```python
import numpy as np
from contextlib import ExitStack
import concourse.bass as bass
import concourse.mybir as mybir
from concourse.bass_utils import run_bass_kernel_spmd

F32 = mybir.dt.float32
BF16 = mybir.dt.bfloat16
AF = mybir.ActivationFunctionType
ALU = mybir.AluOpType
AX = mybir.AxisListType

NCORES = 8
T = 2560
SEGS = [(0, 2048, 0, True), (2048, 256, 1, False), (2304, 256, 1, False)]
EPS = 1e-6
NV = 220
C_CS, C_CP = 0, 8
G_MIX0, G_FFN0, G_MIX1, G_FFN1, G_FIN, S_POOL = 16, 24, 32, 40, 48, 56
B_MOD0, B_MOD1 = 64, 112
G_Q, G_KV = 160, 163
CONV_W, CONV_B, B_RG, B_IG, LAM, H0 = 168, 184, 188, 196, 204, 212
ARENA = 53200
NDS = 20
ATT_SCALE = 192.0 ** -0.5


class Tl:
    def __init__(self, name, ap):
        self.name = name
        self.ap = ap

    def __getitem__(self, k):
        return self.ap[k]


def _view(ap, shape):
    if len(shape) == 1:
        return ap
    if len(shape) == 2:
        return ap.rearrange("p (a b) -> p a b", a=shape[0])
    if len(shape) == 3:
        return ap.rearrange("p (a b c) -> p a b c", a=shape[0], b=shape[1])
    if len(shape) == 4:
        return ap.rearrange("p (a b c d) -> p a b c d", a=shape[0], b=shape[1], c=shape[2])
    raise ValueError


class KB:
    def __init__(self, nc, es):
        self.nc = nc
        self.eng = dict(pe=nc.tensor, act=nc.scalar, dve=nc.vector, pool=nc.gpsimd, sp=nc.sync)
        self.sem = {e: es.enter_context(nc.semaphore("s_" + e)) for e in self.eng}
        self.cnt = {e: 0 for e in self.eng}
        self.seen = {e: {} for e in self.eng}
        self.dsem = [es.enter_context(nc.semaphore("d%d" % i)) for i in range(NDS)]
        self.dcnt = [0] * NDS
        self.dnext = 0
        self.lastw = {}
        self.readers = {}
        self.arena = es.enter_context(nc.sbuf_tensor("arena", [128, ARENA], F32))
        self.top = 0
        self.psum = es.enter_context(nc.psum_tensor("ps", [128, 4096], F32))
        self.uid = 0
        self.rr = 0
        self.bgsem = [es.enter_context(nc.semaphore("bg%d" % i)) for i in range(4)]
        self.bgcnt = [0] * 4

    def bg_cast_dma(self, si, out, in_):
        self.nc.gpsimd.dma_start(out=out, in_=in_).then_inc(self.bgsem[si], 16)
        self.bgcnt[si] += 16

    def bg_wait(self, e, si):
        self.eng[e].wait_ge(self.bgsem[si], self.bgcnt[si])

    def alloc(self, name, shape, dt=F32):
        n = int(np.prod(shape))
        words = n if dt == F32 else (n + 1) // 2
        words = (words + 15) // 16 * 16
        assert self.top + words <= ARENA, "arena overflow %s %d" % (name, self.top + words)
        ap = self.arena[:, self.top:self.top + words]
        if dt != F32:
            ap = ap.bitcast(dt)
        ap = ap[:, 0:n]
        self.top += words
        self.uid += 1
        return Tl("%s#%d" % (name, self.uid), _view(ap, shape))

    def bank(self, b, n=512, dt=F32):
        ap = self.psum[:, b * 512:(b + 1) * 512]
        if dt != F32:
            ap = ap.bitcast(dt)
        return ap[:, 0:n]

    def _wait(self, e, tok, raw):
        kind, src, n = tok
        if kind == 'e' and src == e:
            if not raw or e == 'pe' or e == 'sp':
                return
        key = (kind, src)
        if self.seen[e].get(key, 0) >= n:
            return
        sem = self.sem[src] if kind == 'e' else self.dsem[src]
        self.eng[e].wait_ge(sem, n)
        self.seen[e][key] = n

    def _keys(self, lst):
        out = []
        for k in lst:
            if isinstance(k, Tl):
                k = k.name
            out.append(k)
        return out

    def _deps(self, e, r, w):
        for k in r:
            t = self.lastw.get(k)
            if t is not None:
                self._wait(e, t, True)
        for k in w:
            t = self.lastw.get(k)
            if t is not None:
                self._wait(e, t, False)
            for (kd, src), n in self.readers.get(k, {}).items():
                self._wait(e, (kd, src, n), False)

    def _commit(self, tok, r, w):
        for k in w:
            self.lastw[k] = tok
            self.readers[k] = {}
        for k in r:
            d = self.readers.setdefault(k, {})
            d[(tok[0], tok[1])] = max(d.get((tok[0], tok[1]), 0), tok[2])

    def op(self, e, fn, r=(), w=(), inc=True):
        r = self._keys(r)
        w = self._keys(w)
        self._deps(e, r, w)
        inst = fn(self.eng[e])
        if inc:
            inst.then_inc(self.sem[e], 1)
            self.cnt[e] += 1
            self._commit(('e', e, self.cnt[e]), r, w)
        else:
            self._commit(('e', e, self.cnt[e] + 1), r, w)

    def dma(self, out, in_, r=(), w=(), q='sp'):
        r = self._keys(r)
        w = self._keys(w)
        self._deps(q, r, w)
        i = self.dnext
        self.dnext = (self.dnext + 1) % NDS
        if self.dcnt[i] > 0:
            self._wait(q, ('d', i, self.dcnt[i]), True)
        inst = self.eng[q].dma_start(out=out, in_=in_)
        inst.then_inc(self.dsem[i], 16)
        self.dcnt[i] += 16
        self._commit(('d', i, self.dcnt[i]), r, w)

    def barrier(self):
        for e in self.eng:
            for e2 in self.eng:
                if e2 != e and self.cnt[e2] > 0:
                    self._wait(e, ('e', e2, self.cnt[e2]), True)
            for i in range(NDS):
                if self.dcnt[i] > 0:
                    self._wait(e, ('d', i, self.dcnt[i]), True)
        self.lastw = {}
        self.readers = {}

    def any_eng(self):
        self.rr += 1
        return ('act', 'dve', 'pool')[self.rr % 3]

    def cast(self, e, out, in_, r, w):
        if e == 'act':
            self.op('act', lambda g: g.activation(out=out, in_=in_, func=AF.Copy), r=r, w=w)
        else:
            self.op(e, lambda g: g.tensor_copy(out=out, in_=in_), r=r, w=w)

    def load_cast(self, dst, dst_ap, src_ap, nfree, stage):
        step = 2048
        for c0 in range(0, nfree, step):
            n = min(step, nfree - c0)
            self.dma(stage.ap[:, 0:n], src_ap[:, c0:c0 + n], w=[stage])
            self.cast(self.any_eng(), dst_ap[:, c0:c0 + n], stage.ap[:, 0:n], r=[stage], w=[dst])


def build_program(debug_stage=None):
    nc = bass.Bass("TRN2", target_bir_lowering=False)
    D = {}

    def din(name, shape, dt=F32):
        D[name] = nc.dram_tensor(name, list(shape), dt, kind="ExternalInput").ap()
        return D[name]

    def dout(name, shape, dt=F32):
        D[name] = nc.dram_tensor(name, list(shape), dt, kind="ExternalOutput").ap()
        return D[name]

    def dscr(name, shape, dt=F32):
        D[name] = nc.dram_tensor(name, list(shape), dt).ap()
        return D[name]

    xT = din("xT", [1024, T])
    vecs_d = din("vecs", [128, NV])
    ident_d = din("ident", [128, 128])
    pmat_d = din("pmat", [64, 64])
    cos_d = din("cosT", [64, 2048])
    sin_d = din("sinT", [64, 2048])
    cckvT = din("cckvT", [256, 256])
    ckrT = din("ckrT", [64, 256])
    w_mod = [din("w_mod0", [1024, 6144]), din("w_mod1", [1024, 6144])]
    w_in = din("w_in", [1024, 1728])
    w_uq = din("w_uq", [384, 768])
    w_ukv = din("w_ukv", [256, 1024])
    w_rg = din("w_rg", [2, 4, 128, 128])
    w_ig = din("w_ig", [2, 4, 128, 128])
    w_o = din("w_o", [1024, 1024])
    w_pool = din("w_pool", [4, 256, 256])
    wq = [din("wq0", [1024, 2048]), din("wq1", [1024, 2048])]
    keysT = [din("keysT0", [128, 16, 128]), din("keysT1", [128, 16, 128])]
    Up = [din("Up0", [64, 128, 2048]), din("Up1", [64, 128, 2048])]
    Vp = [din("Vp0", [64, 128, 2048]), din("Vp1", [64, 128, 2048])]
    yT = dout("yT", [1024, T])
    ckv_o = dout("ckv_o", [256, 512])
    kr_o = dout("kr_o", [64, 512])
    lru_o = dout("lru_o", [128, 16])
    xres = dscr("xres", [1024, T])
    uxs = dscr("uxs", [512, T])
    gugs = dscr("gugs", [512, T], BF16)
    hTs = dscr("hTs", [1024, T], BF16)
    qTs = dscr("qTs", [128, 16, T], BF16)
    Ub = [dscr("Ub0", [64, 128, 2048], BF16), dscr("Ub1", [64, 128, 2048], BF16)]
    Vb = [dscr("Vb0", [64, 128, 2048], BF16), dscr("Vb1", [64, 128, 2048], BF16)]

    es = ExitStack()
    with es:
        k = KB(nc, es)
        ps = k.psum

        for l in range(2):
            for mi, (src, dst) in enumerate(((Up[l], Ub[l]), (Vp[l], Vb[l]))):
                for jb in range(0, 64, 4):
                    k.bg_cast_dma(l * 2 + mi, dst[jb:jb + 4].rearrange("j p f -> p j f"), src[jb:jb + 4].rearrange("j p f -> p j f"))

        vecs = k.alloc("vecs", [NV])
        identf = k.alloc("identf", [128])
        identb = k.alloc("identb", [128], BF16)
        onesf = k.alloc("onesf", [128])
        onesb = k.alloc("onesb", [128], BF16)
        epsv = k.alloc("epsv", [1])
        onev = k.alloc("onev", [1])
        der = k.alloc("der", [2, 2, 6, 8])
        nsp8 = k.alloc("nsp8", [8])
        k.dma(vecs.ap, vecs_d, w=[vecs])
        k.dma(identf.ap, ident_d, w=[identf])
        k.op('dve', lambda g: g.memset(onesf.ap, 1.0), w=[onesf])
        k.op('dve', lambda g: g.memset(epsv.ap, EPS), w=[epsv])
        k.op('dve', lambda g: g.memset(onev.ap, 1.0), w=[onev])
        k.cast('dve', onesb.ap, onesf.ap, r=[onesf], w=[onesb])
        k.cast('dve', identb.ap, identf.ap, r=[identf], w=[identb])
        k.op('act', lambda g: g.activation(out=nsp8.ap, in_=vecs.ap[:, LAM:LAM + 8], func=AF.Exp, scale=-1.0), r=[vecs], w=[nsp8])
        k.op('act', lambda g: g.activation(out=nsp8.ap, in_=nsp8.ap, func=AF.Ln, bias=onev.ap), r=[nsp8, onev], w=[nsp8])
        k.op('dve', lambda g: g.tensor_scalar(out=nsp8.ap, in0=nsp8.ap, scalar1=-8.0, scalar2=None, op0=ALU.mult), r=[nsp8], w=[nsp8])

        mark0 = k.top
        scT = k.alloc("scT", [8, 2])
        modT = k.alloc("modT", [2, 48, 2])
        k.op('act', lambda g: g.activation(out=scT.ap[:, :, 0], in_=vecs.ap[:, C_CS:C_CS + 8], func=AF.Silu), r=[vecs], w=[scT])
        k.op('act', lambda g: g.activation(out=scT.ap[:, :, 1], in_=vecs.ap[:, C_CP:C_CP + 8], func=AF.Silu), r=[vecs], w=[scT])
        wblk = [k.alloc("wblk%d" % i, [8, 512]) for i in range(2)]
        it = 0
        for l in range(2):
            wv = w_mod[l].rearrange("(kc p) n -> p kc n", p=128)
            bcol = B_MOD0 if l == 0 else B_MOD1
            for blk in range(12):
                wb = wblk[it % 2]
                k.dma(wb.ap, wv[:, :, blk * 512:(blk + 1) * 512], w=[wb])
                for cc in range(4):
                    b = (it * 4 + cc) % 8
                    pk = ('ps', b)
                    for kc in range(8):
                        k.op('pe', lambda g, wb=wb, cc=cc, kc=kc, b=b: g.matmul(
                            ps[:, b * 512:b * 512 + 2], lhsT=wb.ap[:, kc, cc * 128:(cc + 1) * 128],
                            rhs=scT.ap[:, kc, :], start=(kc == 0), stop=(kc == 7)),
                            r=[wb, scT], w=[pk])
                    ch = blk * 4 + cc
                    k.op('dve', lambda g, b=b, ch=ch, l=l, bcol=bcol: g.tensor_scalar(
                        out=modT.ap[:, l, ch, :], in0=ps[:, b * 512:b * 512 + 2],
                        scalar1=vecs.ap[:, bcol + ch:bcol + ch + 1], scalar2=None, op0=ALU.add),
                        r=[pk, vecs], w=[modT])
                it += 1
        gcols = {(0, 0): G_MIX0, (0, 3): G_FFN0, (1, 0): G_MIX1, (1, 3): G_FFN1}
        for l in range(2):
            for c in range(2):
                for (wh, sh_i, sc_i, gt_i) in ((0, 0, 1, 2), (3, 3, 4, 5)):
                    gc = gcols[(l, wh)]
                    k.op('dve', lambda g, l=l, c=c, wh=wh, sc_i=sc_i: g.tensor_scalar(
                        out=der.ap[:, l, c, wh, :], in0=modT.ap[:, l, sc_i * 8:sc_i * 8 + 8, c],
                        scalar1=1.0, scalar2=None, op0=ALU.add), r=[modT], w=[der])
                    k.op('dve', lambda g, l=l, c=c, wh=wh, gc=gc: g.tensor_tensor(
                        out=der.ap[:, l, c, wh, :], in0=der.ap[:, l, c, wh, :], in1=vecs.ap[:, gc:gc + 8],
                        op=ALU.mult), r=[der, vecs], w=[der])
                    k.op('dve', lambda g, l=l, c=c, wh=wh, sh_i=sh_i: g.tensor_copy(
                        out=der.ap[:, l, c, wh + 1, :], in_=modT.ap[:, l, sh_i * 8:sh_i * 8 + 8, c]), r=[modT], w=[der])
                    k.op('dve', lambda g, l=l, c=c, wh=wh, gt_i=gt_i: g.tensor_copy(
                        out=der.ap[:, l, c, wh + 2, :], in_=modT.ap[:, l, gt_i * 8:gt_i * 8 + 8, c]), r=[modT], w=[der])
        k.barrier()
        k.top = mark0

        def rms_rstd(xsq_ap_fn, nch, G, scale, rstd, pbank, rkeys):
            pk = ('ps', pbank)
            for c in range(nch):
                k.op('pe', lambda g, c=c: g.matmul(ps[:, pbank * 512:pbank * 512 + G], lhsT=onesf.ap,
                                                    rhs=xsq_ap_fn(c), start=(c == 0), stop=(c == nch - 1)),
                     r=rkeys + [onesf], w=[pk])
            k.op('act', lambda g: g.activation(out=rstd.ap[:, 0:G], in_=ps[:, pbank * 512:pbank * 512 + G],
                                                func=AF.Sqrt, scale=scale, bias=epsv.ap), r=[pk, epsv], w=[rstd])
            k.op('dve', lambda g: g.reciprocal(out=rstd.ap[:, 0:G], in_=rstd.ap[:, 0:G]), r=[rstd], w=[rstd])

        def norm_mod(xg, G, A_ap, B_ap, hT_out_fn, hkey, xsq, rstd, pbank):
            k.op('act', lambda g: g.activation(out=xsq.ap[:, :, 0:G], in_=xg.ap[:, :, 0:G], func=AF.Square), r=[xg], w=[xsq])
            rms_rstd(lambda c: xsq.ap[:, c, 0:G], 8, G, 1.0 / 1024.0, rstd, pbank, [xsq])
            k.op('dve', lambda g: g.tensor_tensor(out=xsq.ap[:, :, 0:G], in0=xg.ap[:, :, 0:G],
                                                  in1=rstd.ap[:, 0:G].unsqueeze(1).to_broadcast([128, 8, G]), op=ALU.mult),
                 r=[xg, rstd], w=[xsq])
            for dc in range(8):
                e = 'dve' if dc % 2 == 0 else 'pool'
                k.op(e, lambda g, dc=dc: g.tensor_scalar(out=hT_out_fn(dc), in0=xsq.ap[:, dc, 0:G],
                                                         scalar1=A_ap[:, dc:dc + 1], scalar2=B_ap[:, dc:dc + 1],
                                                         op0=ALU.mult, op1=ALU.add), r=[xsq, der], w=[hkey])

        xres_v = xres.rearrange("(dc p) t -> p dc t", p=128)
        xT_v = xT.rearrange("(dc p) t -> p dc t", p=128)
        yT_v = yT.rearrange("(dc p) t -> p dc t", p=128)
        hTs_v = hTs.rearrange("(dc p) t -> p dc t", p=128)
        uxs_v = uxs.rearrange("(n p) t -> p n t", p=128)
        gugs_v = gugs.rearrange("(n p) t -> p n t", p=128)

        markA = k.top
        stage = k.alloc("stage", [2048])
        w_uq_b = k.alloc("w_uq_b", [3, 768], BF16)
        w_ukv_b = k.alloc("w_ukv_b", [2, 1024], BF16)
        wrg_b = k.alloc("wrg_b", [8, 128], BF16)
        wig_b = k.alloc("wig_b", [8, 128], BF16)
        w_o_b = k.alloc("w_o_b", [8, 1024], BF16)
        pmat_b = k.alloc("pmat_b", [64], BF16)
        lru_t = k.alloc("lru_t", [16])
        for kc in range(3):
            k.load_cast(w_uq_b, w_uq_b.ap[:, kc, :], w_uq[kc * 128:(kc + 1) * 128, :], 768, stage)
        for kc in range(2):
            k.load_cast(w_ukv_b, w_ukv_b.ap[:, kc, :], w_ukv[kc * 128:(kc + 1) * 128, :], 1024, stage)
        for a in range(2):
            for n in range(4):
                k.load_cast(wrg_b, wrg_b.ap[:, a * 4 + n, :], w_rg[a, n], 128, stage)
                k.load_cast(wig_b, wig_b.ap[:, a * 4 + n, :], w_ig[a, n], 128, stage)
        for kc in range(8):
            k.load_cast(w_o_b, w_o_b.ap[:, kc, :], w_o[kc * 128:(kc + 1) * 128, :], 1024, stage)
        k.dma(stage.ap[0:64, 0:64], pmat_d, w=[stage])
        k.cast('dve', pmat_b.ap[0:64, :], stage.ap[0:64, 0:64], r=[stage], w=[pmat_b])
        k.op('dve', lambda g: g.memset(lru_t.ap, 0.0), w=[lru_t])
        k.barrier()
        markA2 = k.top

        for si, (t0, S, cond, has_ctx) in enumerate(SEGS):
            k.top = markA2
            G = min(512, S)
            NG = S // G
            Sk = S + (256 if has_ctx else 0)
            koff = 256 if has_ctx else 0
            NKC = Sk // 128
            A_m = der.ap[:, 0, cond, 0, :]
            B_m = der.ap[:, 0, cond, 1, :]
            G_m = der.ap[:, 0, cond, 2, :]
            attnT = k.alloc("attnT", [4, S], BF16)
            recT = k.alloc("recT", [4, S], BF16)
            qn = k.alloc("qn", [4, S], BF16)
            qr = k.alloc("qr", [4, S], BF16)
            ckvnT = k.alloc("ckvnT", [2, Sk], BF16)
            kropeT = k.alloc("kropeT", [Sk], BF16)
            markI = k.top
            G = min(256, S)
            NG = S // G
            w_in_b = k.alloc("w_in_b", [8, 1728], BF16)
            for kc in range(8):
                k.load_cast(w_in_b, w_in_b.ap[:, kc, :], w_in[kc * 128:(kc + 1) * 128, :], 1728, stage)
            xg = k.alloc("xg", [8, G])
            xsq = k.alloc("xsq", [8, G])
            rstd = k.alloc("rstd", [G])
            hTg = k.alloc("hTg", [8, G], BF16)
            cqf = k.alloc("cqf", [3, G])
            cqs = k.alloc("cqs", [3, G])
            cqn = k.alloc("cqn", [3, G], BF16)
            krf = k.alloc("krf", [G])
            krs = k.alloc("krs", [G])
            krb = k.alloc("krb", [G], BF16)
            uxt = k.alloc("uxt", [4, G])
            gut = k.alloc("gut", [4, G], BF16)
            if has_ctx:
                cosT = k.alloc("cosT", [S])
                sinT = k.alloc("sinT", [S])
                k.dma(cosT.ap[0:64, :], cos_d[:, 0:S], w=[cosT])
                k.dma(sinT.ap[0:64, :], sin_d[:, 0:S], w=[sinT])
                for kc in range(2):
                    k.dma(stage.ap[:, 0:256], cckvT[kc * 128:(kc + 1) * 128, :], w=[stage])
                    k.cast('dve', ckvnT.ap[:, kc, 0:256], stage.ap[:, 0:256], r=[stage], w=[ckvnT])
                k.dma(stage.ap[0:64, 0:256], ckrT, w=[stage])
                k.cast('dve', kropeT.ap[0:64, 0:256], stage.ap[0:64, 0:256], r=[stage], w=[kropeT])
            pb = 0

            def nb():
                nonlocal pb
                pb = (pb + 1) % 8
                return pb

            def rope(src_f, dst_bf_ap, dkey, g0, tmpf, tmpb):
                k.cast('act', tmpb.ap[0:64, 0:G], src_f.ap[0:64, 0:G], r=[src_f], w=[tmpb])
                b = nb()
                pk = ('ps', b)
                k.op('pe', lambda g: g.matmul(ps[0:64, b * 512:b * 512 + G], lhsT=pmat_b.ap[0:64, :], rhs=tmpb.ap[0:64, 0:G],
                                              start=True, stop=True), r=[pmat_b, tmpb], w=[pk])
                k.op('dve', lambda g: g.tensor_tensor(out=tmpf.ap[0:64, 0:G], in0=ps[0:64, b * 512:b * 512 + G],
                                                      in1=sinT.ap[0:64, g0:g0 + G], op=ALU.mult), r=[pk, sinT], w=[tmpf])
                k.op('dve', lambda g: g.tensor_tensor(out=src_f.ap[0:64, 0:G], in0=src_f.ap[0:64, 0:G],
                                                      in1=cosT.ap[0:64, g0:g0 + G], op=ALU.mult), r=[src_f, cosT], w=[src_f])
                k.op('dve', lambda g: g.tensor_tensor(out=dst_bf_ap, in0=src_f.ap[0:64, 0:G], in1=tmpf.ap[0:64, 0:G],
                                                      op=ALU.add), r=[src_f, tmpf], w=[dkey])

            for gi in range(NG):
                g0 = gi * G
                k.dma(xg.ap, xT_v[:, :, t0 + g0:t0 + g0 + G], w=[xg])
                norm_mod(xg, G, A_m, B_m, lambda dc: hTg.ap[:, dc, :], hTg, xsq, rstd, nb())

                def proj(c0, M):
                    b = nb()
                    pk = ('ps', b)
                    for kc in range(8):
                        k.op('pe', lambda g, kc=kc: g.matmul(ps[0:M, b * 512:b * 512 + G], lhsT=w_in_b.ap[:, kc, c0:c0 + M],
                                                             rhs=hTg.ap[:, kc, :], start=(kc == 0), stop=(kc == 7)),
                             r=[w_in_b, hTg], w=[pk])
                    return b, pk
                for c in range(3):
                    b, pk = proj(c * 128, 128)
                    k.op('act', lambda g, c=c, b=b: g.activation(out=cqf.ap[:, c, :], in_=ps[:, b * 512:b * 512 + G], func=AF.Copy), r=[pk], w=[cqf])
                k.op('act', lambda g: g.activation(out=cqs.ap, in_=cqf.ap, func=AF.Square), r=[cqf], w=[cqs])
                rms_rstd(lambda c: cqs.ap[:, c, :], 3, G, 1.0 / 384.0, rstd, nb(), [cqs])
                k.op('dve', lambda g: g.tensor_tensor(out=cqs.ap, in0=cqf.ap, in1=rstd.ap[:, 0:G].unsqueeze(1).to_broadcast([128, 3, G]),
                                                      op=ALU.mult), r=[cqf, rstd], w=[cqs])
                for c in range(3):
                    k.op('dve', lambda g, c=c: g.tensor_scalar(out=cqn.ap[:, c, :], in0=cqs.ap[:, c, :], scalar1=vecs.ap[:, G_Q + c:G_Q + c + 1],
                                                               scalar2=None, op0=ALU.mult), r=[cqs, vecs], w=[cqn])
                for h in range(4):
                    b = nb()
                    pk = ('ps', b)
                    for kc in range(3):
                        k.op('pe', lambda g, kc=kc, h=h, b=b: g.matmul(ps[:, b * 512:b * 512 + G], lhsT=w_uq_b.ap[:, kc, h * 192:h * 192 + 128],
                                                                       rhs=cqn.ap[:, kc, :], start=(kc == 0), stop=(kc == 2)),
                             r=[w_uq_b, cqn], w=[pk])
                    k.op('act', lambda g, h=h, b=b: g.activation(out=qn.ap[:, h, g0:g0 + G], in_=ps[:, b * 512:b * 512 + G], func=AF.Copy), r=[pk], w=[qn])
                    b = nb()
                    pk = ('ps', b)
                    for kc in range(3):
                        k.op('pe', lambda g, kc=kc, h=h, b=b: g.matmul(ps[0:64, b * 512:b * 512 + G], lhsT=w_uq_b.ap[:, kc, h * 192 + 128:h * 192 + 192],
                                                                       rhs=cqn.ap[:, kc, :], start=(kc == 0), stop=(kc == 2)),
                             r=[w_uq_b, cqn], w=[pk])
                    if has_ctx:
                        k.op('act', lambda g, b=b: g.activation(out=krf.ap[0:64, :], in_=ps[0:64, b * 512:b * 512 + G], func=AF.Copy), r=[pk], w=[krf])
                        rope(krf, qr.ap[0:64, h, g0:g0 + G], qr, g0, krs, krb)
                    else:
                        k.op('act', lambda g, h=h, b=b: g.activation(out=qr.ap[0:64, h, g0:g0 + G], in_=ps[0:64, b * 512:b * 512 + G], func=AF.Copy), r=[pk], w=[qr])
                for c in range(2):
                    b, pk = proj(384 + c * 128, 128)
                    k.op('act', lambda g, c=c, b=b: g.activation(out=cqf.ap[:, c, :], in_=ps[:, b * 512:b * 512 + G], func=AF.Copy), r=[pk], w=[cqf])
                k.op('act', lambda g: g.activation(out=cqs.ap[:, 0:2, :], in_=cqf.ap[:, 0:2, :], func=AF.Square), r=[cqf], w=[cqs])
                rms_rstd(lambda c: cqs.ap[:, c, :], 2, G, 1.0 / 256.0, rstd, nb(), [cqs])
                k.op('dve', lambda g: g.tensor_tensor(out=cqs.ap[:, 0:2, :], in0=cqf.ap[:, 0:2, :],
                                                      in1=rstd.ap[:, 0:G].unsqueeze(1).to_broadcast([128, 2, G]), op=ALU.mult), r=[cqf, rstd], w=[cqs])
                for c in range(2):
                    k.op('dve', lambda g, c=c: g.tensor_scalar(out=cqf.ap[:, c, :], in0=cqs.ap[:, c, :], scalar1=vecs.ap[:, G_KV + c:G_KV + c + 1],
                                                               scalar2=None, op0=ALU.mult), r=[cqs, vecs], w=[cqf])
                    k.cast('act', ckvnT.ap[:, c, koff + g0:koff + g0 + G], cqf.ap[:, c, :], r=[cqf], w=[ckvnT])
                    if not has_ctx:
                        k.dma(ckv_o[c * 128:(c + 1) * 128, (si - 1) * 256:(si - 1) * 256 + G], cqf.ap[:, c, :], r=[cqf], w=[("o", "ckv")])
                b, pk = proj(640, 64)
                k.op('act', lambda g, b=b: g.activation(out=krf.ap[0:64, :], in_=ps[0:64, b * 512:b * 512 + G], func=AF.Copy), r=[pk], w=[krf])
                if has_ctx:
                    rope(krf, kropeT.ap[0:64, koff + g0:koff + g0 + G], kropeT, g0, krs, krb)
                else:
                    k.cast('dve', kropeT.ap[0:64, g0:g0 + G], krf.ap[0:64, :], r=[krf], w=[kropeT])
                    k.dma(kr_o[:, (si - 1) * 256:(si - 1) * 256 + G], krf.ap[0:64, :], r=[krf], w=[("o", "kr")])
                for n in range(4):
                    b, pk = proj(704 + n * 128, 128)
                    k.op('act', lambda g, n=n, b=b: g.activation(out=uxt.ap[:, n, :], in_=ps[:, b * 512:b * 512 + G], func=AF.Copy), r=[pk], w=[uxt])
                    b, pk = proj(1216 + n * 128, 128)
                    k.op('act', lambda g, n=n, b=b: g.activation(out=gut.ap[:, n, :], in_=ps[:, b * 512:b * 512 + G], func=AF.Gelu_apprx_tanh), r=[pk], w=[gut])
                k.dma(uxs_v[:, :, t0 + g0:t0 + g0 + G], uxt.ap, r=[uxt], w=[("scr", "uxs")])
                k.dma(gugs_v[:, :, t0 + g0:t0 + g0 + G], gut.ap, r=[gut], w=[("scr", "gugs")])
            k.barrier()
            k.top = markI
            G = min(512, S)
            NG = S // G
            knT = k.alloc("knT", [4, Sk], BF16)
            Vt = k.alloc("Vt", [NKC, 512], BF16)
            pT = [k.alloc("pT%d" % i, [G], BF16) for i in range(2)]
            rden = k.alloc("rden", [G])
            KG = min(512, Sk)
            for h in range(4):
                for kg0 in range(0, Sk, KG):
                    kn = min(KG, Sk - kg0)
                    b = nb()
                    pk = ('ps', b)
                    for kc in range(2):
                        k.op('pe', lambda g, kc=kc, h=h, b=b, kg0=kg0, kn=kn: g.matmul(
                            ps[:, b * 512:b * 512 + kn], lhsT=w_ukv_b.ap[:, kc, h * 256:h * 256 + 128],
                            rhs=ckvnT.ap[:, kc, kg0:kg0 + kn], start=(kc == 0), stop=(kc == 1)), r=[w_ukv_b, ckvnT], w=[pk])
                    k.op('act', lambda g, h=h, b=b, kg0=kg0, kn=kn: g.activation(out=knT.ap[:, h, kg0:kg0 + kn], in_=ps[:, b * 512:b * 512 + kn], func=AF.Copy), r=[pk], w=[knT])
            for kc_ in range(NKC):
                b = nb()
                pk = ('ps', b)
                for h in range(4):
                    for kc in range(2):
                        k.op('pe', lambda g, kc=kc, h=h, b=b, kc_=kc_: g.matmul(
                            ps[:, b * 512 + h * 128:b * 512 + (h + 1) * 128], lhsT=ckvnT.ap[:, kc, kc_ * 128:(kc_ + 1) * 128],
                            rhs=w_ukv_b.ap[:, kc, h * 256 + 128:h * 256 + 256], start=(kc == 0), stop=(kc == 1)), r=[w_ukv_b, ckvnT], w=[pk])
                k.op('dve', lambda g, b=b, kc_=kc_: g.tensor_copy(out=Vt.ap[:, kc_, :], in_=ps[:, b * 512:(b + 1) * 512]), r=[pk], w=[Vt])
            for h in range(4):
                for gi in range(NG):
                    g0 = gi * G
                    bo, bd = 6, 7
                    def s_exp(kc_):
                        b = kc_ % 4
                        pk = ('ps', b)
                        p = pT[kc_ % 2]
                        k.op('pe', lambda g: g.matmul(
                            ps[:, b * 512:b * 512 + G], lhsT=knT.ap[:, h, kc_ * 128:(kc_ + 1) * 128], rhs=qn.ap[:, h, g0:g0 + G],
                            start=True, stop=False), r=[knT, qn], w=[pk])
                        k.op('pe', lambda g: g.matmul(
                            ps[:, b * 512:b * 512 + G], lhsT=kropeT.ap[0:64, kc_ * 128:(kc_ + 1) * 128], rhs=qr.ap[0:64, h, g0:g0 + G],
                            start=False, stop=True), r=[kropeT, qr], w=[pk])
                        k.op('act', lambda g: g.activation(out=p.ap, in_=ps[:, b * 512:b * 512 + G], func=AF.Exp, scale=ATT_SCALE), r=[pk], w=[p])

                    def pv(kc_):
                        p = pT[kc_ % 2]
                        k.op('pe', lambda g: g.matmul(
                            ps[:, bo * 512:bo * 512 + G], lhsT=Vt.ap[:, kc_, h * 128:(h + 1) * 128], rhs=p.ap,
                            start=(kc_ == 0), stop=(kc_ == NKC - 1)), r=[Vt, p], w=[('ps', bo)])
                        k.op('pe', lambda g: g.matmul(
                            ps[:, bd * 512:bd * 512 + G], lhsT=onesb.ap, rhs=p.ap,
                            start=(kc_ == 0), stop=(kc_ == NKC - 1)), r=[onesb, p], w=[('ps', bd)])

                    s_exp(0)
                    for kc_ in range(NKC):
                        if kc_ + 1 < NKC:
                            s_exp(kc_ + 1)
                        pv(kc_)
                    k.op('dve', lambda g: g.reciprocal(out=rden.ap, in_=ps[:, bd * 512:bd * 512 + G]), r=[('ps', bd)], w=[rden])
                    k.op('dve', lambda g, h=h, g0=g0: g.tensor_tensor(out=attnT.ap[:, h, g0:g0 + G], in0=ps[:, bo * 512:bo * 512 + G],
                                                                      in1=rden.ap, op=ALU.mult), r=[('ps', bo), rden], w=[attnT])
            k.barrier()
            k.top = markI
            uxp = k.alloc("uxp", [S + 4])
            gub = k.alloc("gub", [S], BF16)
            xc = k.alloc("xc", [S])
            xcb = k.alloc("xcb", [S], BF16)
            at = k.alloc("at", [S])
            bx = k.alloc("bx", [S])
            tmp = k.alloc("tmp", [S])
            hd = [k.alloc("hf", [S]), k.alloc("hb", [S])]
            for n in range(4):
                k.op('pool', lambda g: g.memset(uxp.ap[:, 0:2], 0.0), w=[uxp])
                k.op('pool', lambda g: g.memset(uxp.ap[:, S + 2:S + 4], 0.0), w=[uxp])
                k.dma(uxp.ap[:, 2:S + 2], uxs_v[:, n, t0:t0 + S], r=[("scr", "uxs")], w=[uxp])
                k.dma(gub.ap, gugs_v[:, n, t0:t0 + S], r=[("scr", "gugs")], w=[gub])
                cw = lambda kk: vecs.ap[:, CONV_W + kk * 4 + n:CONV_W + kk * 4 + n + 1]
                k.op('dve', lambda g: g.tensor_scalar(out=xc.ap, in0=uxp.ap[:, 0:S], scalar1=cw(0), scalar2=vecs.ap[:, CONV_B + n:CONV_B + n + 1],
                                                      op0=ALU.mult, op1=ALU.add), r=[uxp, vecs], w=[xc])
                for kk in range(1, 4):
                    k.op('dve', lambda g, kk=kk: g.scalar_tensor_tensor(out=xc.ap, in0=uxp.ap[:, kk:kk + S], scalar=cw(kk), in1=xc.ap,
                                                                        op0=ALU.mult, op1=ALU.add), r=[uxp, vecs, xc], w=[xc])
                k.cast('act', xcb.ap, xc.ap, r=[xc], w=[xcb])
                for a in range(2):
                    idx = a * 4 + n
                    for gi in range(NG):
                        g0 = gi * G
                        b = nb()
                        pk = ('ps', b)
                        k.op('pe', lambda g, b=b, g0=g0: g.matmul(ps[:, b * 512:b * 512 + G], lhsT=wrg_b.ap[:, idx, :], rhs=xcb.ap[:, g0:g0 + G],
                                                                 start=True, stop=True), r=[wrg_b, xcb], w=[pk])
                        k.op('act', lambda g, b=b, g0=g0: g.activation(out=tmp.ap[:, g0:g0 + G], in_=ps[:, b * 512:b * 512 + G], func=AF.Sigmoid,
                                                                       bias=vecs.ap[:, B_RG + idx:B_RG + idx + 1]), r=[pk, vecs], w=[tmp])
                        k.op('act', lambda g, g0=g0: g.activation(out=at.ap[:, g0:g0 + G], in_=tmp.ap[:, g0:g0 + G], func=AF.Exp,
                                                                  scale=nsp8.ap[:, idx:idx + 1]), r=[tmp, nsp8], w=[at])
                        b = nb()
                        pk = ('ps', b)
                        k.op('pe', lambda g, b=b, g0=g0: g.matmul(ps[:, b * 512:b * 512 + G], lhsT=wig_b.ap[:, idx, :], rhs=xcb.ap[:, g0:g0 + G],
                                                                 start=True, stop=True), r=[wig_b, xcb], w=[pk])
                        k.op('act', lambda g, b=b, g0=g0: g.activation(out=bx.ap[:, g0:g0 + G], in_=ps[:, b * 512:b * 512 + G], func=AF.Sigmoid,
                                                                       bias=vecs.ap[:, B_IG + idx:B_IG + idx + 1]), r=[pk, vecs], w=[bx])
                    k.op('pool', lambda g: g.tensor_tensor(out=bx.ap, in0=bx.ap, in1=xc.ap, op=ALU.mult), r=[bx, xc], w=[bx])
                    k.op('dve', lambda g: g.tensor_tensor(out=tmp.ap, in0=at.ap, in1=at.ap, op=ALU.mult), r=[at], w=[tmp])
                    k.op('dve', lambda g: g.tensor_scalar(out=tmp.ap, in0=tmp.ap, scalar1=-1.0, scalar2=1.0, op0=ALU.mult, op1=ALU.add), r=[tmp], w=[tmp])
                    k.op('act', lambda g: g.activation(out=tmp.ap, in_=tmp.ap, func=AF.Sqrt), r=[tmp], w=[tmp])
                    k.op('dve', lambda g: g.tensor_tensor(out=bx.ap, in0=bx.ap, in1=tmp.ap, op=ALU.mult), r=[bx, tmp], w=[bx])
                    init = vecs.ap[:, H0 + idx:H0 + idx + 1] if has_ctx else 0.0
                    hh = hd[a]
                    if a == 0:
                        k.op('dve', lambda g, hh=hh: g.tensor_tensor_scan(out=hh.ap, data0=at.ap, data1=bx.ap, initial=init, op0=ALU.mult, op1=ALU.add),
                             r=[at, bx, vecs], w=[hh])
                    else:
                        k.op('dve', lambda g, hh=hh: g.tensor_tensor_scan(out=hh.ap[:, ::-1], data0=at.ap[:, ::-1], data1=bx.ap[:, ::-1], initial=init,
                                                                          op0=ALU.mult, op1=ALU.add), r=[at, bx, vecs], w=[hh])
                if not has_ctx:
                    col = (si - 1) * 8
                    k.op('pool', lambda g: g.tensor_copy(out=lru_t.ap[:, col + n:col + n + 1], in_=hd[0].ap[:, S - 1:S]), r=[hd[0]], w=[lru_t])
                    k.op('pool', lambda g: g.tensor_copy(out=lru_t.ap[:, col + 4 + n:col + 4 + n + 1], in_=hd[1].ap[:, 0:1]), r=[hd[1]], w=[lru_t])
                k.op('dve', lambda g: g.tensor_tensor(out=tmp.ap, in0=hd[0].ap, in1=hd[1].ap, op=ALU.add), r=[hd[0], hd[1]], w=[tmp])
                k.op('dve', lambda g, n=n: g.tensor_tensor(out=recT.ap[:, n, :], in0=tmp.ap, in1=gub.ap, op=ALU.mult), r=[tmp, gub], w=[recT])
            k.barrier()
            k.top = markI
            xg2 = [k.alloc("xg2_%d" % i, [8, G]) for i in range(2)]
            for gi in range(NG):
                g0 = gi * G
                xo = xg2[gi % 2]
                k.dma(xo.ap, xT_v[:, :, t0 + g0:t0 + g0 + G], w=[xo])
                for oc in range(8):
                    b = nb()
                    pk = ('ps', b)
                    for kk in range(8):
                        rhs = attnT.ap[:, kk, g0:g0 + G] if kk < 4 else recT.ap[:, kk - 4, g0:g0 + G]
                        k.op('pe', lambda g, kk=kk, rhs=rhs, b=b, oc=oc: g.matmul(ps[:, b * 512:b * 512 + G], lhsT=w_o_b.ap[:, kk, oc * 128:(oc + 1) * 128],
                                                                                  rhs=rhs, start=(kk == 0), stop=(kk == 7)), r=[w_o_b, attnT, recT], w=[pk])
                    k.op('dve', lambda g, b=b, oc=oc, xo=xo: g.scalar_tensor_tensor(out=xo.ap[:, oc, :], in0=ps[:, b * 512:b * 512 + G], scalar=G_m[:, oc:oc + 1],
                                                                                    in1=xo.ap[:, oc, :], op0=ALU.mult, op1=ALU.add), r=[pk, xo, der], w=[xo])
                k.dma(xres_v[:, :, t0 + g0:t0 + g0 + G], xo.ap, r=[xo], w=[("scr", "xres")])
            k.barrier()
        k.dma(lru_o, lru_t.ap, r=[lru_t], w=[("o", "lru")])
        k.barrier()
        k.top = markA

        def peer(l, final):
            mark = k.top
            stage = k.alloc("stageP", [2048])
            wq_b = k.alloc("wq_b", [8, 2048], BF16)
            for kc in range(8):
                k.load_cast(wq_b, wq_b.ap[:, kc, :], wq[l][kc * 128:(kc + 1) * 128, :], 2048, stage)
            G = 512
            xg = k.alloc("xgP", [8, G])
            xsq = k.alloc("xsqP", [8, G])
            rstd = k.alloc("rstdP", [G])
            hTg = k.alloc("hTgP", [8, G], BF16)
            qTg = k.alloc("qTgP", [16, G], BF16)
            pb = 0
            for gi in range(T // G):
                g0 = gi * G
                cond = 0 if g0 < 2048 else 1
                k.dma(xg.ap, xres_v[:, :, g0:g0 + G], r=[("scr", "xres")], w=[xg])
                norm_mod(xg, G, der.ap[:, l, cond, 3, :], der.ap[:, l, cond, 4, :], lambda dc: hTg.ap[:, dc, :], hTg, xsq, rstd, 7)
                k.dma(hTs_v[:, :, g0:g0 + G], hTg.ap, r=[hTg], w=[("scr", "hTs")])
                for hp in range(16):
                    b = pb = (pb + 1) % 6
                    pk = ('ps', b)
                    for kc in range(8):
                        k.op('pe', lambda g, kc=kc, hp=hp, b=b: g.matmul(ps[:, b * 512:(b + 1) * 512], lhsT=wq_b.ap[:, kc, hp * 128:(hp + 1) * 128],
                                                                         rhs=hTg.ap[:, kc, :], start=(kc == 0), stop=(kc == 7)), r=[wq_b, hTg], w=[pk])
                    k.cast('act' if hp % 2 else 'dve', qTg.ap[:, hp, :], ps[:, b * 512:(b + 1) * 512], r=[pk], w=[qTg])
                k.dma(qTs[:, :, g0:g0 + G], qTg.ap, r=[qTg], w=[("scr", "qTs")])
            k.barrier()
            k.top = mark
            keys_b = k.alloc("keys_b", [16, 128], BF16)
            hT = k.alloc("hT", [2, 8, 128], BF16)
            s_sb = k.alloc("s_sb", [8, 2, 128])
            sv = k.alloc("sv", [8, 2, 16])
            fv = k.alloc("fv", [8, 16])
            ef = k.alloc("ef", [8, 16])
            sm = k.alloc("sm", [8, 8])
            th = k.alloc("th", [8, 16])
            cc = k.alloc("cc", [8, 16], BF16)
            e2 = k.alloc("e2", [8, 128], BF16)
            Rm = k.alloc("Rm", [128, 128], BF16)
            P1 = k.alloc("P1", [128, 128], BF16)
            RT = k.alloc("RT", [128, 64], BF16)
            P1T = k.alloc("P1T", [128, 64], BF16)
            W = k.alloc("W", [2, 128, 128], BF16)
            JB = 2
            NJ = 128 // JB
            HALF = NJ // 2
            ut = [k.alloc("ut%d" % i, [8, JB * 128], BF16) for i in range(3)]
            vt = [k.alloc("vt%d" % i, [JB, 1024], BF16) for i in range(2)]
            gl = [k.alloc("gl%d" % i, [JB * 128], BF16) for i in range(2)]
            wa = [k.alloc("wa%d" % i, [JB, 128], BF16) for i in range(3)]
            rstd = k.alloc("rstdT", [128])
            RT_f = RT.ap.rearrange("p a b -> p (a b)").bitcast(F32)
            P1T_f = P1T.ap.rearrange("p a b -> p (a b)").bitcast(F32)
            xt = Tl(RT.name, RT_f[:, 0:1024].rearrange("p (a b) -> p a b", a=8))
            xsq = Tl(RT.name, RT_f[:, 1024:2048].rearrange("p (a b) -> p a b", a=8))
            qT = Tl(P1T.name, P1T.ap.rearrange("p a b -> p (a b)")[:, 0:2048].rearrange("p (a b) -> p a b", a=16))
            e2f = Tl(P1T.name, P1T_f[:, 1024:2048].rearrange("p (a b) -> p a b", a=8))
            Rm_f = Rm.ap.rearrange("p a b -> p (a b)").bitcast(F32)
            work = Rm_f[:, 0:2048].rearrange("p (h q n) -> p h q n", h=8, q=2)
            cand = Rm_f[:, 2048:4096].rearrange("p (h a b) -> p h a b", h=8, a=16)
            candw = Rm_f[:, 4096:6144].rearrange("p (h a b) -> p h a b", h=8, a=16)
            k.dma(Rm_f[:, 0:2048], keysT[l].rearrange("p a b -> p (a b)"), w=[Rm])
            k.cast('dve', keys_b.ap.rearrange("p a b -> p (a b)"), Rm_f[:, 0:2048], r=[Rm], w=[keys_b])
            Ubv = Ub[l]
            Vbv = Vb[l]
            psb = ps[:, :].bitcast(BF16)
            NT = T // 128
            Rv = Rm.ap.rearrange("p j (h k) -> p j h k", h=8)
            P1v = P1.ap.rearrange("p i (h k) -> p i h k", h=8)
            s_flat = s_sb.ap.rearrange("p h q n -> p (h q n)")
            BANK1 = []

            def route_a(ti):
                t0 = ti * 128
                k.dma(qT.ap, qTs[:, :, t0:t0 + 128], r=[("scr", "qTs")], w=[qT])
                for rd in range(4):
                    for q4 in range(4):
                        hp = rd * 4 + q4
                        k.op('pe', lambda g, hp=hp, q4=q4: g.matmul(ps[:, q4 * 128:(q4 + 1) * 128], lhsT=qT.ap[:, hp, :], rhs=keys_b.ap[:, hp, :],
                                                                    start=True, stop=True), r=[qT, keys_b], w=[('ps', 0)], inc=(q4 == 3))
                    k.op('act', lambda g, rd=rd: g.activation(out=s_flat[:, rd * 512:(rd + 1) * 512], in_=ps[:, 0:512], func=AF.Copy),
                         r=[('ps', 0)], w=[('s', rd), s_sb])
                    yield 0.3
                for hp in range(16):
                    h_, p_ = hp // 2, hp % 2
                    k.op('dve', lambda g, h_=h_, p_=p_: g.max(out=sv.ap[:, h_, p_, 0:8], in_=s_sb.ap[:, h_, p_, :]), r=[s_sb], w=[('sv', hp)])
                    if hp % 2:
                        yield 0.55
                for hp in range(16):
                    h_, p_ = hp // 2, hp % 2
                    k.op('dve', lambda g, h_=h_, p_=p_: g.match_replace(out=work[:, h_, p_, :], in_to_replace=sv.ap[:, h_, p_, 0:8],
                                                                        in_values=s_sb.ap[:, h_, p_, :], imm_value=-1e30),
                         r=[s_sb, ('sv', hp)], w=[('wk', hp), Rm])
                    if hp % 2:
                        yield 0.55
                for hp in range(16):
                    h_, p_ = hp // 2, hp % 2
                    k.op('dve', lambda g, h_=h_, p_=p_: g.max(out=sv.ap[:, h_, p_, 8:16], in_=work[:, h_, p_, :]), r=[('wk', hp)], w=[('sv', hp), sv])
                    if hp % 2:
                        yield 0.55
                k.op('dve', lambda g: g.tensor_tensor(out=cand, in0=sv.ap[:, :, 0, :].unsqueeze(3).to_broadcast([128, 8, 16, 16]),
                                                      in1=sv.ap[:, :, 1, :].unsqueeze(2).to_broadcast([128, 8, 16, 16]), op=ALU.add),
                     r=[sv] + [('sv', i) for i in range(16)], w=[('cand',)])
                yield 0.6
                for h_ in range(8):
                    k.op('dve', lambda g, h_=h_: g.max(out=fv.ap[:, h_, 0:8], in_=cand[:, h_]), r=[('cand',)], w=[('fv', h_)])
                    if h_ % 2:
                        yield 0.55
                for h_ in range(8):
                    k.op('dve', lambda g, h_=h_: g.match_replace(out=candw[:, h_], in_to_replace=fv.ap[:, h_, 0:8], in_values=cand[:, h_], imm_value=-1e30),
                         r=[('cand',), ('fv', h_)], w=[('cw', h_)])
                    if h_ % 2:
                        yield 0.55
                for h_ in range(8):
                    k.op('dve', lambda g, h_=h_: g.max(out=fv.ap[:, h_, 8:16], in_=candw[:, h_]), r=[('cw', h_)], w=[('fv', h_), fv])
                    if h_ % 2:
                        yield 0.55
                fvk = [fv] + [('fv', i) for i in range(8)]
                k.op('dve', lambda g: g.tensor_tensor(out=ef.ap, in0=fv.ap, in1=fv.ap[:, :, 0:1].to_broadcast([128, 8, 16]), op=ALU.subtract), r=fvk, w=[ef])
                k.op('act', lambda g: g.activation(out=ef.ap, in_=ef.ap, func=AF.Exp), r=[ef], w=[ef])
                yield 0.6
                k.op('dve', lambda g: g.tensor_reduce(out=sm.ap[:, :, 0], in_=ef.ap, axis=AX.X, op=ALU.add), r=[ef], w=[sm])
                k.op('dve', lambda g: g.reciprocal(out=sm.ap[:, :, 1], in_=sm.ap[:, :, 0]), r=[sm], w=[sm])
                k.op('dve', lambda g: g.tensor_scalar(out=sm.ap[:, :, 2], in0=fv.ap[:, :, 15], scalar1=-1e-5, scalar2=None, op0=ALU.add), r=fvk, w=[sm])
                yield 0.6
                k.op('dve', lambda g: g.tensor_tensor(out=th.ap, in0=sm.ap[:, :, 2:3].to_broadcast([128, 8, 16]), in1=sv.ap[:, :, 0, :], op=ALU.subtract),
                     r=[sm, sv], w=[th])
                k.op('dve', lambda g: g.tensor_tensor(out=ef.ap, in0=sv.ap[:, :, 0, :], in1=sv.ap[:, :, 0, 0:1].to_broadcast([128, 8, 16]), op=ALU.subtract),
                     r=[sv], w=[ef])
                k.op('act', lambda g: g.activation(out=ef.ap, in_=ef.ap, func=AF.Exp), r=[ef], w=[ef])
                yield 0.6
                k.op('dve', lambda g: g.tensor_tensor(out=cc.ap, in0=ef.ap, in1=sm.ap[:, :, 1:2].to_broadcast([128, 8, 16]), op=ALU.mult), r=[ef, sm], w=[cc])
                k.op('dve', lambda g: g.tensor_tensor(out=e2f.ap, in0=s_sb.ap[:, :, 1, :], in1=sv.ap[:, :, 1, 0:1].to_broadcast([128, 8, 128]), op=ALU.subtract),
                     r=[s_sb, sv], w=[e2f])
                k.op('act', lambda g: g.activation(out=e2.ap, in_=e2f.ap, func=AF.Exp), r=[e2f], w=[e2])
                yield 0.6
                NCH = 8
                CJ = 128 // NCH
                alias_keys = [('cand',)] + [('cw', i) for i in range(8)] + [('wk', i) for i in range(16)]
                for c in range(NCH):
                    js = slice(c * CJ, (c + 1) * CJ)
                    k.op('dve', lambda g, js=js: g.tensor_tensor(out=Rv[:, js], in0=s_sb.ap[:, :, 1, js].rearrange("p h j -> p j h").unsqueeze(3).to_broadcast([128, CJ, 8, 16]),
                                                                 in1=th.ap.unsqueeze(1).to_broadcast([128, CJ, 8, 16]), op=ALU.is_ge),
                         r=[s_sb, th] + alias_keys, w=[('R', c)])
                    k.op('pool', lambda g, js=js: g.tensor_tensor(out=Rv[:, js], in0=Rv[:, js], in1=e2.ap[:, :, js].rearrange("p h j -> p j h").unsqueeze(3).to_broadcast([128, CJ, 8, 16]),
                                                                  op=ALU.mult), r=[('R', c), e2], w=[('R', c)])
                    k.op('pool', lambda g, js=js, c=c: g.tensor_tensor(out=Rv[:, js], in0=Rv[:, js], in1=cc.ap.unsqueeze(1).to_broadcast([128, CJ, 8, 16]), op=ALU.mult),
                         r=[('R', c), cc], w=[('R', c), Rm] if c == NCH - 1 else [('R', c)])
                    yield 4.0
                for c in range(NCH):
                    js = slice(c * CJ, (c + 1) * CJ)
                    k.op('dve', lambda g, js=js: g.tensor_tensor(out=P1v[:, js], in0=s_sb.ap[:, :, 0, js].rearrange("p h i -> p i h").unsqueeze(3).to_broadcast([128, CJ, 8, 16]),
                                                                 in1=sv.ap[:, :, 0, :].unsqueeze(1).to_broadcast([128, CJ, 8, 16]), op=ALU.is_equal),
                         r=[s_sb, sv], w=[P1])
                    yield 4.0

            def route_b(ti):
                Rkeys = [Rm] + [('R', c) for c in range(8)]
                wp = ti % 2
                for hf in range(2):
                    tp = slice(hf * 64, hf * 64 + 64)
                    for (src, dstT, rk) in ((Rm, RT, Rkeys), (P1, P1T, [P1])):
                        for rnd in range(4):
                            banks = [0, 1] if rnd % 2 == 0 else [2, 3]
                            pk = [('ps', bb) for bb in banks] + (BANK1 if rnd % 2 == 0 else [])
                            base = banks[0] * 1024
                            for jj in range(32):
                                j = rnd * 32 + jj
                                k.op('pe', lambda g, j=j, jj=jj, src=src, base=base: g.transpose(psb[:, base + jj * 64:base + (jj + 1) * 64], src.ap[tp, j, :], identb.ap[tp, tp]),
                                     r=rk + [identb], w=pk, inc=(jj == 31))
                            k.cast('act' if rnd % 2 else 'dve', dstT.ap[:, rnd * 32:(rnd + 1) * 32, :].rearrange("p j t -> p (j t)"), psb[:, base:base + 2048], r=pk, w=[dstT])
                    for rnd in range(8):
                        bb0 = 0 if rnd % 2 == 0 else 2
                        pk = [('ps', bb0), ('ps', bb0 + 1)] + (BANK1 if bb0 == 0 else [])
                        for tt in range(8):
                            tl = rnd * 8 + tt
                            k.op('pe', lambda g, tl=tl, tt=tt, bb0=bb0: g.matmul(ps[:, bb0 * 512 + tt * 128:bb0 * 512 + (tt + 1) * 128], lhsT=P1T.ap[:, :, tl], rhs=RT.ap[:, :, tl],
                                                                                 start=True, stop=True), r=[P1T, RT], w=pk, inc=(tt == 7))
                        k.cast('act' if rnd % 2 == 0 else 'dve', W.ap[:, wp, hf * 64 + rnd * 8:hf * 64 + (rnd + 1) * 8, :].rearrange("p t j -> p (t j)"),
                               ps[:, bb0 * 512:bb0 * 512 + 1024], r=pk, w=[('W', wp)])

            def tail(n):
                t0 = n * 128
                cond = 0 if t0 < 2048 else 1
                G_f = der.ap[:, l, cond, 5, :]
                ob = 4 + 2 * (n % 2)
                k.dma(xt.ap, xres_v[:, :, t0:t0 + 128], r=[("scr", "xres")], w=[xt])
                osb = xsq.ap.rearrange("p a b -> p (a b)")
                k.op('act', lambda g: g.activation(out=osb, in_=ps[:, ob * 512:(ob + 2) * 512], func=AF.Copy), r=[('ps', ob), ('ps', ob + 1)], w=[xsq])
                for dc in range(8):
                    k.op('pe', lambda g, dc=dc: g.transpose(ps[:, 1024 + dc * 128:1024 + (dc + 1) * 128], osb[:, dc * 128:(dc + 1) * 128], identf.ap),
                         r=[xsq, identf], w=[('ps', 2 + dc // 4)])
                for dc in range(8):
                    k.op('dve', lambda g, dc=dc: g.scalar_tensor_tensor(out=xt.ap[:, dc, :], in0=ps[:, 1024 + dc * 128:1024 + (dc + 1) * 128], scalar=G_f[:, dc:dc + 1],
                                                                        in1=xt.ap[:, dc, :], op0=ALU.mult, op1=ALU.add), r=[('ps', 2 + dc // 4), xt, der], w=[xt])
                if not final:
                    k.dma(xres_v[:, :, t0:t0 + 128], xt.ap, r=[xt], w=[("scr", "xres")])
                else:
                    k.op('act', lambda g: g.activation(out=xsq.ap, in_=xt.ap, func=AF.Square), r=[xt], w=[xsq])
                    rms_rstd(lambda c: xsq.ap[:, c, :], 8, 128, 1.0 / 1024.0, rstd, 0, [xsq])
                    k.op('dve', lambda g: g.tensor_tensor(out=xsq.ap, in0=xt.ap, in1=rstd.ap[:, 0:128].unsqueeze(1).to_broadcast([128, 8, 128]), op=ALU.mult),
                         r=[xt, rstd], w=[xsq])
                    k.op('dve', lambda g: g.tensor_tensor(out=xsq.ap, in0=xsq.ap, in1=vecs.ap[:, G_FIN:G_FIN + 8].unsqueeze(2).to_broadcast([128, 8, 128]), op=ALU.mult),
                         r=[xsq, vecs], w=[xsq])
                    k.dma(yT_v[:, :, t0:t0 + 128], xsq.ap, r=[xsq], w=[("o", "y")])

            def half_loop(s_, gen):
                act_t = [n for n in (s_ - 1, s_) if 0 <= n < NT]
                c0 = (s_ % 2) * HALF
                items = [(p_, n) for p_ in range(HALF) for n in act_t]
                L = len(items)

                def load_u(p_, cbase=None):
                    cb = c0 if cbase is None else cbase
                    k.dma(ut[p_ % 3].ap.rearrange("p c f -> p (c f)"), Ubv[cb + p_], r=[("scr", id(Ubv))], w=[ut[p_ % 3]])

                def load_v(p_, cbase=None):
                    cb = c0 if cbase is None else cbase
                    k.dma(vt[p_ % 2].ap.rearrange("p j f -> p (j f)"), Vbv[cb + p_], r=[("scr", id(Vbv))], w=[vt[p_ % 2]], q='act')

                def prologue(sn):
                    cbn = (sn % 2) * HALF
                    if sn < NT:
                        k.dma(hT.ap[:, sn % 2], hTs_v[:, :, sn * 128:(sn + 1) * 128], r=[("scr", "hTs")], w=[('hT', sn % 2)])
                    load_u(0, cbn)
                    load_v(0, cbn)
                    load_u(1, cbn)
                    load_v(1, cbn)

                def m1(qi):
                    p_, n = items[qi]
                    u, gg, ba = ut[p_ % 3], gl[qi % 2], 2 + (qi % 2)
                    pk = ('ps', ba)
                    for dc in range(8):
                        k.op('pe', lambda g, dc=dc: g.matmul(ps[:, ba * 512:ba * 512 + JB * 128], lhsT=hT.ap[:, n % 2, dc, :], rhs=u.ap[:, dc, :],
                                                             start=(dc == 0), stop=(dc == 7)), r=[u, ('hT', n % 2)], w=[pk], inc=(dc == 7))
                    k.op('act', lambda g: g.activation(out=gg.ap, in_=ps[:, ba * 512:ba * 512 + JB * 128], func=AF.Gelu_apprx_tanh), r=[pk], w=[gg])

                def tr(qi):
                    p_, n = items[qi]
                    gg, ww = gl[qi % 2], wa[qi % 3]
                    off = (qi % 2) * 1024
                    pkt = ('ps', qi % 2)
                    wk_ = [pkt]
                    for jj in range(JB):
                        k.op('pe', lambda g, jj=jj: g.transpose(psb[:, off + jj * 128:off + (jj + 1) * 128], gg.ap[:, jj * 128:(jj + 1) * 128], identb.ap),
                             r=[gg, identb], w=wk_, inc=(jj == JB - 1))
                    jg = (c0 + p_) * JB
                    k.op('dve', lambda g: g.tensor_tensor(out=ww.ap, in0=psb[:, off:off + JB * 128].rearrange("p (j t) -> p j t", j=JB),
                                                          in1=W.ap[:, n % 2, :, jg:jg + JB].rearrange("p t j -> p j t"), op=ALU.mult),
                         r=[pkt, ('W', n % 2)], w=[ww])

                def m2(qi):
                    p_, n = items[qi]
                    v_, ww = vt[p_ % 2], wa[qi % 3]
                    first = (n == s_) and p_ == 0
                    last = (n == s_ - 1 or NT == 1) and p_ == HALF - 1
                    for jj in range(JB):
                        for hh in range(2):
                            bo = 4 + 2 * (n % 2) + hh
                            k.op('pe', lambda g, jj=jj, hh=hh, bo=bo: g.matmul(
                                ps[:, bo * 512:(bo + 1) * 512], lhsT=ww.ap[:, jj, :], rhs=v_.ap[:, jj, hh * 512:(hh + 1) * 512],
                                start=(first and jj == 0), stop=(last and jj == JB - 1)), r=[v_, ww], w=[('ps', bo)], inc=(jj == JB - 1 and hh == 1))

                budget = 0.0
                spent = 0.0
                if s_ == 0:
                    prologue(0)
                m1(0)
                for qi in range(L):
                    p_, n = items[qi]
                    tr(qi)
                    if n == act_t[0] and p_ + 2 < HALF:
                        load_u(p_ + 2)
                    if qi + 1 < L:
                        m1(qi + 1)
                    if qi >= 1:
                        m2(qi - 1)
                        pp, pn = items[qi - 1]
                        if pn == act_t[-1] and pp + 2 < HALF:
                            load_v(pp + 2)
                    if gen is not None:
                        budget += 2.9 * 2.0 / len(act_t) / 2.0
                        while spent < budget:
                            cst_ = next(gen, None)
                            if cst_ is None:
                                break
                            spent += cst_
                m2(L - 1)
                if gen is not None:
                    for _ in gen:
                        pass
                if s_ + 1 <= NT:
                    prologue(s_ + 1)

            k.bg_wait('sp', l * 2)
            k.bg_wait('sp', l * 2 + 1)
            k.bg_wait('act', l * 2 + 1)
            g0 = route_a(0)
            for _ in g0:
                pass
            route_b(0)
            for s_ in range(NT + 1):
                gen = route_a(s_ + 1) if s_ + 1 < NT else None
                half_loop(s_, gen)
                if s_ - 1 >= 0:
                    tail(s_ - 1)
                if s_ + 1 < NT:
                    route_b(s_ + 1)
            k.barrier()
            k.top = mark

        if debug_stage != "A":
            peer(0, False)

        if debug_stage not in ("A", "P0"):
            mark = k.top
            stage = k.alloc("stageQ", [2048])
            wp_b = k.alloc("wp_b", [4, 2, 256], BF16)
            for gq in range(4):
                for kc in range(2):
                    k.load_cast(wp_b, wp_b.ap[:, gq, kc, :], w_pool[gq, kc * 128:(kc + 1) * 128, :], 256, stage)
            PADW = 8
            for si, (t0, S, cond, has_ctx) in enumerate(SEGS):
                markS = k.top
                A_m = der.ap[:, 1, cond, 0, :]
                B_m = der.ap[:, 1, cond, 1, :]
                G_m = der.ap[:, 1, cond, 2, :]
                G = min(512, S)
                NG = S // G
                hp_ = k.alloc("hpad", [8, S + 2 * PADW])
                dT = k.alloc("dT", [8, S], BF16)
                inv = k.alloc("inv", [4, S])
                xg = k.alloc("xgQ", [8, G])
                xsq = k.alloc("xsqQ", [8, G])
                rstd = k.alloc("rstdQ", [G])
                t1 = k.alloc("t1", [S + 2 * PADW])
                t2 = k.alloc("t2", [S + 2 * PADW])
                k.op('pool', lambda g: g.memset(hp_.ap, 0.0), w=[hp_])
                for gq, wdw in enumerate((2, 4, 8, 16)):
                    k.op('pool', lambda g, gq=gq, wdw=wdw: g.memset(inv.ap[:, gq, :], 1.0 / wdw), w=[inv])
                    lo_h, hi_h = wdw // 2, wdw - wdw // 2
                    for t in range(0, lo_h):
                        cnt = min(t + hi_h, S) - max(t - lo_h, 0)
                        k.op('pool', lambda g, gq=gq, t=t, cnt=cnt: g.memset(inv.ap[:, gq, t:t + 1], 1.0 / cnt), w=[inv])
                    for t in range(S - hi_h + 1, S):
                        cnt = min(t + hi_h, S) - max(t - lo_h, 0)
                        k.op('pool', lambda g, gq=gq, t=t, cnt=cnt: g.memset(inv.ap[:, gq, t:t + 1], 1.0 / cnt), w=[inv])
                for gi in range(NG):
                    g0 = gi * G
                    k.dma(xg.ap, xres_v[:, :, t0 + g0:t0 + g0 + G], r=[("scr", "xres")], w=[xg])
                    norm_mod(xg, G, A_m, B_m, lambda dc: hp_.ap[:, dc, PADW + g0:PADW + g0 + G], hp_, xsq, rstd, gi % 8)
                for dc in range(8):
                    gq = dc // 2
                    wdw = (2, 4, 8, 16)[gq]
                    L = S + 2 * PADW
                    src = hp_.ap[:, dc, :]
                    cur, n = src, 1
                    bufs = [t1, t2]
                    bi = 0
                    while n < wdw:
                        o = bufs[bi]
                        bi ^= 1
                        k.op('dve', lambda g, o=o, cur=cur, n=n, L=L: g.tensor_tensor(out=o.ap[:, 0:L - n], in0=cur[:, 0:L - n], in1=cur[:, n:L], op=ALU.add),
                             r=[hp_, t1, t2], w=[o])
                        cur = o.ap
                        n *= 2
                    st = PADW - wdw // 2
                    k.op('dve', lambda g, cur=cur, st=st, gq=gq: g.tensor_tensor(out=t1.ap[:, 0:S] if cur is not t1.ap else t2.ap[:, 0:S], in0=cur[:, st:st + S], in1=inv.ap[:, gq, :], op=ALU.mult),
                         r=[t1, t2, inv], w=[t1, t2])
                    mres = t1.ap if cur is not t1.ap else t2.ap
                    k.op('dve', lambda g, dc=dc, mres=mres: g.tensor_tensor(out=dT.ap[:, dc, :], in0=mres[:, 0:S], in1=hp_.ap[:, dc, PADW:PADW + S], op=ALU.subtract),
                         r=[t1, t2, hp_], w=[dT])
                xo2 = [xg, xg]
                pb2 = 0
                for gi in range(NG):
                    g0 = gi * G
                    xo = xo2[gi % 2]
                    k.dma(xo.ap, xres_v[:, :, t0 + g0:t0 + g0 + G], r=[("scr", "xres")], w=[xo])
                    for oc in range(8):
                        gq, dd = oc // 2, oc % 2
                        b = pb2 = (pb2 + 1) % 8
                        pk = ('ps', b)
                        for kc in range(2):
                            k.op('pe', lambda g, kc=kc, gq=gq, dd=dd, b=b: g.matmul(ps[:, b * 512:b * 512 + G], lhsT=wp_b.ap[:, gq, kc, dd * 128:(dd + 1) * 128],
                                                                                    rhs=dT.ap[:, gq * 2 + kc, g0:g0 + G], start=(kc == 0), stop=(kc == 1)), r=[wp_b, dT], w=[pk])
                        k.op('act', lambda g, b=b, oc=oc: g.activation(out=xsq.ap[:, 0, 0:G], in_=ps[:, b * 512:b * 512 + G], func=AF.Copy,
                                                                       scale=vecs.ap[:, S_POOL + oc:S_POOL + oc + 1]), r=[pk, vecs], w=[xsq])
                        k.op('dve', lambda g, oc=oc, xo=xo: g.scalar_tensor_tensor(out=xo.ap[:, oc, :], in0=xsq.ap[:, 0, 0:G], scalar=G_m[:, oc:oc + 1], in1=xo.ap[:, oc, :],
                                                                                   op0=ALU.mult, op1=ALU.add), r=[xsq, xo, der], w=[xo])
                    k.dma(xres_v[:, :, t0 + g0:t0 + g0 + G], xo.ap, r=[xo], w=[("scr", "xres")])
                k.barrier()
                k.top = markS
            k.barrier()
            k.top = mark
            if debug_stage != "P1":
                peer(1, True)

        if debug_stage is not None:
            mark = k.top
            xg = k.alloc("xdump", [8, 512])
            for gi in range(T // 512):
                k.dma(xg.ap, xres_v[:, :, gi * 512:(gi + 1) * 512], r=[("scr", "xres")], w=[xg])
                k.dma(yT_v[:, :, gi * 512:(gi + 1) * 512], xg.ap, r=[xg], w=[("o", "y")])
        k.barrier()
    return nc


_PROG = {}


def _pack_vec(v):
    v = np.asarray(v, np.float32).reshape(-1)
    return v.reshape(-1, 128).T


def kernel(**inp):
    debug_stage = inp.pop("_debug_stage", None)
    _trace = inp.pop("_trace", False)
    f = lambda a: np.ascontiguousarray(np.asarray(a, dtype=np.float32))
    x_prompt, x_sample = f(inp["x_prompt"]), f(inp["x_sample"])
    c, c_ctx = f(inp["c"]), f(inp["c_ctx"])
    if debug_stage not in _PROG:
        _PROG[debug_stage] = build_program(debug_stage)
    nc = _PROG[debug_stage]
    ident = np.eye(128, dtype=np.float32)
    pmat = np.zeros((64, 64), np.float32)
    for i in range(32):
        pmat[2 * i + 1, 2 * i] = -1.0
        pmat[2 * i, 2 * i + 1] = 1.0
    n_tok = 2048
    rows = np.repeat(np.arange(n_tok // 64, dtype=np.float32), 64)
    cols = np.tile(np.arange(64, dtype=np.float32), n_tok // 64)
    inv_freq = (np.float32(10000.0) ** (-np.arange(0, 32, 2, dtype=np.float32) / np.float32(32))).astype(np.float32)
    ang = np.concatenate([rows[:, None] * inv_freq, cols[:, None] * inv_freq], axis=-1).astype(np.float32)
    cosT = np.ascontiguousarray(np.repeat(np.cos(ang).T, 2, axis=0).astype(np.float32))
    sinT = np.ascontiguousarray(np.repeat(np.sin(ang).T, 2, axis=0).astype(np.float32))
    shared = {
        "ident": ident, "pmat": pmat, "cosT": cosT, "sinT": sinT,
        "w_mod0": f(inp["w_mod_l0"]), "w_mod1": f(inp["w_mod_l1"]),
        "w_in": f(inp["w_in_l0"]), "w_uq": f(inp["w_uq_l0"]), "w_ukv": f(inp["w_ukv_l0"]),
        "w_rg": f(inp["w_rg_l0"]), "w_ig": f(inp["w_ig_l0"]), "w_o": f(inp["w_o_l0"]),
        "w_pool": f(inp["w_pool_l1"]),
        "wq0": f(inp["peer_wq_l0"]), "wq1": f(inp["peer_wq_l1"]),
    }
    peer_in = ((inp["peer_keys_l0"], inp["peer_u_l0"], inp["peer_v_l0"]), (inp["peer_keys_l1"], inp["peer_u_l1"], inp["peer_v_l1"]))
    for l in range(2):
        keys = f(peer_in[l][0])
        shared["keysT%d" % l] = np.ascontiguousarray(keys.reshape(16, 128, 128).transpose(2, 0, 1))
        u = f(peer_in[l][1])
        v = f(peer_in[l][2])
        shared["Up%d" % l] = np.ascontiguousarray(u.reshape(128, 64, 2, 8, 128).transpose(1, 4, 3, 2, 0)).reshape(64, 128, 2048)
        shared["Vp%d" % l] = np.ascontiguousarray(v.reshape(128, 64, 2, 1024).transpose(1, 0, 2, 3)).reshape(64, 128, 2048)
    in_maps = []
    for core in range(NCORES):
        xcat = np.concatenate([x_sample[core], x_prompt[2 * core], x_prompt[2 * core + 1]], axis=0)
        vecs = np.zeros((128, NV), np.float32)
        vecs[:, C_CS:C_CS + 8] = _pack_vec(c[core])
        vecs[:, C_CP:C_CP + 8] = _pack_vec(c_ctx)
        for col, name in ((G_MIX0, "g_mix_l0"), (G_FFN0, "g_ffn_l0"), (G_MIX1, "g_mix_l1"), (G_FFN1, "g_ffn_l1"), (G_FIN, "g_final"), (S_POOL, "s_pool_l1")):
            vecs[:, col:col + 8] = _pack_vec(inp[name])
        vecs[:, B_MOD0:B_MOD0 + 48] = _pack_vec(inp["b_mod_l0"])
        vecs[:, B_MOD1:B_MOD1 + 48] = _pack_vec(inp["b_mod_l1"])
        vecs[:, G_Q:G_Q + 3] = _pack_vec(inp["g_q_l0"])
        vecs[:, G_KV:G_KV + 2] = _pack_vec(inp["g_kv_l0"])
        cw = f(inp["conv_w_l0"])
        for kk in range(4):
            vecs[:, CONV_W + kk * 4:CONV_W + kk * 4 + 4] = _pack_vec(cw[kk])
        vecs[:, CONV_B:CONV_B + 4] = _pack_vec(inp["conv_b_l0"])
        vecs[:, B_RG:B_RG + 8] = _pack_vec(inp["b_rg_l0"])
        vecs[:, B_IG:B_IG + 8] = _pack_vec(inp["b_ig_l0"])
        vecs[:, LAM:LAM + 8] = _pack_vec(inp["lam_l0"])
        vecs[:, H0:H0 + 8] = _pack_vec(f(inp["state_lru_l0"])[core])
        m = dict(shared)
        m["xT"] = np.ascontiguousarray(xcat.T)
        m["vecs"] = vecs
        m["cckvT"] = np.ascontiguousarray(f(inp["cache_ckv_l0"])[core].T)
        m["ckrT"] = np.ascontiguousarray(f(inp["cache_krope_l0"])[core].T)
        in_maps.append(m)
    if _trace:
        res = run_bass_kernel_spmd(nc, in_maps, core_ids=list(range(NCORES)), trace=True)
        print("EXEC_TIME_NS", res.exec_time_ns)
    else:
        res = run_bass_kernel_spmd(nc, in_maps, core_ids=list(range(NCORES)))
    y_prompt = np.zeros((16, 256, 1024), np.float32)
    y_sample = np.zeros((8, 2048, 1024), np.float32)
    new_ckv = np.zeros((16, 256, 256), np.float32)
    new_kr = np.zeros((16, 256, 64), np.float32)
    new_lru = np.zeros((16, 2, 512), np.float32)
    for core in range(NCORES):
        r = res.results[core]
        y = np.asarray(r["yT"]).T
        y_sample[core] = y[0:2048]
        y_prompt[2 * core] = y[2048:2304]
        y_prompt[2 * core + 1] = y[2304:2560]
        ck = np.asarray(r["ckv_o"]).T
        kr = np.asarray(r["kr_o"]).T
        lr = np.asarray(r["lru_o"])
        for bi in range(2):
            new_ckv[2 * core + bi] = ck[bi * 256:(bi + 1) * 256]
            new_kr[2 * core + bi] = kr[bi * 256:(bi + 1) * 256]
            new_lru[2 * core + bi] = lr[:, bi * 8:(bi + 1) * 8].reshape(128, 2, 4).transpose(1, 2, 0).reshape(2, 512)
    return (y_prompt, y_sample, new_ckv, new_kr, new_lru)
```

```python
import numpy as np
from contextlib import ExitStack
import concourse.bass as bass
import concourse.mybir as mybir
from concourse.bass_utils import run_bass_kernel_spmd

F32 = mybir.dt.float32
BF16 = mybir.dt.bfloat16
AF = mybir.ActivationFunctionType
ALU = mybir.AluOpType
AX = mybir.AxisListType

NCORES = 8
T = 2560
SEGS = [(0, 2048, 0, True), (2048, 256, 1, False), (2304, 256, 1, False)]
EPS = 1e-6
NV = 220
C_CS, C_CP = 0, 8
G_MIX0, G_FFN0, G_MIX1, G_FFN1, G_FIN, S_POOL = 16, 24, 32, 40, 48, 56
B_MOD0, B_MOD1 = 64, 112
G_Q, G_KV = 160, 163
CONV_W, CONV_B, B_RG, B_IG, LAM, H0 = 168, 184, 188, 196, 204, 212
ARENA = 53200
NDS = 20
ATT_SCALE = 192.0 ** -0.5


class Tl:
    def __init__(self, name, ap):
        self.name = name
        self.ap = ap

    def __getitem__(self, k):
        return self.ap[k]


def _view(ap, shape):
    if len(shape) == 1:
        return ap
    if len(shape) == 2:
        return ap.rearrange("p (a b) -> p a b", a=shape[0])
    if len(shape) == 3:
        return ap.rearrange("p (a b c) -> p a b c", a=shape[0], b=shape[1])
    if len(shape) == 4:
        return ap.rearrange("p (a b c d) -> p a b c d", a=shape[0], b=shape[1], c=shape[2])
    raise ValueError


class KB:
    def __init__(self, nc, es):
        self.nc = nc
        self.eng = dict(pe=nc.tensor, act=nc.scalar, dve=nc.vector, pool=nc.gpsimd, sp=nc.sync)
        self.sem = {e: es.enter_context(nc.semaphore("s_" + e)) for e in self.eng}
        self.cnt = {e: 0 for e in self.eng}
        self.seen = {e: {} for e in self.eng}
        self.dsem = [es.enter_context(nc.semaphore("d%d" % i)) for i in range(NDS)]
        self.dcnt = [0] * NDS
        self.dnext = 0
        self.lastw = {}
        self.readers = {}
        self.arena = es.enter_context(nc.sbuf_tensor("arena", [128, ARENA], F32))
        self.top = 0
        self.psum = es.enter_context(nc.psum_tensor("ps", [128, 4096], F32))
        self.uid = 0
        self.rr = 0
        self.bgsem = [es.enter_context(nc.semaphore("bg%d" % i)) for i in range(4)]
        self.bgcnt = [0] * 4

    def bg_cast_dma(self, si, out, in_):
        self.nc.gpsimd.dma_start(out=out, in_=in_).then_inc(self.bgsem[si], 16)
        self.bgcnt[si] += 16

    def bg_wait(self, e, si):
        self.eng[e].wait_ge(self.bgsem[si], self.bgcnt[si])

    def alloc(self, name, shape, dt=F32):
        n = int(np.prod(shape))
        words = n if dt == F32 else (n + 1) // 2
        words = (words + 15) // 16 * 16
        assert self.top + words <= ARENA, "arena overflow %s %d" % (name, self.top + words)
        ap = self.arena[:, self.top:self.top + words]
        if dt != F32:
            ap = ap.bitcast(dt)
        ap = ap[:, 0:n]
        self.top += words
        self.uid += 1
        return Tl("%s#%d" % (name, self.uid), _view(ap, shape))

    def bank(self, b, n=512, dt=F32):
        ap = self.psum[:, b * 512:(b + 1) * 512]
        if dt != F32:
            ap = ap.bitcast(dt)
        return ap[:, 0:n]

    def _wait(self, e, tok, raw):
        kind, src, n = tok
        if kind == 'e' and src == e:
            if not raw or e == 'pe' or e == 'sp':
                return
        key = (kind, src)
        if self.seen[e].get(key, 0) >= n:
            return
        sem = self.sem[src] if kind == 'e' else self.dsem[src]
        self.eng[e].wait_ge(sem, n)
        self.seen[e][key] = n

    def _keys(self, lst):
        out = []
        for k in lst:
            if isinstance(k, Tl):
                k = k.name
            out.append(k)
        return out

    def _deps(self, e, r, w):
        for k in r:
            t = self.lastw.get(k)
            if t is not None:
                self._wait(e, t, True)
        for k in w:
            t = self.lastw.get(k)
            if t is not None:
                self._wait(e, t, False)
            for (kd, src), n in self.readers.get(k, {}).items():
                self._wait(e, (kd, src, n), False)

    def _commit(self, tok, r, w):
        for k in w:
            self.lastw[k] = tok
            self.readers[k] = {}
        for k in r:
            d = self.readers.setdefault(k, {})
            d[(tok[0], tok[1])] = max(d.get((tok[0], tok[1]), 0), tok[2])

    def op(self, e, fn, r=(), w=(), inc=True):
        r = self._keys(r)
        w = self._keys(w)
        self._deps(e, r, w)
        inst = fn(self.eng[e])
        if inc:
            inst.then_inc(self.sem[e], 1)
            self.cnt[e] += 1
            self._commit(('e', e, self.cnt[e]), r, w)
        else:
            self._commit(('e', e, self.cnt[e] + 1), r, w)

    def dma(self, out, in_, r=(), w=(), q='sp'):
        r = self._keys(r)
        w = self._keys(w)
        self._deps(q, r, w)
        i = self.dnext
        self.dnext = (self.dnext + 1) % NDS
        if self.dcnt[i] > 0:
            self._wait(q, ('d', i, self.dcnt[i]), True)
        inst = self.eng[q].dma_start(out=out, in_=in_)
        inst.then_inc(self.dsem[i], 16)
        self.dcnt[i] += 16
        self._commit(('d', i, self.dcnt[i]), r, w)

    def barrier(self):
        for e in self.eng:
            for e2 in self.eng:
                if e2 != e and self.cnt[e2] > 0:
                    self._wait(e, ('e', e2, self.cnt[e2]), True)
            for i in range(NDS):
                if self.dcnt[i] > 0:
                    self._wait(e, ('d', i, self.dcnt[i]), True)
        self.lastw = {}
        self.readers = {}

    def any_eng(self):
        self.rr += 1
        return ('act', 'dve', 'pool')[self.rr % 3]

    def cast(self, e, out, in_, r, w):
        if e == 'act':
            self.op('act', lambda g: g.activation(out=out, in_=in_, func=AF.Copy), r=r, w=w)
        else:
            self.op(e, lambda g: g.tensor_copy(out=out, in_=in_), r=r, w=w)

    def load_cast(self, dst, dst_ap, src_ap, nfree, stage):
        step = 2048
        for c0 in range(0, nfree, step):
            n = min(step, nfree - c0)
            self.dma(stage.ap[:, 0:n], src_ap[:, c0:c0 + n], w=[stage])
            self.cast(self.any_eng(), dst_ap[:, c0:c0 + n], stage.ap[:, 0:n], r=[stage], w=[dst])


def build_program(debug_stage=None):
    nc = bass.Bass("TRN2", target_bir_lowering=False)
    D = {}

    def din(name, shape, dt=F32):
        D[name] = nc.dram_tensor(name, list(shape), dt, kind="ExternalInput").ap()
        return D[name]

    def dout(name, shape, dt=F32):
        D[name] = nc.dram_tensor(name, list(shape), dt, kind="ExternalOutput").ap()
        return D[name]

    def dscr(name, shape, dt=F32):
        D[name] = nc.dram_tensor(name, list(shape), dt).ap()
        return D[name]

    xT = din("xT", [1024, T])
    vecs_d = din("vecs", [128, NV])
    ident_d = din("ident", [128, 128])
    pmat_d = din("pmat", [64, 64])
    cos_d = din("cosT", [64, 2048])
    sin_d = din("sinT", [64, 2048])
    cckvT = din("cckvT", [256, 256])
    ckrT = din("ckrT", [64, 256])
    w_mod = [din("w_mod0", [1024, 6144]), din("w_mod1", [1024, 6144])]
    w_in = din("w_in", [1024, 1728])
    w_uq = din("w_uq", [384, 768])
    w_ukv = din("w_ukv", [256, 1024])
    w_rg = din("w_rg", [2, 4, 128, 128])
    w_ig = din("w_ig", [2, 4, 128, 128])
    w_o = din("w_o", [1024, 1024])
    w_pool = din("w_pool", [4, 256, 256])
    wq = [din("wq0", [1024, 2048]), din("wq1", [1024, 2048])]
    keysT = [din("keysT0", [128, 16, 128]), din("keysT1", [128, 16, 128])]
    Up = [din("Up0", [64, 128, 2048]), din("Up1", [64, 128, 2048])]
    Vp = [din("Vp0", [64, 128, 2048]), din("Vp1", [64, 128, 2048])]
    yT = dout("yT", [1024, T])
    ckv_o = dout("ckv_o", [256, 512])
    kr_o = dout("kr_o", [64, 512])
    lru_o = dout("lru_o", [128, 16])
    xres = dscr("xres", [1024, T])
    uxs = dscr("uxs", [512, T])
    gugs = dscr("gugs", [512, T], BF16)
    hTs = dscr("hTs", [1024, T], BF16)
    qTs = dscr("qTs", [128, 16, T], BF16)
    Ub = [dscr("Ub0", [64, 128, 2048], BF16), dscr("Ub1", [64, 128, 2048], BF16)]
    Vb = [dscr("Vb0", [64, 128, 2048], BF16), dscr("Vb1", [64, 128, 2048], BF16)]

    es = ExitStack()
    with es:
        k = KB(nc, es)
        ps = k.psum

        vecs = k.alloc("vecs", [NV])
        identf = k.alloc("identf", [128])
        identb = k.alloc("identb", [128], BF16)
        onesf = k.alloc("onesf", [128])
        onesb = k.alloc("onesb", [128], BF16)
        epsv = k.alloc("epsv", [1])
        onev = k.alloc("onev", [1])
        der = k.alloc("der", [2, 2, 6, 8])
        nsp8 = k.alloc("nsp8", [8])
        k.dma(vecs.ap, vecs_d, w=[vecs])
        k.dma(identf.ap, ident_d, w=[identf])
        k.op('dve', lambda g: g.memset(onesf.ap, 1.0), w=[onesf])
        k.op('dve', lambda g: g.memset(epsv.ap, EPS), w=[epsv])
        k.op('dve', lambda g: g.memset(onev.ap, 1.0), w=[onev])
        k.cast('dve', onesb.ap, onesf.ap, r=[onesf], w=[onesb])
        k.cast('dve', identb.ap, identf.ap, r=[identf], w=[identb])
        k.op('act', lambda g: g.activation(out=nsp8.ap, in_=vecs.ap[:, LAM:LAM + 8], func=AF.Exp, scale=-1.0), r=[vecs], w=[nsp8])
        k.op('act', lambda g: g.activation(out=nsp8.ap, in_=nsp8.ap, func=AF.Ln, bias=onev.ap), r=[nsp8, onev], w=[nsp8])
        k.op('dve', lambda g: g.tensor_scalar(out=nsp8.ap, in0=nsp8.ap, scalar1=-8.0, scalar2=None, op0=ALU.mult), r=[nsp8], w=[nsp8])

        mark0 = k.top
        scT = k.alloc("scT", [8, 2])
        modT = k.alloc("modT", [2, 48, 2])
        k.op('act', lambda g: g.activation(out=scT.ap[:, :, 0], in_=vecs.ap[:, C_CS:C_CS + 8], func=AF.Silu), r=[vecs], w=[scT])
        k.op('act', lambda g: g.activation(out=scT.ap[:, :, 1], in_=vecs.ap[:, C_CP:C_CP + 8], func=AF.Silu), r=[vecs], w=[scT])
        wblk = [k.alloc("wblk%d" % i, [8, 512]) for i in range(2)]
        it = 0
        for l in range(2):
            wv = w_mod[l].rearrange("(kc p) n -> p kc n", p=128)
            bcol = B_MOD0 if l == 0 else B_MOD1
            for blk in range(12):
                wb = wblk[it % 2]
                k.dma(wb.ap, wv[:, :, blk * 512:(blk + 1) * 512], w=[wb])
                for cc in range(4):
                    b = (it * 4 + cc) % 8
                    pk = ('ps', b)
                    for kc in range(8):
                        k.op('pe', lambda g, wb=wb, cc=cc, kc=kc, b=b: g.matmul(
                            ps[:, b * 512:b * 512 + 2], lhsT=wb.ap[:, kc, cc * 128:(cc + 1) * 128],
                            rhs=scT.ap[:, kc, :], start=(kc == 0), stop=(kc == 7)),
                            r=[wb, scT], w=[pk])
                    ch = blk * 4 + cc
                    k.op('dve', lambda g, b=b, ch=ch, l=l, bcol=bcol: g.tensor_scalar(
                        out=modT.ap[:, l, ch, :], in0=ps[:, b * 512:b * 512 + 2],
                        scalar1=vecs.ap[:, bcol + ch:bcol + ch + 1], scalar2=None, op0=ALU.add),
                        r=[pk, vecs], w=[modT])
                it += 1
        gcols = {(0, 0): G_MIX0, (0, 3): G_FFN0, (1, 0): G_MIX1, (1, 3): G_FFN1}
        for l in range(2):
            for c in range(2):
                for (wh, sh_i, sc_i, gt_i) in ((0, 0, 1, 2), (3, 3, 4, 5)):
                    gc = gcols[(l, wh)]
                    k.op('dve', lambda g, l=l, c=c, wh=wh, sc_i=sc_i: g.tensor_scalar(
                        out=der.ap[:, l, c, wh, :], in0=modT.ap[:, l, sc_i * 8:sc_i * 8 + 8, c],
                        scalar1=1.0, scalar2=None, op0=ALU.add), r=[modT], w=[der])
                    k.op('dve', lambda g, l=l, c=c, wh=wh, gc=gc: g.tensor_tensor(
                        out=der.ap[:, l, c, wh, :], in0=der.ap[:, l, c, wh, :], in1=vecs.ap[:, gc:gc + 8],
                        op=ALU.mult), r=[der, vecs], w=[der])
                    k.op('dve', lambda g, l=l, c=c, wh=wh, sh_i=sh_i: g.tensor_copy(
                        out=der.ap[:, l, c, wh + 1, :], in_=modT.ap[:, l, sh_i * 8:sh_i * 8 + 8, c]), r=[modT], w=[der])
                    k.op('dve', lambda g, l=l, c=c, wh=wh, gt_i=gt_i: g.tensor_copy(
                        out=der.ap[:, l, c, wh + 2, :], in_=modT.ap[:, l, gt_i * 8:gt_i * 8 + 8, c]), r=[modT], w=[der])
        k.barrier()
        k.top = mark0

        for l in range(2):
            for mi, (src, dst) in enumerate(((Up[l], Ub[l]), (Vp[l], Vb[l]))):
                for jb in range(0, 64, 4):
                    k.bg_cast_dma(l * 2 + mi, dst[jb:jb + 4].rearrange("j p f -> p j f"), src[jb:jb + 4].rearrange("j p f -> p j f"))


        def rms_rstd(xsq_ap_fn, nch, G, scale, rstd, pbank, rkeys):
            pk = ('ps', pbank)
            for c in range(nch):
                k.op('pe', lambda g, c=c: g.matmul(ps[:, pbank * 512:pbank * 512 + G], lhsT=onesf.ap,
                                                    rhs=xsq_ap_fn(c), start=(c == 0), stop=(c == nch - 1)),
                     r=rkeys + [onesf], w=[pk])
            k.op('act', lambda g: g.activation(out=rstd.ap[:, 0:G], in_=ps[:, pbank * 512:pbank * 512 + G],
                                                func=AF.Sqrt, scale=scale, bias=epsv.ap), r=[pk, epsv], w=[rstd])
            k.op('dve', lambda g: g.reciprocal(out=rstd.ap[:, 0:G], in_=rstd.ap[:, 0:G]), r=[rstd], w=[rstd])

        def norm_mod(xg, G, A_ap, B_ap, hT_out_fn, hkey, xsq, rstd, pbank):
            k.op('act', lambda g: g.activation(out=xsq.ap[:, :, 0:G], in_=xg.ap[:, :, 0:G], func=AF.Square), r=[xg], w=[xsq])
            rms_rstd(lambda c: xsq.ap[:, c, 0:G], 8, G, 1.0 / 1024.0, rstd, pbank, [xsq])
            k.op('dve', lambda g: g.tensor_tensor(out=xsq.ap[:, :, 0:G], in0=xg.ap[:, :, 0:G],
                                                  in1=rstd.ap[:, 0:G].unsqueeze(1).to_broadcast([128, 8, G]), op=ALU.mult),
                 r=[xg, rstd], w=[xsq])
            for dc in range(8):
                e = 'dve' if dc % 2 == 0 else 'pool'
                k.op(e, lambda g, dc=dc: g.tensor_scalar(out=hT_out_fn(dc), in0=xsq.ap[:, dc, 0:G],
                                                         scalar1=A_ap[:, dc:dc + 1], scalar2=B_ap[:, dc:dc + 1],
                                                         op0=ALU.mult, op1=ALU.add), r=[xsq, der], w=[hkey])

        xres_v = xres.rearrange("(dc p) t -> p dc t", p=128)
        xT_v = xT.rearrange("(dc p) t -> p dc t", p=128)
        yT_v = yT.rearrange("(dc p) t -> p dc t", p=128)
        hTs_v = hTs.rearrange("(dc p) t -> p dc t", p=128)
        uxs_v = uxs.rearrange("(n p) t -> p n t", p=128)
        gugs_v = gugs.rearrange("(n p) t -> p n t", p=128)

        markA = k.top
        stage = k.alloc("stage", [2048])
        w_uq_b = k.alloc("w_uq_b", [3, 768], BF16)
        w_ukv_b = k.alloc("w_ukv_b", [2, 1024], BF16)
        wrg_b = k.alloc("wrg_b", [8, 128], BF16)
        wig_b = k.alloc("wig_b", [8, 128], BF16)
        w_o_b = k.alloc("w_o_b", [8, 1024], BF16)
        pmat_b = k.alloc("pmat_b", [64], BF16)
        lru_t = k.alloc("lru_t", [16])
        for kc in range(3):
            k.load_cast(w_uq_b, w_uq_b.ap[:, kc, :], w_uq[kc * 128:(kc + 1) * 128, :], 768, stage)
        for kc in range(2):
            k.load_cast(w_ukv_b, w_ukv_b.ap[:, kc, :], w_ukv[kc * 128:(kc + 1) * 128, :], 1024, stage)
        for a in range(2):
            for n in range(4):
                k.load_cast(wrg_b, wrg_b.ap[:, a * 4 + n, :], w_rg[a, n], 128, stage)
                k.load_cast(wig_b, wig_b.ap[:, a * 4 + n, :], w_ig[a, n], 128, stage)
        for kc in range(8):
            k.load_cast(w_o_b, w_o_b.ap[:, kc, :], w_o[kc * 128:(kc + 1) * 128, :], 1024, stage)
        k.dma(stage.ap[0:64, 0:64], pmat_d, w=[stage])
        k.cast('dve', pmat_b.ap[0:64, :], stage.ap[0:64, 0:64], r=[stage], w=[pmat_b])
        k.op('dve', lambda g: g.memset(lru_t.ap, 0.0), w=[lru_t])
        k.barrier()
        markA2 = k.top

        for si, (t0, S, cond, has_ctx) in enumerate(SEGS):
            k.top = markA2
            G = min(512, S)
            NG = S // G
            Sk = S + (256 if has_ctx else 0)
            koff = 256 if has_ctx else 0
            NKC = Sk // 128
            A_m = der.ap[:, 0, cond, 0, :]
            B_m = der.ap[:, 0, cond, 1, :]
            G_m = der.ap[:, 0, cond, 2, :]
            attnT = k.alloc("attnT", [4, S], BF16)
            recT = k.alloc("recT", [4, S], BF16)
            qn = k.alloc("qn", [4, S], BF16)
            qr = k.alloc("qr", [4, S], BF16)
            ckvnT = k.alloc("ckvnT", [2, Sk], BF16)
            kropeT = k.alloc("kropeT", [Sk], BF16)
            markI = k.top
            G = min(256, S)
            NG = S // G
            w_in_b = k.alloc("w_in_b", [8, 1728], BF16)
            for kc in range(8):
                k.load_cast(w_in_b, w_in_b.ap[:, kc, :], w_in[kc * 128:(kc + 1) * 128, :], 1728, stage)
            xg = k.alloc("xg", [8, G])
            xsq = k.alloc("xsq", [8, G])
            rstd = k.alloc("rstd", [G])
            hTg = k.alloc("hTg", [8, G], BF16)
            cqf = k.alloc("cqf", [3, G])
            cqs = k.alloc("cqs", [3, G])
            cqn = k.alloc("cqn", [3, G], BF16)
            krf = k.alloc("krf", [G])
            krs = k.alloc("krs", [G])
            krb = k.alloc("krb", [G], BF16)
            uxt = k.alloc("uxt", [4, G])
            gut = k.alloc("gut", [4, G], BF16)
            if has_ctx:
                cosT = k.alloc("cosT", [S])
                sinT = k.alloc("sinT", [S])
                k.dma(cosT.ap[0:64, :], cos_d[:, 0:S], w=[cosT])
                k.dma(sinT.ap[0:64, :], sin_d[:, 0:S], w=[sinT])
                for kc in range(2):
                    k.dma(stage.ap[:, 0:256], cckvT[kc * 128:(kc + 1) * 128, :], w=[stage])
                    k.cast('dve', ckvnT.ap[:, kc, 0:256], stage.ap[:, 0:256], r=[stage], w=[ckvnT])
                k.dma(stage.ap[0:64, 0:256], ckrT, w=[stage])
                k.cast('dve', kropeT.ap[0:64, 0:256], stage.ap[0:64, 0:256], r=[stage], w=[kropeT])
            pb = 0

            def nb():
                nonlocal pb
                pb = (pb + 1) % 8
                return pb

            def rope(src_f, dst_bf_ap, dkey, g0, tmpf, tmpb):
                k.cast('act', tmpb.ap[0:64, 0:G], src_f.ap[0:64, 0:G], r=[src_f], w=[tmpb])
                b = nb()
                pk = ('ps', b)
                k.op('pe', lambda g: g.matmul(ps[0:64, b * 512:b * 512 + G], lhsT=pmat_b.ap[0:64, :], rhs=tmpb.ap[0:64, 0:G],
                                              start=True, stop=True), r=[pmat_b, tmpb], w=[pk])
                k.op('dve', lambda g: g.tensor_tensor(out=tmpf.ap[0:64, 0:G], in0=ps[0:64, b * 512:b * 512 + G],
                                                      in1=sinT.ap[0:64, g0:g0 + G], op=ALU.mult), r=[pk, sinT], w=[tmpf])
                k.op('dve', lambda g: g.tensor_tensor(out=src_f.ap[0:64, 0:G], in0=src_f.ap[0:64, 0:G],
                                                      in1=cosT.ap[0:64, g0:g0 + G], op=ALU.mult), r=[src_f, cosT], w=[src_f])
                k.op('dve', lambda g: g.tensor_tensor(out=dst_bf_ap, in0=src_f.ap[0:64, 0:G], in1=tmpf.ap[0:64, 0:G],
                                                      op=ALU.add), r=[src_f, tmpf], w=[dkey])

            for gi in range(NG):
                g0 = gi * G
                k.dma(xg.ap, xT_v[:, :, t0 + g0:t0 + g0 + G], w=[xg])
                norm_mod(xg, G, A_m, B_m, lambda dc: hTg.ap[:, dc, :], hTg, xsq, rstd, nb())

                def proj(c0, M):
                    b = nb()
                    pk = ('ps', b)
                    for kc in range(8):
                        k.op('pe', lambda g, kc=kc: g.matmul(ps[0:M, b * 512:b * 512 + G], lhsT=w_in_b.ap[:, kc, c0:c0 + M],
                                                             rhs=hTg.ap[:, kc, :], start=(kc == 0), stop=(kc == 7)),
                             r=[w_in_b, hTg], w=[pk])
                    return b, pk
                for c in range(3):
                    b, pk = proj(c * 128, 128)
                    k.op('act', lambda g, c=c, b=b: g.activation(out=cqf.ap[:, c, :], in_=ps[:, b * 512:b * 512 + G], func=AF.Copy), r=[pk], w=[cqf])
                k.op('act', lambda g: g.activation(out=cqs.ap, in_=cqf.ap, func=AF.Square), r=[cqf], w=[cqs])
                rms_rstd(lambda c: cqs.ap[:, c, :], 3, G, 1.0 / 384.0, rstd, nb(), [cqs])
                k.op('dve', lambda g: g.tensor_tensor(out=cqs.ap, in0=cqf.ap, in1=rstd.ap[:, 0:G].unsqueeze(1).to_broadcast([128, 3, G]),
                                                      op=ALU.mult), r=[cqf, rstd], w=[cqs])
                for c in range(3):
                    k.op('dve', lambda g, c=c: g.tensor_scalar(out=cqn.ap[:, c, :], in0=cqs.ap[:, c, :], scalar1=vecs.ap[:, G_Q + c:G_Q + c + 1],
                                                               scalar2=None, op0=ALU.mult), r=[cqs, vecs], w=[cqn])
                for h in range(4):
                    b = nb()
                    pk = ('ps', b)
                    for kc in range(3):
                        k.op('pe', lambda g, kc=kc, h=h, b=b: g.matmul(ps[:, b * 512:b * 512 + G], lhsT=w_uq_b.ap[:, kc, h * 192:h * 192 + 128],
                                                                       rhs=cqn.ap[:, kc, :], start=(kc == 0), stop=(kc == 2)),
                             r=[w_uq_b, cqn], w=[pk])
                    k.op('act', lambda g, h=h, b=b: g.activation(out=qn.ap[:, h, g0:g0 + G], in_=ps[:, b * 512:b * 512 + G], func=AF.Copy), r=[pk], w=[qn])
                    b = nb()
                    pk = ('ps', b)
                    for kc in range(3):
                        k.op('pe', lambda g, kc=kc, h=h, b=b: g.matmul(ps[0:64, b * 512:b * 512 + G], lhsT=w_uq_b.ap[:, kc, h * 192 + 128:h * 192 + 192],
                                                                       rhs=cqn.ap[:, kc, :], start=(kc == 0), stop=(kc == 2)),
                             r=[w_uq_b, cqn], w=[pk])
                    if has_ctx:
                        k.op('act', lambda g, b=b: g.activation(out=krf.ap[0:64, :], in_=ps[0:64, b * 512:b * 512 + G], func=AF.Copy), r=[pk], w=[krf])
                        rope(krf, qr.ap[0:64, h, g0:g0 + G], qr, g0, krs, krb)
                    else:
                        k.op('act', lambda g, h=h, b=b: g.activation(out=qr.ap[0:64, h, g0:g0 + G], in_=ps[0:64, b * 512:b * 512 + G], func=AF.Copy), r=[pk], w=[qr])
                for c in range(2):
                    b, pk = proj(384 + c * 128, 128)
                    k.op('act', lambda g, c=c, b=b: g.activation(out=cqf.ap[:, c, :], in_=ps[:, b * 512:b * 512 + G], func=AF.Copy), r=[pk], w=[cqf])
                k.op('act', lambda g: g.activation(out=cqs.ap[:, 0:2, :], in_=cqf.ap[:, 0:2, :], func=AF.Square), r=[cqf], w=[cqs])
                rms_rstd(lambda c: cqs.ap[:, c, :], 2, G, 1.0 / 256.0, rstd, nb(), [cqs])
                k.op('dve', lambda g: g.tensor_tensor(out=cqs.ap[:, 0:2, :], in0=cqf.ap[:, 0:2, :],
                                                      in1=rstd.ap[:, 0:G].unsqueeze(1).to_broadcast([128, 2, G]), op=ALU.mult), r=[cqf, rstd], w=[cqs])
                for c in range(2):
                    k.op('dve', lambda g, c=c: g.tensor_scalar(out=cqf.ap[:, c, :], in0=cqs.ap[:, c, :], scalar1=vecs.ap[:, G_KV + c:G_KV + c + 1],
                                                               scalar2=None, op0=ALU.mult), r=[cqs, vecs], w=[cqf])
                    k.cast('act', ckvnT.ap[:, c, koff + g0:koff + g0 + G], cqf.ap[:, c, :], r=[cqf], w=[ckvnT])
                    if not has_ctx:
                        k.dma(ckv_o[c * 128:(c + 1) * 128, (si - 1) * 256:(si - 1) * 256 + G], cqf.ap[:, c, :], r=[cqf], w=[("o", "ckv")])
                b, pk = proj(640, 64)
                k.op('act', lambda g, b=b: g.activation(out=krf.ap[0:64, :], in_=ps[0:64, b * 512:b * 512 + G], func=AF.Copy), r=[pk], w=[krf])
                if has_ctx:
                    rope(krf, kropeT.ap[0:64, koff + g0:koff + g0 + G], kropeT, g0, krs, krb)
                else:
                    k.cast('dve', kropeT.ap[0:64, g0:g0 + G], krf.ap[0:64, :], r=[krf], w=[kropeT])
                    k.dma(kr_o[:, (si - 1) * 256:(si - 1) * 256 + G], krf.ap[0:64, :], r=[krf], w=[("o", "kr")])
                for n in range(4):
                    b, pk = proj(704 + n * 128, 128)
                    k.op('act', lambda g, n=n, b=b: g.activation(out=uxt.ap[:, n, :], in_=ps[:, b * 512:b * 512 + G], func=AF.Copy), r=[pk], w=[uxt])
                    b, pk = proj(1216 + n * 128, 128)
                    k.op('act', lambda g, n=n, b=b: g.activation(out=gut.ap[:, n, :], in_=ps[:, b * 512:b * 512 + G], func=AF.Gelu_apprx_tanh), r=[pk], w=[gut])
                k.dma(uxs_v[:, :, t0 + g0:t0 + g0 + G], uxt.ap, r=[uxt], w=[("scr", "uxs")])
                k.dma(gugs_v[:, :, t0 + g0:t0 + g0 + G], gut.ap, r=[gut], w=[("scr", "gugs")])
            k.barrier()
            k.top = markI
            G = min(512, S)
            NG = S // G
            knT = k.alloc("knT", [4, Sk], BF16)
            Vt = k.alloc("Vt", [NKC, 512], BF16)
            pT = [k.alloc("pT%d" % i, [G], BF16) for i in range(2)]
            rden = k.alloc("rden", [G])
            KG = min(512, Sk)
            for h in range(4):
                for kg0 in range(0, Sk, KG):
                    kn = min(KG, Sk - kg0)
                    b = nb()
                    pk = ('ps', b)
                    for kc in range(2):
                        k.op('pe', lambda g, kc=kc, h=h, b=b, kg0=kg0, kn=kn: g.matmul(
                            ps[:, b * 512:b * 512 + kn], lhsT=w_ukv_b.ap[:, kc, h * 256:h * 256 + 128],
                            rhs=ckvnT.ap[:, kc, kg0:kg0 + kn], start=(kc == 0), stop=(kc == 1)), r=[w_ukv_b, ckvnT], w=[pk])
                    k.op('act', lambda g, h=h, b=b, kg0=kg0, kn=kn: g.activation(out=knT.ap[:, h, kg0:kg0 + kn], in_=ps[:, b * 512:b * 512 + kn], func=AF.Copy), r=[pk], w=[knT])
            for kc_ in range(NKC):
                b = nb()
                pk = ('ps', b)
                for h in range(4):
                    for kc in range(2):
                        k.op('pe', lambda g, kc=kc, h=h, b=b, kc_=kc_: g.matmul(
                            ps[:, b * 512 + h * 128:b * 512 + (h + 1) * 128], lhsT=ckvnT.ap[:, kc, kc_ * 128:(kc_ + 1) * 128],
                            rhs=w_ukv_b.ap[:, kc, h * 256 + 128:h * 256 + 256], start=(kc == 0), stop=(kc == 1)), r=[w_ukv_b, ckvnT], w=[pk])
                k.op('dve', lambda g, b=b, kc_=kc_: g.tensor_copy(out=Vt.ap[:, kc_, :], in_=ps[:, b * 512:(b + 1) * 512]), r=[pk], w=[Vt])
            for h in range(4):
                for gi in range(NG):
                    g0 = gi * G
                    bo, bd = 6, 7
                    def s_exp(kc_):
                        b = kc_ % 4
                        pk = ('ps', b)
                        p = pT[kc_ % 2]
                        k.op('pe', lambda g: g.matmul(
                            ps[:, b * 512:b * 512 + G], lhsT=knT.ap[:, h, kc_ * 128:(kc_ + 1) * 128], rhs=qn.ap[:, h, g0:g0 + G],
                            start=True, stop=False), r=[knT, qn], w=[pk])
                        k.op('pe', lambda g: g.matmul(
                            ps[:, b * 512:b * 512 + G], lhsT=kropeT.ap[0:64, kc_ * 128:(kc_ + 1) * 128], rhs=qr.ap[0:64, h, g0:g0 + G],
                            start=False, stop=True), r=[kropeT, qr], w=[pk])
                        k.op('act', lambda g: g.activation(out=p.ap, in_=ps[:, b * 512:b * 512 + G], func=AF.Exp, scale=ATT_SCALE), r=[pk], w=[p])

                    def pv(kc_):
                        p = pT[kc_ % 2]
                        k.op('pe', lambda g: g.matmul(
                            ps[:, bo * 512:bo * 512 + G], lhsT=Vt.ap[:, kc_, h * 128:(h + 1) * 128], rhs=p.ap,
                            start=(kc_ == 0), stop=(kc_ == NKC - 1)), r=[Vt, p], w=[('ps', bo)])
                        k.op('pe', lambda g: g.matmul(
                            ps[:, bd * 512:bd * 512 + G], lhsT=onesb.ap, rhs=p.ap,
                            start=(kc_ == 0), stop=(kc_ == NKC - 1)), r=[onesb, p], w=[('ps', bd)])

                    s_exp(0)
                    for kc_ in range(NKC):
                        if kc_ + 1 < NKC:
                            s_exp(kc_ + 1)
                        pv(kc_)
                    k.op('dve', lambda g: g.reciprocal(out=rden.ap, in_=ps[:, bd * 512:bd * 512 + G]), r=[('ps', bd)], w=[rden])
                    k.op('dve', lambda g, h=h, g0=g0: g.tensor_tensor(out=attnT.ap[:, h, g0:g0 + G], in0=ps[:, bo * 512:bo * 512 + G],
                                                                      in1=rden.ap, op=ALU.mult), r=[('ps', bo), rden], w=[attnT])
            k.barrier()
            k.top = markI
            uxp = k.alloc("uxp", [S + 4])
            gub = k.alloc("gub", [S], BF16)
            xc = k.alloc("xc", [S])
            xcb = k.alloc("xcb", [S], BF16)
            at = k.alloc("at", [S])
            bx = k.alloc("bx", [S])
            tmp = k.alloc("tmp", [S])
            hd = [k.alloc("hf", [S]), k.alloc("hb", [S])]
            for n in range(4):
                k.op('pool', lambda g: g.memset(uxp.ap[:, 0:2], 0.0), w=[uxp])
                k.op('pool', lambda g: g.memset(uxp.ap[:, S + 2:S + 4], 0.0), w=[uxp])
                k.dma(uxp.ap[:, 2:S + 2], uxs_v[:, n, t0:t0 + S], r=[("scr", "uxs")], w=[uxp])
                k.dma(gub.ap, gugs_v[:, n, t0:t0 + S], r=[("scr", "gugs")], w=[gub])
                cw = lambda kk: vecs.ap[:, CONV_W + kk * 4 + n:CONV_W + kk * 4 + n + 1]
                k.op('dve', lambda g: g.tensor_scalar(out=xc.ap, in0=uxp.ap[:, 0:S], scalar1=cw(0), scalar2=vecs.ap[:, CONV_B + n:CONV_B + n + 1],
                                                      op0=ALU.mult, op1=ALU.add), r=[uxp, vecs], w=[xc])
                for kk in range(1, 4):
                    k.op('dve', lambda g, kk=kk: g.scalar_tensor_tensor(out=xc.ap, in0=uxp.ap[:, kk:kk + S], scalar=cw(kk), in1=xc.ap,
                                                                        op0=ALU.mult, op1=ALU.add), r=[uxp, vecs, xc], w=[xc])
                k.cast('act', xcb.ap, xc.ap, r=[xc], w=[xcb])
                for a in range(2):
                    idx = a * 4 + n
                    for gi in range(NG):
                        g0 = gi * G
                        b = nb()
                        pk = ('ps', b)
                        k.op('pe', lambda g, b=b, g0=g0: g.matmul(ps[:, b * 512:b * 512 + G], lhsT=wrg_b.ap[:, idx, :], rhs=xcb.ap[:, g0:g0 + G],
                                                                 start=True, stop=True), r=[wrg_b, xcb], w=[pk])
                        k.op('act', lambda g, b=b, g0=g0: g.activation(out=tmp.ap[:, g0:g0 + G], in_=ps[:, b * 512:b * 512 + G], func=AF.Sigmoid,
                                                                       bias=vecs.ap[:, B_RG + idx:B_RG + idx + 1]), r=[pk, vecs], w=[tmp])
                        k.op('act', lambda g, g0=g0: g.activation(out=at.ap[:, g0:g0 + G], in_=tmp.ap[:, g0:g0 + G], func=AF.Exp,
                                                                  scale=nsp8.ap[:, idx:idx + 1]), r=[tmp, nsp8], w=[at])
                        b = nb()
                        pk = ('ps', b)
                        k.op('pe', lambda g, b=b, g0=g0: g.matmul(ps[:, b * 512:b * 512 + G], lhsT=wig_b.ap[:, idx, :], rhs=xcb.ap[:, g0:g0 + G],
                                                                 start=True, stop=True), r=[wig_b, xcb], w=[pk])
                        k.op('act', lambda g, b=b, g0=g0: g.activation(out=bx.ap[:, g0:g0 + G], in_=ps[:, b * 512:b * 512 + G], func=AF.Sigmoid,
                                                                       bias=vecs.ap[:, B_IG + idx:B_IG + idx + 1]), r=[pk, vecs], w=[bx])
                    k.op('pool', lambda g: g.tensor_tensor(out=bx.ap, in0=bx.ap, in1=xc.ap, op=ALU.mult), r=[bx, xc], w=[bx])
                    k.op('dve', lambda g: g.tensor_tensor(out=tmp.ap, in0=at.ap, in1=at.ap, op=ALU.mult), r=[at], w=[tmp])
                    k.op('dve', lambda g: g.tensor_scalar(out=tmp.ap, in0=tmp.ap, scalar1=-1.0, scalar2=1.0, op0=ALU.mult, op1=ALU.add), r=[tmp], w=[tmp])
                    k.op('act', lambda g: g.activation(out=tmp.ap, in_=tmp.ap, func=AF.Sqrt), r=[tmp], w=[tmp])
                    k.op('dve', lambda g: g.tensor_tensor(out=bx.ap, in0=bx.ap, in1=tmp.ap, op=ALU.mult), r=[bx, tmp], w=[bx])
                    init = vecs.ap[:, H0 + idx:H0 + idx + 1] if has_ctx else 0.0
                    hh = hd[a]
                    if a == 0:
                        k.op('dve', lambda g, hh=hh: g.tensor_tensor_scan(out=hh.ap, data0=at.ap, data1=bx.ap, initial=init, op0=ALU.mult, op1=ALU.add),
                             r=[at, bx, vecs], w=[hh])
                    else:
                        k.op('dve', lambda g, hh=hh: g.tensor_tensor_scan(out=hh.ap[:, ::-1], data0=at.ap[:, ::-1], data1=bx.ap[:, ::-1], initial=init,
                                                                          op0=ALU.mult, op1=ALU.add), r=[at, bx, vecs], w=[hh])
                if not has_ctx:
                    col = (si - 1) * 8
                    k.op('pool', lambda g: g.tensor_copy(out=lru_t.ap[:, col + n:col + n + 1], in_=hd[0].ap[:, S - 1:S]), r=[hd[0]], w=[lru_t])
                    k.op('pool', lambda g: g.tensor_copy(out=lru_t.ap[:, col + 4 + n:col + 4 + n + 1], in_=hd[1].ap[:, 0:1]), r=[hd[1]], w=[lru_t])
                k.op('dve', lambda g: g.tensor_tensor(out=tmp.ap, in0=hd[0].ap, in1=hd[1].ap, op=ALU.add), r=[hd[0], hd[1]], w=[tmp])
                k.op('dve', lambda g, n=n: g.tensor_tensor(out=recT.ap[:, n, :], in0=tmp.ap, in1=gub.ap, op=ALU.mult), r=[tmp, gub], w=[recT])
            k.barrier()
            k.top = markI
            xg2 = [k.alloc("xg2_%d" % i, [8, G]) for i in range(2)]
            for gi in range(NG):
                g0 = gi * G
                xo = xg2[gi % 2]
                k.dma(xo.ap, xT_v[:, :, t0 + g0:t0 + g0 + G], w=[xo])
                for oc in range(8):
                    b = nb()
                    pk = ('ps', b)
                    for kk in range(8):
                        rhs = attnT.ap[:, kk, g0:g0 + G] if kk < 4 else recT.ap[:, kk - 4, g0:g0 + G]
                        k.op('pe', lambda g, kk=kk, rhs=rhs, b=b, oc=oc: g.matmul(ps[:, b * 512:b * 512 + G], lhsT=w_o_b.ap[:, kk, oc * 128:(oc + 1) * 128],
                                                                                  rhs=rhs, start=(kk == 0), stop=(kk == 7)), r=[w_o_b, attnT, recT], w=[pk])
                    k.op('dve', lambda g, b=b, oc=oc, xo=xo: g.scalar_tensor_tensor(out=xo.ap[:, oc, :], in0=ps[:, b * 512:b * 512 + G], scalar=G_m[:, oc:oc + 1],
                                                                                    in1=xo.ap[:, oc, :], op0=ALU.mult, op1=ALU.add), r=[pk, xo, der], w=[xo])
                k.dma(xres_v[:, :, t0 + g0:t0 + g0 + G], xo.ap, r=[xo], w=[("scr", "xres")])
            k.barrier()
        k.dma(lru_o, lru_t.ap, r=[lru_t], w=[("o", "lru")])
        k.barrier()
        k.top = markA

        def peer(l, final):
            mark = k.top
            stage = k.alloc("stageP", [2048])
            wq_b = k.alloc("wq_b", [8, 2048], BF16)
            for kc in range(8):
                k.load_cast(wq_b, wq_b.ap[:, kc, :], wq[l][kc * 128:(kc + 1) * 128, :], 2048, stage)
            G = 512
            xg = k.alloc("xgP", [8, G])
            xsq = k.alloc("xsqP", [8, G])
            rstd = k.alloc("rstdP", [G])
            hTg = k.alloc("hTgP", [8, G], BF16)
            qTg = k.alloc("qTgP", [16, G], BF16)
            pb = 0
            for gi in range(T // G):
                g0 = gi * G
                cond = 0 if g0 < 2048 else 1
                k.dma(xg.ap, xres_v[:, :, g0:g0 + G], r=[("scr", "xres")], w=[xg])
                norm_mod(xg, G, der.ap[:, l, cond, 3, :], der.ap[:, l, cond, 4, :], lambda dc: hTg.ap[:, dc, :], hTg, xsq, rstd, 7)
                k.dma(hTs_v[:, :, g0:g0 + G], hTg.ap, r=[hTg], w=[("scr", "hTs")])
                for hp in range(16):
                    b = pb = (pb + 1) % 6
                    pk = ('ps', b)
                    for kc in range(8):
                        k.op('pe', lambda g, kc=kc, hp=hp, b=b: g.matmul(ps[:, b * 512:(b + 1) * 512], lhsT=wq_b.ap[:, kc, hp * 128:(hp + 1) * 128],
                                                                         rhs=hTg.ap[:, kc, :], start=(kc == 0), stop=(kc == 7)), r=[wq_b, hTg], w=[pk])
                    k.cast('act' if hp % 2 else 'dve', qTg.ap[:, hp, :], ps[:, b * 512:(b + 1) * 512], r=[pk], w=[qTg])
                k.dma(qTs[:, :, g0:g0 + G], qTg.ap, r=[qTg], w=[("scr", "qTs")])
            k.barrier()
            k.top = mark
            keys_b = k.alloc("keys_b", [16, 128], BF16)
            hT = k.alloc("hT", [2, 8, 128], BF16)
            s_sb = k.alloc("s_sb", [8, 2, 128])
            sv = k.alloc("sv", [8, 2, 16])
            fv = k.alloc("fv", [8, 16])
            ef = k.alloc("ef", [8, 16])
            sm = k.alloc("sm", [8, 8])
            th = k.alloc("th", [8, 16])
            cc = k.alloc("cc", [8, 16], BF16)
            e2 = k.alloc("e2", [8, 128], BF16)
            Rm = k.alloc("Rm", [128, 128], BF16)
            P1 = k.alloc("P1", [128, 128], BF16)
            RT = k.alloc("RT", [128, 64], BF16)
            P1T = k.alloc("P1T", [128, 64], BF16)
            W = k.alloc("W", [2, 128, 128], BF16)
            JB = 2
            NJ = 128 // JB
            HALF = NJ // 2
            ut = [k.alloc("ut%d" % i, [8, JB * 128], BF16) for i in range(3)]
            vt = [k.alloc("vt%d" % i, [JB, 1024], BF16) for i in range(2)]
            gl = [k.alloc("gl%d" % i, [JB * 128], BF16) for i in range(2)]
            wa = [k.alloc("wa%d" % i, [JB, 128], BF16) for i in range(3)]
            rstd = k.alloc("rstdT", [128])
            RT_f = RT.ap.rearrange("p a b -> p (a b)").bitcast(F32)
            P1T_f = P1T.ap.rearrange("p a b -> p (a b)").bitcast(F32)
            xt = Tl(RT.name, RT_f[:, 0:1024].rearrange("p (a b) -> p a b", a=8))
            xsq = Tl(RT.name, RT_f[:, 1024:2048].rearrange("p (a b) -> p a b", a=8))
            qT = Tl(P1T.name, P1T.ap.rearrange("p a b -> p (a b)")[:, 0:2048].rearrange("p (a b) -> p a b", a=16))
            e2f = Tl(P1T.name, P1T_f[:, 1024:2048].rearrange("p (a b) -> p a b", a=8))
            Rm_f = Rm.ap.rearrange("p a b -> p (a b)").bitcast(F32)
            work = Rm_f[:, 0:2048].rearrange("p (h q n) -> p h q n", h=8, q=2)
            cand = Rm_f[:, 2048:4096].rearrange("p (h a b) -> p h a b", h=8, a=16)
            candw = Rm_f[:, 4096:6144].rearrange("p (h a b) -> p h a b", h=8, a=16)
            k.dma(Rm_f[:, 0:2048], keysT[l].rearrange("p a b -> p (a b)"), w=[Rm])
            k.cast('dve', keys_b.ap.rearrange("p a b -> p (a b)"), Rm_f[:, 0:2048], r=[Rm], w=[keys_b])
            Ubv = Ub[l]
            Vbv = Vb[l]
            psb = ps[:, :].bitcast(BF16)
            NT = T // 128
            Rv = Rm.ap.rearrange("p j (h k) -> p j h k", h=8)
            P1v = P1.ap.rearrange("p i (h k) -> p i h k", h=8)
            s_flat = s_sb.ap.rearrange("p h q n -> p (h q n)")
            BANK1 = []

            def route_a(ti):
                t0 = ti * 128
                k.dma(qT.ap, qTs[:, :, t0:t0 + 128], r=[("scr", "qTs")], w=[qT])
                for rd in range(4):
                    for q4 in range(4):
                        hp = rd * 4 + q4
                        k.op('pe', lambda g, hp=hp, q4=q4: g.matmul(ps[:, q4 * 128:(q4 + 1) * 128], lhsT=qT.ap[:, hp, :], rhs=keys_b.ap[:, hp, :],
                                                                    start=True, stop=True), r=[qT, keys_b], w=[('ps', 0)], inc=(q4 == 3))
                    k.op('act', lambda g, rd=rd: g.activation(out=s_flat[:, rd * 512:(rd + 1) * 512], in_=ps[:, 0:512], func=AF.Copy),
                         r=[('ps', 0)], w=[('s', rd), s_sb])
                    yield 0.3
                for hp in range(16):
                    h_, p_ = hp // 2, hp % 2
                    k.op('dve', lambda g, h_=h_, p_=p_: g.max(out=sv.ap[:, h_, p_, 0:8], in_=s_sb.ap[:, h_, p_, :]), r=[s_sb], w=[('sv', hp)])
                    if hp % 2:
                        yield 0.55
                for hp in range(16):
                    h_, p_ = hp // 2, hp % 2
                    k.op('dve', lambda g, h_=h_, p_=p_: g.match_replace(out=work[:, h_, p_, :], in_to_replace=sv.ap[:, h_, p_, 0:8],
                                                                        in_values=s_sb.ap[:, h_, p_, :], imm_value=-1e30),
                         r=[s_sb, ('sv', hp)], w=[('wk', hp), Rm])
                    if hp % 2:
                        yield 0.55
                for hp in range(16):
                    h_, p_ = hp // 2, hp % 2
                    k.op('dve', lambda g, h_=h_, p_=p_: g.max(out=sv.ap[:, h_, p_, 8:16], in_=work[:, h_, p_, :]), r=[('wk', hp)], w=[('sv', hp), sv])
                    if hp % 2:
                        yield 0.55
                k.op('dve', lambda g: g.tensor_tensor(out=cand, in0=sv.ap[:, :, 0, :].unsqueeze(3).to_broadcast([128, 8, 16, 16]),
                                                      in1=sv.ap[:, :, 1, :].unsqueeze(2).to_broadcast([128, 8, 16, 16]), op=ALU.add),
                     r=[sv] + [('sv', i) for i in range(16)], w=[('cand',)])
                yield 0.6
                for h_ in range(8):
                    k.op('dve', lambda g, h_=h_: g.max(out=fv.ap[:, h_, 0:8], in_=cand[:, h_]), r=[('cand',)], w=[('fv', h_)])
                    if h_ % 2:
                        yield 0.55
                for h_ in range(8):
                    k.op('dve', lambda g, h_=h_: g.match_replace(out=candw[:, h_], in_to_replace=fv.ap[:, h_, 0:8], in_values=cand[:, h_], imm_value=-1e30),
                         r=[('cand',), ('fv', h_)], w=[('cw', h_)])
                    if h_ % 2:
                        yield 0.55
                for h_ in range(8):
                    k.op('dve', lambda g, h_=h_: g.max(out=fv.ap[:, h_, 8:16], in_=candw[:, h_]), r=[('cw', h_)], w=[('fv', h_), fv])
                    if h_ % 2:
                        yield 0.55
                fvk = [fv] + [('fv', i) for i in range(8)]
                k.op('dve', lambda g: g.tensor_tensor(out=ef.ap, in0=fv.ap, in1=fv.ap[:, :, 0:1].to_broadcast([128, 8, 16]), op=ALU.subtract), r=fvk, w=[ef])
                k.op('act', lambda g: g.activation(out=ef.ap, in_=ef.ap, func=AF.Exp), r=[ef], w=[ef])
                yield 0.6
                k.op('dve', lambda g: g.tensor_reduce(out=sm.ap[:, :, 0], in_=ef.ap, axis=AX.X, op=ALU.add), r=[ef], w=[sm])
                k.op('dve', lambda g: g.reciprocal(out=sm.ap[:, :, 1], in_=sm.ap[:, :, 0]), r=[sm], w=[sm])
                k.op('dve', lambda g: g.tensor_scalar(out=sm.ap[:, :, 2], in0=fv.ap[:, :, 15], scalar1=-1e-5, scalar2=None, op0=ALU.add), r=fvk, w=[sm])
                yield 0.6
                k.op('dve', lambda g: g.tensor_tensor(out=th.ap, in0=sm.ap[:, :, 2:3].to_broadcast([128, 8, 16]), in1=sv.ap[:, :, 0, :], op=ALU.subtract),
                     r=[sm, sv], w=[th])
                k.op('dve', lambda g: g.tensor_tensor(out=ef.ap, in0=sv.ap[:, :, 0, :], in1=sv.ap[:, :, 0, 0:1].to_broadcast([128, 8, 16]), op=ALU.subtract),
                     r=[sv], w=[ef])
                k.op('act', lambda g: g.activation(out=ef.ap, in_=ef.ap, func=AF.Exp), r=[ef], w=[ef])
                yield 0.6
                k.op('dve', lambda g: g.tensor_tensor(out=cc.ap, in0=ef.ap, in1=sm.ap[:, :, 1:2].to_broadcast([128, 8, 16]), op=ALU.mult), r=[ef, sm], w=[cc])
                k.op('dve', lambda g: g.tensor_tensor(out=e2f.ap, in0=s_sb.ap[:, :, 1, :], in1=sv.ap[:, :, 1, 0:1].to_broadcast([128, 8, 128]), op=ALU.subtract),
                     r=[s_sb, sv], w=[e2f])
                k.op('act', lambda g: g.activation(out=e2.ap, in_=e2f.ap, func=AF.Exp), r=[e2f], w=[e2])
                yield 0.6
                NCH = 8
                CJ = 128 // NCH
                alias_keys = [('cand',)] + [('cw', i) for i in range(8)] + [('wk', i) for i in range(16)]
                for c in range(NCH):
                    js = slice(c * CJ, (c + 1) * CJ)
                    k.op('dve', lambda g, js=js: g.tensor_tensor(out=Rv[:, js], in0=s_sb.ap[:, :, 1, js].rearrange("p h j -> p j h").unsqueeze(3).to_broadcast([128, CJ, 8, 16]),
                                                                 in1=th.ap.unsqueeze(1).to_broadcast([128, CJ, 8, 16]), op=ALU.is_ge),
                         r=[s_sb, th] + alias_keys, w=[('R', c)])
                    k.op('pool', lambda g, js=js: g.tensor_tensor(out=Rv[:, js], in0=Rv[:, js], in1=e2.ap[:, :, js].rearrange("p h j -> p j h").unsqueeze(3).to_broadcast([128, CJ, 8, 16]),
                                                                  op=ALU.mult), r=[('R', c), e2], w=[('R', c)])
                    k.op('pool', lambda g, js=js, c=c: g.tensor_tensor(out=Rv[:, js], in0=Rv[:, js], in1=cc.ap.unsqueeze(1).to_broadcast([128, CJ, 8, 16]), op=ALU.mult),
                         r=[('R', c), cc], w=[('R', c), Rm] if c == NCH - 1 else [('R', c)])
                    yield 4.0
                for c in range(NCH):
                    js = slice(c * CJ, (c + 1) * CJ)
                    k.op('dve', lambda g, js=js: g.tensor_tensor(out=P1v[:, js], in0=s_sb.ap[:, :, 0, js].rearrange("p h i -> p i h").unsqueeze(3).to_broadcast([128, CJ, 8, 16]),
                                                                 in1=sv.ap[:, :, 0, :].unsqueeze(1).to_broadcast([128, CJ, 8, 16]), op=ALU.is_equal),
                         r=[s_sb, sv], w=[P1])
                    yield 4.0

            def route_b(ti):
                Rkeys = [Rm] + [('R', c) for c in range(8)]
                wp = ti % 2
                for hf in range(2):
                    tp = slice(hf * 64, hf * 64 + 64)
                    for (src, dstT, rk) in ((Rm, RT, Rkeys), (P1, P1T, [P1])):
                        for rnd in range(4):
                            banks = [0, 1] if rnd % 2 == 0 else [2, 3]
                            pk = [('ps', bb) for bb in banks] + (BANK1 if rnd % 2 == 0 else [])
                            base = banks[0] * 1024
                            for jj in range(32):
                                j = rnd * 32 + jj
                                k.op('pe', lambda g, j=j, jj=jj, src=src, base=base: g.transpose(psb[:, base + jj * 64:base + (jj + 1) * 64], src.ap[tp, j, :], identb.ap[tp, tp]),
                                     r=rk + [identb], w=pk, inc=(jj == 31))
                            k.cast('act' if rnd % 2 else 'dve', dstT.ap[:, rnd * 32:(rnd + 1) * 32, :].rearrange("p j t -> p (j t)"), psb[:, base:base + 2048], r=pk, w=[dstT])
                    for rnd in range(8):
                        bb0 = 0 if rnd % 2 == 0 else 2
                        pk = [('ps', bb0), ('ps', bb0 + 1)] + (BANK1 if bb0 == 0 else [])
                        for tt in range(8):
                            tl = rnd * 8 + tt
                            k.op('pe', lambda g, tl=tl, tt=tt, bb0=bb0: g.matmul(ps[:, bb0 * 512 + tt * 128:bb0 * 512 + (tt + 1) * 128], lhsT=P1T.ap[:, :, tl], rhs=RT.ap[:, :, tl],
                                                                                 start=True, stop=True), r=[P1T, RT], w=pk, inc=(tt == 7))
                        k.cast('act' if rnd % 2 == 0 else 'dve', W.ap[:, wp, hf * 64 + rnd * 8:hf * 64 + (rnd + 1) * 8, :].rearrange("p t j -> p (t j)"),
                               ps[:, bb0 * 512:bb0 * 512 + 1024], r=pk, w=[('W', wp)])

            def tail(n):
                t0 = n * 128
                cond = 0 if t0 < 2048 else 1
                G_f = der.ap[:, l, cond, 5, :]
                ob = 4 + 2 * (n % 2)
                k.dma(xt.ap, xres_v[:, :, t0:t0 + 128], r=[("scr", "xres")], w=[xt])
                osb = xsq.ap.rearrange("p a b -> p (a b)")
                k.op('act', lambda g: g.activation(out=osb, in_=ps[:, ob * 512:(ob + 2) * 512], func=AF.Copy), r=[('ps', ob), ('ps', ob + 1)], w=[xsq])
                for dc in range(8):
                    k.op('pe', lambda g, dc=dc: g.transpose(ps[:, 1024 + dc * 128:1024 + (dc + 1) * 128], osb[:, dc * 128:(dc + 1) * 128], identf.ap),
                         r=[xsq, identf], w=[('ps', 2 + dc // 4)])
                for dc in range(8):
                    k.op('dve', lambda g, dc=dc: g.scalar_tensor_tensor(out=xt.ap[:, dc, :], in0=ps[:, 1024 + dc * 128:1024 + (dc + 1) * 128], scalar=G_f[:, dc:dc + 1],
                                                                        in1=xt.ap[:, dc, :], op0=ALU.mult, op1=ALU.add), r=[('ps', 2 + dc // 4), xt, der], w=[xt])
                if not final:
                    k.dma(xres_v[:, :, t0:t0 + 128], xt.ap, r=[xt], w=[("scr", "xres")])
                else:
                    k.op('act', lambda g: g.activation(out=xsq.ap, in_=xt.ap, func=AF.Square), r=[xt], w=[xsq])
                    rms_rstd(lambda c: xsq.ap[:, c, :], 8, 128, 1.0 / 1024.0, rstd, 0, [xsq])
                    k.op('dve', lambda g: g.tensor_tensor(out=xsq.ap, in0=xt.ap, in1=rstd.ap[:, 0:128].unsqueeze(1).to_broadcast([128, 8, 128]), op=ALU.mult),
                         r=[xt, rstd], w=[xsq])
                    k.op('dve', lambda g: g.tensor_tensor(out=xsq.ap, in0=xsq.ap, in1=vecs.ap[:, G_FIN:G_FIN + 8].unsqueeze(2).to_broadcast([128, 8, 128]), op=ALU.mult),
                         r=[xsq, vecs], w=[xsq])
                    k.dma(yT_v[:, :, t0:t0 + 128], xsq.ap, r=[xsq], w=[("o", "y")])

            def half_loop(s_, gen):
                act_t = [n for n in (s_ - 1, s_) if 0 <= n < NT]
                c0 = (s_ % 2) * HALF
                if s_ < NT:
                    k.dma(hT.ap[:, s_ % 2], hTs_v[:, :, s_ * 128:(s_ + 1) * 128], r=[("scr", "hTs")], w=[('hT', s_ % 2)])
                items = [(p_, n) for p_ in range(HALF) for n in act_t]
                L = len(items)

                def load_u(p_):
                    k.dma(ut[p_ % 3].ap.rearrange("p c f -> p (c f)"), Ubv[c0 + p_], r=[("scr", id(Ubv))], w=[ut[p_ % 3]])

                def load_v(p_):
                    k.dma(vt[p_ % 2].ap.rearrange("p j f -> p (j f)"), Vbv[c0 + p_], r=[("scr", id(Vbv))], w=[vt[p_ % 2]], q='act')

                def m1(qi):
                    p_, n = items[qi]
                    u, gg, ba = ut[p_ % 3], gl[qi % 2], 2 + (qi % 2)
                    pk = ('ps', ba)
                    for dc in range(8):
                        k.op('pe', lambda g, dc=dc: g.matmul(ps[:, ba * 512:ba * 512 + JB * 128], lhsT=hT.ap[:, n % 2, dc, :], rhs=u.ap[:, dc, :],
                                                             start=(dc == 0), stop=(dc == 7)), r=[u, ('hT', n % 2)], w=[pk], inc=(dc == 7))
                    k.op('act', lambda g: g.activation(out=gg.ap, in_=ps[:, ba * 512:ba * 512 + JB * 128], func=AF.Gelu_apprx_tanh), r=[pk], w=[gg])

                def tr(qi):
                    p_, n = items[qi]
                    gg, ww = gl[qi % 2], wa[qi % 3]
                    off = (qi % 2) * 1024
                    pkt = ('ps', qi % 2)
                    wk_ = [pkt]
                    for jj in range(JB):
                        k.op('pe', lambda g, jj=jj: g.transpose(psb[:, off + jj * 128:off + (jj + 1) * 128], gg.ap[:, jj * 128:(jj + 1) * 128], identb.ap),
                             r=[gg, identb], w=wk_, inc=(jj == JB - 1))
                    jg = (c0 + p_) * JB
                    k.op('dve', lambda g: g.tensor_tensor(out=ww.ap, in0=psb[:, off:off + JB * 128].rearrange("p (j t) -> p j t", j=JB),
                                                          in1=W.ap[:, n % 2, :, jg:jg + JB].rearrange("p t j -> p j t"), op=ALU.mult),
                         r=[pkt, ('W', n % 2)], w=[ww])

                def m2(qi):
                    p_, n = items[qi]
                    v_, ww = vt[p_ % 2], wa[qi % 3]
                    first = (n == s_) and p_ == 0
                    last = (n == s_ - 1 or NT == 1) and p_ == HALF - 1
                    for jj in range(JB):
                        for hh in range(2):
                            bo = 4 + 2 * (n % 2) + hh
                            k.op('pe', lambda g, jj=jj, hh=hh, bo=bo: g.matmul(
                                ps[:, bo * 512:(bo + 1) * 512], lhsT=ww.ap[:, jj, :], rhs=v_.ap[:, jj, hh * 512:(hh + 1) * 512],
                                start=(first and jj == 0), stop=(last and jj == JB - 1)), r=[v_, ww], w=[('ps', bo)], inc=(jj == JB - 1 and hh == 1))

                budget = 0.0
                spent = 0.0
                load_u(0)
                load_v(0)
                load_u(1)
                load_v(1)
                m1(0)
                for qi in range(L):
                    p_, n = items[qi]
                    tr(qi)
                    if n == act_t[0] and p_ + 2 < HALF:
                        load_u(p_ + 2)
                    if qi + 1 < L:
                        m1(qi + 1)
                    if qi >= 1:
                        m2(qi - 1)
                        pp, pn = items[qi - 1]
                        if pn == act_t[-1] and pp + 2 < HALF:
                            load_v(pp + 2)
                    if gen is not None:
                        budget += 2.9 * 2.0 / len(act_t) / 2.0
                        while spent < budget:
                            cst_ = next(gen, None)
                            if cst_ is None:
                                break
                            spent += cst_
                m2(L - 1)
                if gen is not None:
                    for _ in gen:
                        pass

            k.bg_wait('sp', l * 2)
            k.bg_wait('sp', l * 2 + 1)
            k.bg_wait('act', l * 2 + 1)
            g0 = route_a(0)
            for _ in g0:
                pass
            route_b(0)
            for s_ in range(NT + 1):
                gen = route_a(s_ + 1) if s_ + 1 < NT else None
                half_loop(s_, gen)
                if s_ - 1 >= 0:
                    tail(s_ - 1)
                if s_ + 1 < NT:
                    route_b(s_ + 1)
            k.barrier()
            k.top = mark

        if debug_stage != "A":
            peer(0, False)

        if debug_stage not in ("A", "P0"):
            mark = k.top
            stage = k.alloc("stageQ", [2048])
            wp_b = k.alloc("wp_b", [4, 2, 256], BF16)
            for gq in range(4):
                for kc in range(2):
                    k.load_cast(wp_b, wp_b.ap[:, gq, kc, :], w_pool[gq, kc * 128:(kc + 1) * 128, :], 256, stage)
            PADW = 8
            for si, (t0, S, cond, has_ctx) in enumerate(SEGS):
                markS = k.top
                A_m = der.ap[:, 1, cond, 0, :]
                B_m = der.ap[:, 1, cond, 1, :]
                G_m = der.ap[:, 1, cond, 2, :]
                G = min(512, S)
                NG = S // G
                hp_ = k.alloc("hpad", [8, S + 2 * PADW])
                dT = k.alloc("dT", [8, S], BF16)
                inv = k.alloc("inv", [4, S])
                xg = k.alloc("xgQ", [8, G])
                xsq = k.alloc("xsqQ", [8, G])
                rstd = k.alloc("rstdQ", [G])
                t1 = k.alloc("t1", [S + 2 * PADW])
                t2 = k.alloc("t2", [S + 2 * PADW])
                k.op('pool', lambda g: g.memset(hp_.ap, 0.0), w=[hp_])
                for gq, wdw in enumerate((2, 4, 8, 16)):
                    k.op('pool', lambda g, gq=gq, wdw=wdw: g.memset(inv.ap[:, gq, :], 1.0 / wdw), w=[inv])
                    lo_h, hi_h = wdw // 2, wdw - wdw // 2
                    for t in range(0, lo_h):
                        cnt = min(t + hi_h, S) - max(t - lo_h, 0)
                        k.op('pool', lambda g, gq=gq, t=t, cnt=cnt: g.memset(inv.ap[:, gq, t:t + 1], 1.0 / cnt), w=[inv])
                    for t in range(S - hi_h + 1, S):
                        cnt = min(t + hi_h, S) - max(t - lo_h, 0)
                        k.op('pool', lambda g, gq=gq, t=t, cnt=cnt: g.memset(inv.ap[:, gq, t:t + 1], 1.0 / cnt), w=[inv])
                for gi in range(NG):
                    g0 = gi * G
                    k.dma(xg.ap, xres_v[:, :, t0 + g0:t0 + g0 + G], r=[("scr", "xres")], w=[xg])
                    norm_mod(xg, G, A_m, B_m, lambda dc: hp_.ap[:, dc, PADW + g0:PADW + g0 + G], hp_, xsq, rstd, gi % 8)
                for dc in range(8):
                    gq = dc // 2
                    wdw = (2, 4, 8, 16)[gq]
                    L = S + 2 * PADW
                    src = hp_.ap[:, dc, :]
                    cur, n = src, 1
                    bufs = [t1, t2]
                    bi = 0
                    while n < wdw:
                        o = bufs[bi]
                        bi ^= 1
                        k.op('dve', lambda g, o=o, cur=cur, n=n, L=L: g.tensor_tensor(out=o.ap[:, 0:L - n], in0=cur[:, 0:L - n], in1=cur[:, n:L], op=ALU.add),
                             r=[hp_, t1, t2], w=[o])
                        cur = o.ap
                        n *= 2
                    st = PADW - wdw // 2
                    k.op('dve', lambda g, cur=cur, st=st, gq=gq: g.tensor_tensor(out=t1.ap[:, 0:S] if cur is not t1.ap else t2.ap[:, 0:S], in0=cur[:, st:st + S], in1=inv.ap[:, gq, :], op=ALU.mult),
                         r=[t1, t2, inv], w=[t1, t2])
                    mres = t1.ap if cur is not t1.ap else t2.ap
                    k.op('dve', lambda g, dc=dc, mres=mres: g.tensor_tensor(out=dT.ap[:, dc, :], in0=mres[:, 0:S], in1=hp_.ap[:, dc, PADW:PADW + S], op=ALU.subtract),
                         r=[t1, t2, hp_], w=[dT])
                xo2 = [xg, xg]
                pb2 = 0
                for gi in range(NG):
                    g0 = gi * G
                    xo = xo2[gi % 2]
                    k.dma(xo.ap, xres_v[:, :, t0 + g0:t0 + g0 + G], r=[("scr", "xres")], w=[xo])
                    for oc in range(8):
                        gq, dd = oc // 2, oc % 2
                        b = pb2 = (pb2 + 1) % 8
                        pk = ('ps', b)
                        for kc in range(2):
                            k.op('pe', lambda g, kc=kc, gq=gq, dd=dd, b=b: g.matmul(ps[:, b * 512:b * 512 + G], lhsT=wp_b.ap[:, gq, kc, dd * 128:(dd + 1) * 128],
                                                                                    rhs=dT.ap[:, gq * 2 + kc, g0:g0 + G], start=(kc == 0), stop=(kc == 1)), r=[wp_b, dT], w=[pk])
                        k.op('act', lambda g, b=b, oc=oc: g.activation(out=xsq.ap[:, 0, 0:G], in_=ps[:, b * 512:b * 512 + G], func=AF.Copy,
                                                                       scale=vecs.ap[:, S_POOL + oc:S_POOL + oc + 1]), r=[pk, vecs], w=[xsq])
                        k.op('dve', lambda g, oc=oc, xo=xo: g.scalar_tensor_tensor(out=xo.ap[:, oc, :], in0=xsq.ap[:, 0, 0:G], scalar=G_m[:, oc:oc + 1], in1=xo.ap[:, oc, :],
                                                                                   op0=ALU.mult, op1=ALU.add), r=[xsq, xo, der], w=[xo])
                    k.dma(xres_v[:, :, t0 + g0:t0 + g0 + G], xo.ap, r=[xo], w=[("scr", "xres")])
                k.barrier()
                k.top = markS
            k.barrier()
            k.top = mark
            if debug_stage != "P1":
                peer(1, True)

        if debug_stage is not None:
            mark = k.top
            xg = k.alloc("xdump", [8, 512])
            for gi in range(T // 512):
                k.dma(xg.ap, xres_v[:, :, gi * 512:(gi + 1) * 512], r=[("scr", "xres")], w=[xg])
                k.dma(yT_v[:, :, gi * 512:(gi + 1) * 512], xg.ap, r=[xg], w=[("o", "y")])
        k.barrier()
    return nc


_PROG = {}


def _pack_vec(v):
    v = np.asarray(v, np.float32).reshape(-1)
    return v.reshape(-1, 128).T


def kernel(**inp):
    debug_stage = inp.pop("_debug_stage", None)
    _trace = inp.pop("_trace", False)
    f = lambda a: np.ascontiguousarray(np.asarray(a, dtype=np.float32))
    x_prompt, x_sample = f(inp["x_prompt"]), f(inp["x_sample"])
    c, c_ctx = f(inp["c"]), f(inp["c_ctx"])
    if debug_stage not in _PROG:
        _PROG[debug_stage] = build_program(debug_stage)
    nc = _PROG[debug_stage]
    ident = np.eye(128, dtype=np.float32)
    pmat = np.zeros((64, 64), np.float32)
    for i in range(32):
        pmat[2 * i + 1, 2 * i] = -1.0
        pmat[2 * i, 2 * i + 1] = 1.0
    n_tok = 2048
    rows = np.repeat(np.arange(n_tok // 64, dtype=np.float32), 64)
    cols = np.tile(np.arange(64, dtype=np.float32), n_tok // 64)
    inv_freq = (np.float32(10000.0) ** (-np.arange(0, 32, 2, dtype=np.float32) / np.float32(32))).astype(np.float32)
    ang = np.concatenate([rows[:, None] * inv_freq, cols[:, None] * inv_freq], axis=-1).astype(np.float32)
    cosT = np.ascontiguousarray(np.repeat(np.cos(ang).T, 2, axis=0).astype(np.float32))
    sinT = np.ascontiguousarray(np.repeat(np.sin(ang).T, 2, axis=0).astype(np.float32))
    shared = {
        "ident": ident, "pmat": pmat, "cosT": cosT, "sinT": sinT,
        "w_mod0": f(inp["w_mod_l0"]), "w_mod1": f(inp["w_mod_l1"]),
        "w_in": f(inp["w_in_l0"]), "w_uq": f(inp["w_uq_l0"]), "w_ukv": f(inp["w_ukv_l0"]),
        "w_rg": f(inp["w_rg_l0"]), "w_ig": f(inp["w_ig_l0"]), "w_o": f(inp["w_o_l0"]),
        "w_pool": f(inp["w_pool_l1"]),
        "wq0": f(inp["peer_wq_l0"]), "wq1": f(inp["peer_wq_l1"]),
    }
    peer_in = ((inp["peer_keys_l0"], inp["peer_u_l0"], inp["peer_v_l0"]), (inp["peer_keys_l1"], inp["peer_u_l1"], inp["peer_v_l1"]))
    for l in range(2):
        keys = f(peer_in[l][0])
        shared["keysT%d" % l] = np.ascontiguousarray(keys.reshape(16, 128, 128).transpose(2, 0, 1))
        u = f(peer_in[l][1])
        v = f(peer_in[l][2])
        shared["Up%d" % l] = np.ascontiguousarray(u.reshape(128, 64, 2, 8, 128).transpose(1, 4, 3, 2, 0)).reshape(64, 128, 2048)
        shared["Vp%d" % l] = np.ascontiguousarray(v.reshape(128, 64, 2, 1024).transpose(1, 0, 2, 3)).reshape(64, 128, 2048)
    in_maps = []
    for core in range(NCORES):
        xcat = np.concatenate([x_sample[core], x_prompt[2 * core], x_prompt[2 * core + 1]], axis=0)
        vecs = np.zeros((128, NV), np.float32)
        vecs[:, C_CS:C_CS + 8] = _pack_vec(c[core])
        vecs[:, C_CP:C_CP + 8] = _pack_vec(c_ctx)
        for col, name in ((G_MIX0, "g_mix_l0"), (G_FFN0, "g_ffn_l0"), (G_MIX1, "g_mix_l1"), (G_FFN1, "g_ffn_l1"), (G_FIN, "g_final"), (S_POOL, "s_pool_l1")):
            vecs[:, col:col + 8] = _pack_vec(inp[name])
        vecs[:, B_MOD0:B_MOD0 + 48] = _pack_vec(inp["b_mod_l0"])
        vecs[:, B_MOD1:B_MOD1 + 48] = _pack_vec(inp["b_mod_l1"])
        vecs[:, G_Q:G_Q + 3] = _pack_vec(inp["g_q_l0"])
        vecs[:, G_KV:G_KV + 2] = _pack_vec(inp["g_kv_l0"])
        cw = f(inp["conv_w_l0"])
        for kk in range(4):
            vecs[:, CONV_W + kk * 4:CONV_W + kk * 4 + 4] = _pack_vec(cw[kk])
        vecs[:, CONV_B:CONV_B + 4] = _pack_vec(inp["conv_b_l0"])
        vecs[:, B_RG:B_RG + 8] = _pack_vec(inp["b_rg_l0"])
        vecs[:, B_IG:B_IG + 8] = _pack_vec(inp["b_ig_l0"])
        vecs[:, LAM:LAM + 8] = _pack_vec(inp["lam_l0"])
        vecs[:, H0:H0 + 8] = _pack_vec(f(inp["state_lru_l0"])[core])
        m = dict(shared)
        m["xT"] = np.ascontiguousarray(xcat.T)
        m["vecs"] = vecs
        m["cckvT"] = np.ascontiguousarray(f(inp["cache_ckv_l0"])[core].T)
        m["ckrT"] = np.ascontiguousarray(f(inp["cache_krope_l0"])[core].T)
        in_maps.append(m)
    if _trace:
        res = run_bass_kernel_spmd(nc, in_maps, core_ids=list(range(NCORES)), trace=True)
        print("EXEC_TIME_NS", res.exec_time_ns)
    else:
        res = run_bass_kernel_spmd(nc, in_maps, core_ids=list(range(NCORES)))
    y_prompt = np.zeros((16, 256, 1024), np.float32)
    y_sample = np.zeros((8, 2048, 1024), np.float32)
    new_ckv = np.zeros((16, 256, 256), np.float32)
    new_kr = np.zeros((16, 256, 64), np.float32)
    new_lru = np.zeros((16, 2, 512), np.float32)
    for core in range(NCORES):
        r = res.results[core]
        y = np.asarray(r["yT"]).T
        y_sample[core] = y[0:2048]
        y_prompt[2 * core] = y[2048:2304]
        y_prompt[2 * core + 1] = y[2304:2560]
        ck = np.asarray(r["ckv_o"]).T
        kr = np.asarray(r["kr_o"]).T
        lr = np.asarray(r["lru_o"])
        for bi in range(2):
            new_ckv[2 * core + bi] = ck[bi * 256:(bi + 1) * 256]
            new_kr[2 * core + bi] = kr[bi * 256:(bi + 1) * 256]
            new_lru[2 * core + bi] = lr[:, bi * 8:(bi + 1) * 8].reshape(128, 2, 4).transpose(1, 2, 0).reshape(2, 512)
    return (y_prompt, y_sample, new_ckv, new_kr, new_lru)
```

```python
import numpy as np
from contextlib import ExitStack
import concourse.bass as bass
import concourse.mybir as mybir
from concourse.bass_utils import run_bass_kernel_spmd

F32 = mybir.dt.float32
BF16 = mybir.dt.bfloat16
AF = mybir.ActivationFunctionType
ALU = mybir.AluOpType
AX = mybir.AxisListType

NCORES = 8
T = 2560
SEGS = [(0, 2048, 0, True), (2048, 256, 1, False), (2304, 256, 1, False)]
EPS = 1e-6
NV = 220
C_CS, C_CP = 0, 8
G_MIX0, G_FFN0, G_MIX1, G_FFN1, G_FIN, S_POOL = 16, 24, 32, 40, 48, 56
B_MOD0, B_MOD1 = 64, 112
G_Q, G_KV = 160, 163
CONV_W, CONV_B, B_RG, B_IG, LAM, H0 = 168, 184, 188, 196, 204, 212
ARENA = 53200
NDS = 20
ATT_SCALE = 192.0 ** -0.5


class Tl:
    def __init__(self, name, ap):
        self.name = name
        self.ap = ap

    def __getitem__(self, k):
        return self.ap[k]


def _view(ap, shape):
    if len(shape) == 1:
        return ap
    if len(shape) == 2:
        return ap.rearrange("p (a b) -> p a b", a=shape[0])
    if len(shape) == 3:
        return ap.rearrange("p (a b c) -> p a b c", a=shape[0], b=shape[1])
    if len(shape) == 4:
        return ap.rearrange("p (a b c d) -> p a b c d", a=shape[0], b=shape[1], c=shape[2])
    raise ValueError


class KB:
    def __init__(self, nc, es):
        self.nc = nc
        self.eng = dict(pe=nc.tensor, act=nc.scalar, dve=nc.vector, pool=nc.gpsimd, sp=nc.sync)
        self.sem = {e: es.enter_context(nc.semaphore("s_" + e)) for e in self.eng}
        self.cnt = {e: 0 for e in self.eng}
        self.seen = {e: {} for e in self.eng}
        self.dsem = [es.enter_context(nc.semaphore("d%d" % i)) for i in range(NDS)]
        self.dcnt = [0] * NDS
        self.dnext = 0
        self.lastw = {}
        self.readers = {}
        self.arena = es.enter_context(nc.sbuf_tensor("arena", [128, ARENA], F32))
        self.top = 0
        self.psum = es.enter_context(nc.psum_tensor("ps", [128, 4096], F32))
        self.uid = 0
        self.rr = 0
        self.bgsem = [es.enter_context(nc.semaphore("bg%d" % i)) for i in range(4)]
        self.bgcnt = [0] * 4

    def bg_cast_dma(self, si, out, in_):
        self.nc.gpsimd.dma_start(out=out, in_=in_).then_inc(self.bgsem[si], 16)
        self.bgcnt[si] += 16

    def bg_wait(self, e, si):
        self.eng[e].wait_ge(self.bgsem[si], self.bgcnt[si])

    def alloc(self, name, shape, dt=F32):
        n = int(np.prod(shape))
        words = n if dt == F32 else (n + 1) // 2
        words = (words + 15) // 16 * 16
        assert self.top + words <= ARENA, "arena overflow %s %d" % (name, self.top + words)
        ap = self.arena[:, self.top:self.top + words]
        if dt != F32:
            ap = ap.bitcast(dt)
        ap = ap[:, 0:n]
        self.top += words
        self.uid += 1
        return Tl("%s#%d" % (name, self.uid), _view(ap, shape))

    def bank(self, b, n=512, dt=F32):
        ap = self.psum[:, b * 512:(b + 1) * 512]
        if dt != F32:
            ap = ap.bitcast(dt)
        return ap[:, 0:n]

    def _wait(self, e, tok, raw):
        kind, src, n = tok
        if kind == 'e' and src == e:
            if not raw or e == 'pe' or e == 'sp':
                return
        key = (kind, src)
        if self.seen[e].get(key, 0) >= n:
            return
        sem = self.sem[src] if kind == 'e' else self.dsem[src]
        self.eng[e].wait_ge(sem, n)
        self.seen[e][key] = n

    def _keys(self, lst):
        out = []
        for k in lst:
            if isinstance(k, Tl):
                k = k.name
            out.append(k)
        return out

    def _deps(self, e, r, w):
        for k in r:
            t = self.lastw.get(k)
            if t is not None:
                self._wait(e, t, True)
        for k in w:
            t = self.lastw.get(k)
            if t is not None:
                self._wait(e, t, False)
            for (kd, src), n in self.readers.get(k, {}).items():
                self._wait(e, (kd, src, n), False)

    def _commit(self, tok, r, w):
        for k in w:
            self.lastw[k] = tok
            self.readers[k] = {}
        for k in r:
            d = self.readers.setdefault(k, {})
            d[(tok[0], tok[1])] = max(d.get((tok[0], tok[1]), 0), tok[2])

    def op(self, e, fn, r=(), w=(), inc=True):
        r = self._keys(r)
        w = self._keys(w)
        self._deps(e, r, w)
        inst = fn(self.eng[e])
        if inc:
            inst.then_inc(self.sem[e], 1)
            self.cnt[e] += 1
            self._commit(('e', e, self.cnt[e]), r, w)
        else:
            self._commit(('e', e, self.cnt[e] + 1), r, w)

    def dma(self, out, in_, r=(), w=(), q='sp'):
        r = self._keys(r)
        w = self._keys(w)
        self._deps(q, r, w)
        i = self.dnext
        self.dnext = (self.dnext + 1) % NDS
        if self.dcnt[i] > 0:
            self._wait(q, ('d', i, self.dcnt[i]), True)
        inst = self.eng[q].dma_start(out=out, in_=in_)
        inst.then_inc(self.dsem[i], 16)
        self.dcnt[i] += 16
        self._commit(('d', i, self.dcnt[i]), r, w)

    def barrier(self):
        for e in self.eng:
            for e2 in self.eng:
                if e2 != e and self.cnt[e2] > 0:
                    self._wait(e, ('e', e2, self.cnt[e2]), True)
            for i in range(NDS):
                if self.dcnt[i] > 0:
                    self._wait(e, ('d', i, self.dcnt[i]), True)
        self.lastw = {}
        self.readers = {}

    def any_eng(self):
        self.rr += 1
        return ('act', 'dve', 'pool')[self.rr % 3]

    def cast(self, e, out, in_, r, w):
        if e == 'act':
            self.op('act', lambda g: g.activation(out=out, in_=in_, func=AF.Copy), r=r, w=w)
        else:
            self.op(e, lambda g: g.tensor_copy(out=out, in_=in_), r=r, w=w)

    def load_cast(self, dst, dst_ap, src_ap, nfree, stage):
        step = 2048
        for c0 in range(0, nfree, step):
            n = min(step, nfree - c0)
            self.dma(stage.ap[:, 0:n], src_ap[:, c0:c0 + n], w=[stage])
            self.cast(self.any_eng(), dst_ap[:, c0:c0 + n], stage.ap[:, 0:n], r=[stage], w=[dst])


def build_program(debug_stage=None):
    nc = bass.Bass("TRN2", target_bir_lowering=False)
    D = {}

    def din(name, shape, dt=F32):
        D[name] = nc.dram_tensor(name, list(shape), dt, kind="ExternalInput").ap()
        return D[name]

    def dout(name, shape, dt=F32):
        D[name] = nc.dram_tensor(name, list(shape), dt, kind="ExternalOutput").ap()
        return D[name]

    def dscr(name, shape, dt=F32):
        D[name] = nc.dram_tensor(name, list(shape), dt).ap()
        return D[name]

    xT = din("xT", [1024, T])
    vecs_d = din("vecs", [128, NV])
    ident_d = din("ident", [128, 128])
    pmat_d = din("pmat", [64, 64])
    cos_d = din("cosT", [64, 2048])
    sin_d = din("sinT", [64, 2048])
    cckvT = din("cckvT", [256, 256])
    ckrT = din("ckrT", [64, 256])
    w_mod = [din("w_mod0", [1024, 6144]), din("w_mod1", [1024, 6144])]
    w_in = din("w_in", [1024, 1728])
    w_uq = din("w_uq", [384, 768])
    w_ukv = din("w_ukv", [256, 1024])
    w_rg = din("w_rg", [2, 4, 128, 128])
    w_ig = din("w_ig", [2, 4, 128, 128])
    w_o = din("w_o", [1024, 1024])
    w_pool = din("w_pool", [4, 256, 256])
    wq = [din("wq0", [1024, 2048]), din("wq1", [1024, 2048])]
    keysT = [din("keysT0", [128, 16, 128]), din("keysT1", [128, 16, 128])]
    Up = [din("Up0", [64, 128, 2048]), din("Up1", [64, 128, 2048])]
    Vp = [din("Vp0", [64, 128, 2048]), din("Vp1", [64, 128, 2048])]
    yT = dout("yT", [1024, T])
    ckv_o = dout("ckv_o", [256, 512])
    kr_o = dout("kr_o", [64, 512])
    lru_o = dout("lru_o", [128, 16])
    xres = dscr("xres", [1024, T])
    uxs = dscr("uxs", [512, T])
    gugs = dscr("gugs", [512, T], BF16)
    hTs = dscr("hTs", [1024, T], BF16)
    qTs = dscr("qTs", [128, 16, T], BF16)
    Ub = [dscr("Ub0", [64, 128, 2048], BF16), dscr("Ub1", [64, 128, 2048], BF16)]
    Vb = [dscr("Vb0", [64, 128, 2048], BF16), dscr("Vb1", [64, 128, 2048], BF16)]

    es = ExitStack()
    with es:
        k = KB(nc, es)
        ps = k.psum

        for l in range(2):
            for mi, (src, dst) in enumerate(((Up[l], Ub[l]), (Vp[l], Vb[l]))):
                for jb in range(0, 64, 4):
                    k.bg_cast_dma(l * 2 + mi, dst[jb:jb + 4].rearrange("j p f -> p j f"), src[jb:jb + 4].rearrange("j p f -> p j f"))

        vecs = k.alloc("vecs", [NV])
        identf = k.alloc("identf", [128])
        identb = k.alloc("identb", [128], BF16)
        onesf = k.alloc("onesf", [128])
        onesb = k.alloc("onesb", [128], BF16)
        epsv = k.alloc("epsv", [1])
        onev = k.alloc("onev", [1])
        der = k.alloc("der", [2, 2, 6, 8])
        nsp8 = k.alloc("nsp8", [8])
        k.dma(vecs.ap, vecs_d, w=[vecs])
        k.dma(identf.ap, ident_d, w=[identf])
        k.op('dve', lambda g: g.memset(onesf.ap, 1.0), w=[onesf])
        k.op('dve', lambda g: g.memset(epsv.ap, EPS), w=[epsv])
        k.op('dve', lambda g: g.memset(onev.ap, 1.0), w=[onev])
        k.cast('dve', onesb.ap, onesf.ap, r=[onesf], w=[onesb])
        k.cast('dve', identb.ap, identf.ap, r=[identf], w=[identb])
        k.op('act', lambda g: g.activation(out=nsp8.ap, in_=vecs.ap[:, LAM:LAM + 8], func=AF.Exp, scale=-1.0), r=[vecs], w=[nsp8])
        k.op('act', lambda g: g.activation(out=nsp8.ap, in_=nsp8.ap, func=AF.Ln, bias=onev.ap), r=[nsp8, onev], w=[nsp8])
        k.op('dve', lambda g: g.tensor_scalar(out=nsp8.ap, in0=nsp8.ap, scalar1=-8.0, scalar2=None, op0=ALU.mult), r=[nsp8], w=[nsp8])

        mark0 = k.top
        scT = k.alloc("scT", [8, 2])
        modT = k.alloc("modT", [2, 48, 2])
        k.op('act', lambda g: g.activation(out=scT.ap[:, :, 0], in_=vecs.ap[:, C_CS:C_CS + 8], func=AF.Silu), r=[vecs], w=[scT])
        k.op('act', lambda g: g.activation(out=scT.ap[:, :, 1], in_=vecs.ap[:, C_CP:C_CP + 8], func=AF.Silu), r=[vecs], w=[scT])
        wblk = [k.alloc("wblk%d" % i, [8, 512]) for i in range(2)]
        it = 0
        for l in range(2):
            wv = w_mod[l].rearrange("(kc p) n -> p kc n", p=128)
            bcol = B_MOD0 if l == 0 else B_MOD1
            for blk in range(12):
                wb = wblk[it % 2]
                k.dma(wb.ap, wv[:, :, blk * 512:(blk + 1) * 512], w=[wb])
                for cc in range(4):
                    b = (it * 4 + cc) % 8
                    pk = ('ps', b)
                    for kc in range(8):
                        k.op('pe', lambda g, wb=wb, cc=cc, kc=kc, b=b: g.matmul(
                            ps[:, b * 512:b * 512 + 2], lhsT=wb.ap[:, kc, cc * 128:(cc + 1) * 128],
                            rhs=scT.ap[:, kc, :], start=(kc == 0), stop=(kc == 7)),
                            r=[wb, scT], w=[pk])
                    ch = blk * 4 + cc
                    k.op('dve', lambda g, b=b, ch=ch, l=l, bcol=bcol: g.tensor_scalar(
                        out=modT.ap[:, l, ch, :], in0=ps[:, b * 512:b * 512 + 2],
                        scalar1=vecs.ap[:, bcol + ch:bcol + ch + 1], scalar2=None, op0=ALU.add),
                        r=[pk, vecs], w=[modT])
                it += 1
        gcols = {(0, 0): G_MIX0, (0, 3): G_FFN0, (1, 0): G_MIX1, (1, 3): G_FFN1}
        for l in range(2):
            for c in range(2):
                for (wh, sh_i, sc_i, gt_i) in ((0, 0, 1, 2), (3, 3, 4, 5)):
                    gc = gcols[(l, wh)]
                    k.op('dve', lambda g, l=l, c=c, wh=wh, sc_i=sc_i: g.tensor_scalar(
                        out=der.ap[:, l, c, wh, :], in0=modT.ap[:, l, sc_i * 8:sc_i * 8 + 8, c],
                        scalar1=1.0, scalar2=None, op0=ALU.add), r=[modT], w=[der])
                    k.op('dve', lambda g, l=l, c=c, wh=wh, gc=gc: g.tensor_tensor(
                        out=der.ap[:, l, c, wh, :], in0=der.ap[:, l, c, wh, :], in1=vecs.ap[:, gc:gc + 8],
                        op=ALU.mult), r=[der, vecs], w=[der])
                    k.op('dve', lambda g, l=l, c=c, wh=wh, sh_i=sh_i: g.tensor_copy(
                        out=der.ap[:, l, c, wh + 1, :], in_=modT.ap[:, l, sh_i * 8:sh_i * 8 + 8, c]), r=[modT], w=[der])
                    k.op('dve', lambda g, l=l, c=c, wh=wh, gt_i=gt_i: g.tensor_copy(
                        out=der.ap[:, l, c, wh + 2, :], in_=modT.ap[:, l, gt_i * 8:gt_i * 8 + 8, c]), r=[modT], w=[der])
        k.barrier()
        k.top = mark0

        def rms_rstd(xsq_ap_fn, nch, G, scale, rstd, pbank, rkeys):
            pk = ('ps', pbank)
            for c in range(nch):
                k.op('pe', lambda g, c=c: g.matmul(ps[:, pbank * 512:pbank * 512 + G], lhsT=onesf.ap,
                                                    rhs=xsq_ap_fn(c), start=(c == 0), stop=(c == nch - 1)),
                     r=rkeys + [onesf], w=[pk])
            k.op('act', lambda g: g.activation(out=rstd.ap[:, 0:G], in_=ps[:, pbank * 512:pbank * 512 + G],
                                                func=AF.Sqrt, scale=scale, bias=epsv.ap), r=[pk, epsv], w=[rstd])
            k.op('dve', lambda g: g.reciprocal(out=rstd.ap[:, 0:G], in_=rstd.ap[:, 0:G]), r=[rstd], w=[rstd])

        def norm_mod(xg, G, A_ap, B_ap, hT_out_fn, hkey, xsq, rstd, pbank):
            k.op('act', lambda g: g.activation(out=xsq.ap[:, :, 0:G], in_=xg.ap[:, :, 0:G], func=AF.Square), r=[xg], w=[xsq])
            rms_rstd(lambda c: xsq.ap[:, c, 0:G], 8, G, 1.0 / 1024.0, rstd, pbank, [xsq])
            k.op('dve', lambda g: g.tensor_tensor(out=xsq.ap[:, :, 0:G], in0=xg.ap[:, :, 0:G],
                                                  in1=rstd.ap[:, 0:G].unsqueeze(1).to_broadcast([128, 8, G]), op=ALU.mult),
                 r=[xg, rstd], w=[xsq])
            for dc in range(8):
                e = 'dve' if dc % 2 == 0 else 'pool'
                k.op(e, lambda g, dc=dc: g.tensor_scalar(out=hT_out_fn(dc), in0=xsq.ap[:, dc, 0:G],
                                                         scalar1=A_ap[:, dc:dc + 1], scalar2=B_ap[:, dc:dc + 1],
                                                         op0=ALU.mult, op1=ALU.add), r=[xsq, der], w=[hkey])

        xres_v = xres.rearrange("(dc p) t -> p dc t", p=128)
        xT_v = xT.rearrange("(dc p) t -> p dc t", p=128)
        yT_v = yT.rearrange("(dc p) t -> p dc t", p=128)
        hTs_v = hTs.rearrange("(dc p) t -> p dc t", p=128)
        uxs_v = uxs.rearrange("(n p) t -> p n t", p=128)
        gugs_v = gugs.rearrange("(n p) t -> p n t", p=128)

        markA = k.top
        stage = k.alloc("stage", [2048])
        w_uq_b = k.alloc("w_uq_b", [3, 768], BF16)
        w_ukv_b = k.alloc("w_ukv_b", [2, 1024], BF16)
        wrg_b = k.alloc("wrg_b", [8, 128], BF16)
        wig_b = k.alloc("wig_b", [8, 128], BF16)
        w_o_b = k.alloc("w_o_b", [8, 1024], BF16)
        pmat_b = k.alloc("pmat_b", [64], BF16)
        lru_t = k.alloc("lru_t", [16])
        for kc in range(3):
            k.load_cast(w_uq_b, w_uq_b.ap[:, kc, :], w_uq[kc * 128:(kc + 1) * 128, :], 768, stage)
        for kc in range(2):
            k.load_cast(w_ukv_b, w_ukv_b.ap[:, kc, :], w_ukv[kc * 128:(kc + 1) * 128, :], 1024, stage)
        for a in range(2):
            for n in range(4):
                k.load_cast(wrg_b, wrg_b.ap[:, a * 4 + n, :], w_rg[a, n], 128, stage)
                k.load_cast(wig_b, wig_b.ap[:, a * 4 + n, :], w_ig[a, n], 128, stage)
        for kc in range(8):
            k.load_cast(w_o_b, w_o_b.ap[:, kc, :], w_o[kc * 128:(kc + 1) * 128, :], 1024, stage)
        k.dma(stage.ap[0:64, 0:64], pmat_d, w=[stage])
        k.cast('dve', pmat_b.ap[0:64, :], stage.ap[0:64, 0:64], r=[stage], w=[pmat_b])
        k.op('dve', lambda g: g.memset(lru_t.ap, 0.0), w=[lru_t])
        k.barrier()
        markA2 = k.top

        for si, (t0, S, cond, has_ctx) in enumerate(SEGS):
            k.top = markA2
            G = min(512, S)
            NG = S // G
            Sk = S + (256 if has_ctx else 0)
            koff = 256 if has_ctx else 0
            NKC = Sk // 128
            A_m = der.ap[:, 0, cond, 0, :]
            B_m = der.ap[:, 0, cond, 1, :]
            G_m = der.ap[:, 0, cond, 2, :]
            attnT = k.alloc("attnT", [4, S], BF16)
            recT = k.alloc("recT", [4, S], BF16)
            qn = k.alloc("qn", [4, S], BF16)
            qr = k.alloc("qr", [4, S], BF16)
            ckvnT = k.alloc("ckvnT", [2, Sk], BF16)
            kropeT = k.alloc("kropeT", [Sk], BF16)
            markI = k.top
            G = min(256, S)
            NG = S // G
            w_in_b = k.alloc("w_in_b", [8, 1728], BF16)
            for kc in range(8):
                k.load_cast(w_in_b, w_in_b.ap[:, kc, :], w_in[kc * 128:(kc + 1) * 128, :], 1728, stage)
            xg = k.alloc("xg", [8, G])
            xsq = k.alloc("xsq", [8, G])
            rstd = k.alloc("rstd", [G])
            hTg = k.alloc("hTg", [8, G], BF16)
            cqf = k.alloc("cqf", [3, G])
            cqs = k.alloc("cqs", [3, G])
            cqn = k.alloc("cqn", [3, G], BF16)
            krf = k.alloc("krf", [G])
            krs = k.alloc("krs", [G])
            krb = k.alloc("krb", [G], BF16)
            uxt = k.alloc("uxt", [4, G])
            gut = k.alloc("gut", [4, G], BF16)
            if has_ctx:
                cosT = k.alloc("cosT", [S])
                sinT = k.alloc("sinT", [S])
                k.dma(cosT.ap[0:64, :], cos_d[:, 0:S], w=[cosT])
                k.dma(sinT.ap[0:64, :], sin_d[:, 0:S], w=[sinT])
                for kc in range(2):
                    k.dma(stage.ap[:, 0:256], cckvT[kc * 128:(kc + 1) * 128, :], w=[stage])
                    k.cast('dve', ckvnT.ap[:, kc, 0:256], stage.ap[:, 0:256], r=[stage], w=[ckvnT])
                k.dma(stage.ap[0:64, 0:256], ckrT, w=[stage])
                k.cast('dve', kropeT.ap[0:64, 0:256], stage.ap[0:64, 0:256], r=[stage], w=[kropeT])
            pb = 0

            def nb():
                nonlocal pb
                pb = (pb + 1) % 8
                return pb

            def rope(src_f, dst_bf_ap, dkey, g0, tmpf, tmpb):
                k.cast('act', tmpb.ap[0:64, 0:G], src_f.ap[0:64, 0:G], r=[src_f], w=[tmpb])
                b = nb()
                pk = ('ps', b)
                k.op('pe', lambda g: g.matmul(ps[0:64, b * 512:b * 512 + G], lhsT=pmat_b.ap[0:64, :], rhs=tmpb.ap[0:64, 0:G],
                                              start=True, stop=True), r=[pmat_b, tmpb], w=[pk])
                k.op('dve', lambda g: g.tensor_tensor(out=tmpf.ap[0:64, 0:G], in0=ps[0:64, b * 512:b * 512 + G],
                                                      in1=sinT.ap[0:64, g0:g0 + G], op=ALU.mult), r=[pk, sinT], w=[tmpf])
                k.op('dve', lambda g: g.tensor_tensor(out=src_f.ap[0:64, 0:G], in0=src_f.ap[0:64, 0:G],
                                                      in1=cosT.ap[0:64, g0:g0 + G], op=ALU.mult), r=[src_f, cosT], w=[src_f])
                k.op('dve', lambda g: g.tensor_tensor(out=dst_bf_ap, in0=src_f.ap[0:64, 0:G], in1=tmpf.ap[0:64, 0:G],
                                                      op=ALU.add), r=[src_f, tmpf], w=[dkey])

            for gi in range(NG):
                g0 = gi * G
                k.dma(xg.ap, xT_v[:, :, t0 + g0:t0 + g0 + G], w=[xg])
                norm_mod(xg, G, A_m, B_m, lambda dc: hTg.ap[:, dc, :], hTg, xsq, rstd, nb())

                def proj(c0, M):
                    b = nb()
                    pk = ('ps', b)
                    for kc in range(8):
                        k.op('pe', lambda g, kc=kc: g.matmul(ps[0:M, b * 512:b * 512 + G], lhsT=w_in_b.ap[:, kc, c0:c0 + M],
                                                             rhs=hTg.ap[:, kc, :], start=(kc == 0), stop=(kc == 7)),
                             r=[w_in_b, hTg], w=[pk])
                    return b, pk
                for c in range(3):
                    b, pk = proj(c * 128, 128)
                    k.op('act', lambda g, c=c, b=b: g.activation(out=cqf.ap[:, c, :], in_=ps[:, b * 512:b * 512 + G], func=AF.Copy), r=[pk], w=[cqf])
                k.op('act', lambda g: g.activation(out=cqs.ap, in_=cqf.ap, func=AF.Square), r=[cqf], w=[cqs])
                rms_rstd(lambda c: cqs.ap[:, c, :], 3, G, 1.0 / 384.0, rstd, nb(), [cqs])
                k.op('dve', lambda g: g.tensor_tensor(out=cqs.ap, in0=cqf.ap, in1=rstd.ap[:, 0:G].unsqueeze(1).to_broadcast([128, 3, G]),
                                                      op=ALU.mult), r=[cqf, rstd], w=[cqs])
                for c in range(3):
                    k.op('dve', lambda g, c=c: g.tensor_scalar(out=cqn.ap[:, c, :], in0=cqs.ap[:, c, :], scalar1=vecs.ap[:, G_Q + c:G_Q + c + 1],
                                                               scalar2=None, op0=ALU.mult), r=[cqs, vecs], w=[cqn])
                for h in range(4):
                    b = nb()
                    pk = ('ps', b)
                    for kc in range(3):
                        k.op('pe', lambda g, kc=kc, h=h, b=b: g.matmul(ps[:, b * 512:b * 512 + G], lhsT=w_uq_b.ap[:, kc, h * 192:h * 192 + 128],
                                                                       rhs=cqn.ap[:, kc, :], start=(kc == 0), stop=(kc == 2)),
                             r=[w_uq_b, cqn], w=[pk])
                    k.op('act', lambda g, h=h, b=b: g.activation(out=qn.ap[:, h, g0:g0 + G], in_=ps[:, b * 512:b * 512 + G], func=AF.Copy), r=[pk], w=[qn])
                    b = nb()
                    pk = ('ps', b)
                    for kc in range(3):
                        k.op('pe', lambda g, kc=kc, h=h, b=b: g.matmul(ps[0:64, b * 512:b * 512 + G], lhsT=w_uq_b.ap[:, kc, h * 192 + 128:h * 192 + 192],
                                                                       rhs=cqn.ap[:, kc, :], start=(kc == 0), stop=(kc == 2)),
                             r=[w_uq_b, cqn], w=[pk])
                    if has_ctx:
                        k.op('act', lambda g, b=b: g.activation(out=krf.ap[0:64, :], in_=ps[0:64, b * 512:b * 512 + G], func=AF.Copy), r=[pk], w=[krf])
                        rope(krf, qr.ap[0:64, h, g0:g0 + G], qr, g0, krs, krb)
                    else:
                        k.op('act', lambda g, h=h, b=b: g.activation(out=qr.ap[0:64, h, g0:g0 + G], in_=ps[0:64, b * 512:b * 512 + G], func=AF.Copy), r=[pk], w=[qr])
                for c in range(2):
                    b, pk = proj(384 + c * 128, 128)
                    k.op('act', lambda g, c=c, b=b: g.activation(out=cqf.ap[:, c, :], in_=ps[:, b * 512:b * 512 + G], func=AF.Copy), r=[pk], w=[cqf])
                k.op('act', lambda g: g.activation(out=cqs.ap[:, 0:2, :], in_=cqf.ap[:, 0:2, :], func=AF.Square), r=[cqf], w=[cqs])
                rms_rstd(lambda c: cqs.ap[:, c, :], 2, G, 1.0 / 256.0, rstd, nb(), [cqs])
                k.op('dve', lambda g: g.tensor_tensor(out=cqs.ap[:, 0:2, :], in0=cqf.ap[:, 0:2, :],
                                                      in1=rstd.ap[:, 0:G].unsqueeze(1).to_broadcast([128, 2, G]), op=ALU.mult), r=[cqf, rstd], w=[cqs])
                for c in range(2):
                    k.op('dve', lambda g, c=c: g.tensor_scalar(out=cqf.ap[:, c, :], in0=cqs.ap[:, c, :], scalar1=vecs.ap[:, G_KV + c:G_KV + c + 1],
                                                               scalar2=None, op0=ALU.mult), r=[cqs, vecs], w=[cqf])
                    k.cast('act', ckvnT.ap[:, c, koff + g0:koff + g0 + G], cqf.ap[:, c, :], r=[cqf], w=[ckvnT])
                    if not has_ctx:
                        k.dma(ckv_o[c * 128:(c + 1) * 128, (si - 1) * 256:(si - 1) * 256 + G], cqf.ap[:, c, :], r=[cqf], w=[("o", "ckv")])
                b, pk = proj(640, 64)
                k.op('act', lambda g, b=b: g.activation(out=krf.ap[0:64, :], in_=ps[0:64, b * 512:b * 512 + G], func=AF.Copy), r=[pk], w=[krf])
                if has_ctx:
                    rope(krf, kropeT.ap[0:64, koff + g0:koff + g0 + G], kropeT, g0, krs, krb)
                else:
                    k.cast('dve', kropeT.ap[0:64, g0:g0 + G], krf.ap[0:64, :], r=[krf], w=[kropeT])
                    k.dma(kr_o[:, (si - 1) * 256:(si - 1) * 256 + G], krf.ap[0:64, :], r=[krf], w=[("o", "kr")])
                for n in range(4):
                    b, pk = proj(704 + n * 128, 128)
                    k.op('act', lambda g, n=n, b=b: g.activation(out=uxt.ap[:, n, :], in_=ps[:, b * 512:b * 512 + G], func=AF.Copy), r=[pk], w=[uxt])
                    b, pk = proj(1216 + n * 128, 128)
                    k.op('act', lambda g, n=n, b=b: g.activation(out=gut.ap[:, n, :], in_=ps[:, b * 512:b * 512 + G], func=AF.Gelu_apprx_tanh), r=[pk], w=[gut])
                k.dma(uxs_v[:, :, t0 + g0:t0 + g0 + G], uxt.ap, r=[uxt], w=[("scr", "uxs")])
                k.dma(gugs_v[:, :, t0 + g0:t0 + g0 + G], gut.ap, r=[gut], w=[("scr", "gugs")])
            k.barrier()
            k.top = markI
            G = min(512, S)
            NG = S // G
            knT = k.alloc("knT", [4, Sk], BF16)
            Vt = k.alloc("Vt", [NKC, 512], BF16)
            pT = [k.alloc("pT%d" % i, [G], BF16) for i in range(2)]
            rden = k.alloc("rden", [G])
            KG = min(512, Sk)
            for h in range(4):
                for kg0 in range(0, Sk, KG):
                    kn = min(KG, Sk - kg0)
                    b = nb()
                    pk = ('ps', b)
                    for kc in range(2):
                        k.op('pe', lambda g, kc=kc, h=h, b=b, kg0=kg0, kn=kn: g.matmul(
                            ps[:, b * 512:b * 512 + kn], lhsT=w_ukv_b.ap[:, kc, h * 256:h * 256 + 128],
                            rhs=ckvnT.ap[:, kc, kg0:kg0 + kn], start=(kc == 0), stop=(kc == 1)), r=[w_ukv_b, ckvnT], w=[pk])
                    k.op('act', lambda g, h=h, b=b, kg0=kg0, kn=kn: g.activation(out=knT.ap[:, h, kg0:kg0 + kn], in_=ps[:, b * 512:b * 512 + kn], func=AF.Copy), r=[pk], w=[knT])
            for kc_ in range(NKC):
                b = nb()
                pk = ('ps', b)
                for h in range(4):
                    for kc in range(2):
                        k.op('pe', lambda g, kc=kc, h=h, b=b, kc_=kc_: g.matmul(
                            ps[:, b * 512 + h * 128:b * 512 + (h + 1) * 128], lhsT=ckvnT.ap[:, kc, kc_ * 128:(kc_ + 1) * 128],
                            rhs=w_ukv_b.ap[:, kc, h * 256 + 128:h * 256 + 256], start=(kc == 0), stop=(kc == 1)), r=[w_ukv_b, ckvnT], w=[pk])
                k.op('dve', lambda g, b=b, kc_=kc_: g.tensor_copy(out=Vt.ap[:, kc_, :], in_=ps[:, b * 512:(b + 1) * 512]), r=[pk], w=[Vt])
            for h in range(4):
                for gi in range(NG):
                    g0 = gi * G
                    bo, bd = 6, 7
                    def s_exp(kc_):
                        b = kc_ % 4
                        pk = ('ps', b)
                        p = pT[kc_ % 2]
                        k.op('pe', lambda g: g.matmul(
                            ps[:, b * 512:b * 512 + G], lhsT=knT.ap[:, h, kc_ * 128:(kc_ + 1) * 128], rhs=qn.ap[:, h, g0:g0 + G],
                            start=True, stop=False), r=[knT, qn], w=[pk])
                        k.op('pe', lambda g: g.matmul(
                            ps[:, b * 512:b * 512 + G], lhsT=kropeT.ap[0:64, kc_ * 128:(kc_ + 1) * 128], rhs=qr.ap[0:64, h, g0:g0 + G],
                            start=False, stop=True), r=[kropeT, qr], w=[pk])
                        k.op('act', lambda g: g.activation(out=p.ap, in_=ps[:, b * 512:b * 512 + G], func=AF.Exp, scale=ATT_SCALE), r=[pk], w=[p])

                    def pv(kc_):
                        p = pT[kc_ % 2]
                        k.op('pe', lambda g: g.matmul(
                            ps[:, bo * 512:bo * 512 + G], lhsT=Vt.ap[:, kc_, h * 128:(h + 1) * 128], rhs=p.ap,
                            start=(kc_ == 0), stop=(kc_ == NKC - 1)), r=[Vt, p], w=[('ps', bo)])
                        k.op('pe', lambda g: g.matmul(
                            ps[:, bd * 512:bd * 512 + G], lhsT=onesb.ap, rhs=p.ap,
                            start=(kc_ == 0), stop=(kc_ == NKC - 1)), r=[onesb, p], w=[('ps', bd)])

                    s_exp(0)
                    for kc_ in range(NKC):
                        if kc_ + 1 < NKC:
                            s_exp(kc_ + 1)
                        pv(kc_)
                    k.op('dve', lambda g: g.reciprocal(out=rden.ap, in_=ps[:, bd * 512:bd * 512 + G]), r=[('ps', bd)], w=[rden])
                    k.op('dve', lambda g, h=h, g0=g0: g.tensor_tensor(out=attnT.ap[:, h, g0:g0 + G], in0=ps[:, bo * 512:bo * 512 + G],
                                                                      in1=rden.ap, op=ALU.mult), r=[('ps', bo), rden], w=[attnT])
            k.barrier()
            k.top = markI
            uxp = k.alloc("uxp", [S + 4])
            gub = k.alloc("gub", [S], BF16)
            xc = k.alloc("xc", [S])
            xcb = k.alloc("xcb", [S], BF16)
            at = k.alloc("at", [S])
            bx = k.alloc("bx", [S])
            tmp = k.alloc("tmp", [S])
            hd = [k.alloc("hf", [S]), k.alloc("hb", [S])]
            for n in range(4):
                k.op('pool', lambda g: g.memset(uxp.ap[:, 0:2], 0.0), w=[uxp])
                k.op('pool', lambda g: g.memset(uxp.ap[:, S + 2:S + 4], 0.0), w=[uxp])
                k.dma(uxp.ap[:, 2:S + 2], uxs_v[:, n, t0:t0 + S], r=[("scr", "uxs")], w=[uxp])
                k.dma(gub.ap, gugs_v[:, n, t0:t0 + S], r=[("scr", "gugs")], w=[gub])
                cw = lambda kk: vecs.ap[:, CONV_W + kk * 4 + n:CONV_W + kk * 4 + n + 1]
                k.op('dve', lambda g: g.tensor_scalar(out=xc.ap, in0=uxp.ap[:, 0:S], scalar1=cw(0), scalar2=vecs.ap[:, CONV_B + n:CONV_B + n + 1],
                                                      op0=ALU.mult, op1=ALU.add), r=[uxp, vecs], w=[xc])
                for kk in range(1, 4):
                    k.op('dve', lambda g, kk=kk: g.scalar_tensor_tensor(out=xc.ap, in0=uxp.ap[:, kk:kk + S], scalar=cw(kk), in1=xc.ap,
                                                                        op0=ALU.mult, op1=ALU.add), r=[uxp, vecs, xc], w=[xc])
                k.cast('act', xcb.ap, xc.ap, r=[xc], w=[xcb])
                for a in range(2):
                    idx = a * 4 + n
                    for gi in range(NG):
                        g0 = gi * G
                        b = nb()
                        pk = ('ps', b)
                        k.op('pe', lambda g, b=b, g0=g0: g.matmul(ps[:, b * 512:b * 512 + G], lhsT=wrg_b.ap[:, idx, :], rhs=xcb.ap[:, g0:g0 + G],
                                                                 start=True, stop=True), r=[wrg_b, xcb], w=[pk])
                        k.op('act', lambda g, b=b, g0=g0: g.activation(out=tmp.ap[:, g0:g0 + G], in_=ps[:, b * 512:b * 512 + G], func=AF.Sigmoid,
                                                                       bias=vecs.ap[:, B_RG + idx:B_RG + idx + 1]), r=[pk, vecs], w=[tmp])
                        k.op('act', lambda g, g0=g0: g.activation(out=at.ap[:, g0:g0 + G], in_=tmp.ap[:, g0:g0 + G], func=AF.Exp,
                                                                  scale=nsp8.ap[:, idx:idx + 1]), r=[tmp, nsp8], w=[at])
                        b = nb()
                        pk = ('ps', b)
                        k.op('pe', lambda g, b=b, g0=g0: g.matmul(ps[:, b * 512:b * 512 + G], lhsT=wig_b.ap[:, idx, :], rhs=xcb.ap[:, g0:g0 + G],
                                                                 start=True, stop=True), r=[wig_b, xcb], w=[pk])
                        k.op('act', lambda g, b=b, g0=g0: g.activation(out=bx.ap[:, g0:g0 + G], in_=ps[:, b * 512:b * 512 + G], func=AF.Sigmoid,
                                                                       bias=vecs.ap[:, B_IG + idx:B_IG + idx + 1]), r=[pk, vecs], w=[bx])
                    k.op('pool', lambda g: g.tensor_tensor(out=bx.ap, in0=bx.ap, in1=xc.ap, op=ALU.mult), r=[bx, xc], w=[bx])
                    k.op('dve', lambda g: g.tensor_tensor(out=tmp.ap, in0=at.ap, in1=at.ap, op=ALU.mult), r=[at], w=[tmp])
                    k.op('dve', lambda g: g.tensor_scalar(out=tmp.ap, in0=tmp.ap, scalar1=-1.0, scalar2=1.0, op0=ALU.mult, op1=ALU.add), r=[tmp], w=[tmp])
                    k.op('act', lambda g: g.activation(out=tmp.ap, in_=tmp.ap, func=AF.Sqrt), r=[tmp], w=[tmp])
                    k.op('dve', lambda g: g.tensor_tensor(out=bx.ap, in0=bx.ap, in1=tmp.ap, op=ALU.mult), r=[bx, tmp], w=[bx])
                    init = vecs.ap[:, H0 + idx:H0 + idx + 1] if has_ctx else 0.0
                    hh = hd[a]
                    if a == 0:
                        k.op('dve', lambda g, hh=hh: g.tensor_tensor_scan(out=hh.ap, data0=at.ap, data1=bx.ap, initial=init, op0=ALU.mult, op1=ALU.add),
                             r=[at, bx, vecs], w=[hh])
                    else:
                        k.op('dve', lambda g, hh=hh: g.tensor_tensor_scan(out=hh.ap[:, ::-1], data0=at.ap[:, ::-1], data1=bx.ap[:, ::-1], initial=init,
                                                                          op0=ALU.mult, op1=ALU.add), r=[at, bx, vecs], w=[hh])
                if not has_ctx:
                    col = (si - 1) * 8
                    k.op('pool', lambda g: g.tensor_copy(out=lru_t.ap[:, col + n:col + n + 1], in_=hd[0].ap[:, S - 1:S]), r=[hd[0]], w=[lru_t])
                    k.op('pool', lambda g: g.tensor_copy(out=lru_t.ap[:, col + 4 + n:col + 4 + n + 1], in_=hd[1].ap[:, 0:1]), r=[hd[1]], w=[lru_t])
                k.op('dve', lambda g: g.tensor_tensor(out=tmp.ap, in0=hd[0].ap, in1=hd[1].ap, op=ALU.add), r=[hd[0], hd[1]], w=[tmp])
                k.op('dve', lambda g, n=n: g.tensor_tensor(out=recT.ap[:, n, :], in0=tmp.ap, in1=gub.ap, op=ALU.mult), r=[tmp, gub], w=[recT])
            k.barrier()
            k.top = markI
            xg2 = [k.alloc("xg2_%d" % i, [8, G]) for i in range(2)]
            for gi in range(NG):
                g0 = gi * G
                xo = xg2[gi % 2]
                k.dma(xo.ap, xT_v[:, :, t0 + g0:t0 + g0 + G], w=[xo])
                for oc in range(8):
                    b = nb()
                    pk = ('ps', b)
                    for kk in range(8):
                        rhs = attnT.ap[:, kk, g0:g0 + G] if kk < 4 else recT.ap[:, kk - 4, g0:g0 + G]
                        k.op('pe', lambda g, kk=kk, rhs=rhs, b=b, oc=oc: g.matmul(ps[:, b * 512:b * 512 + G], lhsT=w_o_b.ap[:, kk, oc * 128:(oc + 1) * 128],
                                                                                  rhs=rhs, start=(kk == 0), stop=(kk == 7)), r=[w_o_b, attnT, recT], w=[pk])
                    k.op('dve', lambda g, b=b, oc=oc, xo=xo: g.scalar_tensor_tensor(out=xo.ap[:, oc, :], in0=ps[:, b * 512:b * 512 + G], scalar=G_m[:, oc:oc + 1],
                                                                                    in1=xo.ap[:, oc, :], op0=ALU.mult, op1=ALU.add), r=[pk, xo, der], w=[xo])
                k.dma(xres_v[:, :, t0 + g0:t0 + g0 + G], xo.ap, r=[xo], w=[("scr", "xres")])
            k.barrier()
        k.dma(lru_o, lru_t.ap, r=[lru_t], w=[("o", "lru")])
        k.barrier()
        k.top = markA

        def peer(l, final):
            mark = k.top
            stage = k.alloc("stageP", [2048])
            wq_b = k.alloc("wq_b", [8, 2048], BF16)
            for kc in range(8):
                k.load_cast(wq_b, wq_b.ap[:, kc, :], wq[l][kc * 128:(kc + 1) * 128, :], 2048, stage)
            G = 512
            xg2 = [k.alloc("xgP%d" % i, [8, G]) for i in range(2)]
            xsq2 = [k.alloc("xsqP%d" % i, [8, G]) for i in range(2)]
            rstd2 = [k.alloc("rstdP%d" % i, [G]) for i in range(2)]
            hTg2 = [k.alloc("hTgP%d" % i, [8, G], BF16) for i in range(2)]
            qTg2 = [k.alloc("qTgP%d" % i, [16, G], BF16) for i in range(2)]
            pb = 0
            for gi in range(T // G):
                xg, xsq, rstd, hTg, qTg = xg2[gi % 2], xsq2[gi % 2], rstd2[gi % 2], hTg2[gi % 2], qTg2[gi % 2]
                g0 = gi * G
                cond = 0 if g0 < 2048 else 1
                k.dma(xg.ap, xres_v[:, :, g0:g0 + G], r=[("scr", "xres")], w=[xg])
                norm_mod(xg, G, der.ap[:, l, cond, 3, :], der.ap[:, l, cond, 4, :], lambda dc, hTg=hTg: hTg.ap[:, dc, :], hTg, xsq, rstd, 6 + gi % 2)
                k.dma(hTs_v[:, :, g0:g0 + G], hTg.ap, r=[hTg], w=[("scr", "hTs")])
                for hp in range(16):
                    b = pb = (pb + 1) % 6
                    pk = ('ps', b)
                    for kc in range(8):
                        k.op('pe', lambda g, kc=kc, hp=hp, b=b: g.matmul(ps[:, b * 512:(b + 1) * 512], lhsT=wq_b.ap[:, kc, hp * 128:(hp + 1) * 128],
                                                                         rhs=hTg.ap[:, kc, :], start=(kc == 0), stop=(kc == 7)), r=[wq_b, hTg], w=[pk])
                    k.cast('act' if hp % 2 else 'dve', qTg.ap[:, hp, :], ps[:, b * 512:(b + 1) * 512], r=[pk], w=[qTg])
                k.dma(qTs[:, :, g0:g0 + G], qTg.ap, r=[qTg], w=[("scr", "qTs")])
            k.barrier()
            k.top = mark
            keys_b = k.alloc("keys_b", [16, 128], BF16)
            hT = k.alloc("hT", [2, 8, 128], BF16)
            s_sb = k.alloc("s_sb", [8, 2, 128])
            sv = k.alloc("sv", [8, 2, 16])
            fv = k.alloc("fv", [8, 16])
            ef = k.alloc("ef", [8, 16])
            sm = k.alloc("sm", [8, 8])
            th = k.alloc("th", [8, 16])
            cc = k.alloc("cc", [8, 16], BF16)
            e2 = k.alloc("e2", [8, 128], BF16)
            Rm = k.alloc("Rm", [128, 128], BF16)
            P1 = k.alloc("P1", [128, 128], BF16)
            RT = k.alloc("RT", [128, 64], BF16)
            P1T = k.alloc("P1T", [128, 64], BF16)
            W = k.alloc("W", [2, 128, 128], BF16)
            JB = 2
            NJ = 128 // JB
            HALF = NJ // 2
            ut = [k.alloc("ut%d" % i, [8, JB * 128], BF16) for i in range(3)]
            vt = [k.alloc("vt%d" % i, [JB, 1024], BF16) for i in range(2)]
            gl = [k.alloc("gl%d" % i, [JB * 128], BF16) for i in range(2)]
            wa = [k.alloc("wa%d" % i, [JB, 128], BF16) for i in range(3)]
            rstd = k.alloc("rstdT", [128])
            RT_f = RT.ap.rearrange("p a b -> p (a b)").bitcast(F32)
            P1T_f = P1T.ap.rearrange("p a b -> p (a b)").bitcast(F32)
            xt = Tl(RT.name, RT_f[:, 0:1024].rearrange("p (a b) -> p a b", a=8))
            xsq = Tl(RT.name, RT_f[:, 1024:2048].rearrange("p (a b) -> p a b", a=8))
            qT = Tl(P1T.name, P1T.ap.rearrange("p a b -> p (a b)")[:, 0:2048].rearrange("p (a b) -> p a b", a=16))
            e2f = Tl(P1T.name, P1T_f[:, 1024:2048].rearrange("p (a b) -> p a b", a=8))
            Rm_f = Rm.ap.rearrange("p a b -> p (a b)").bitcast(F32)
            work = Rm_f[:, 0:2048].rearrange("p (h q n) -> p h q n", h=8, q=2)
            cand = Rm_f[:, 2048:4096].rearrange("p (h a b) -> p h a b", h=8, a=16)
            candw = Rm_f[:, 4096:6144].rearrange("p (h a b) -> p h a b", h=8, a=16)
            k.dma(Rm_f[:, 0:2048], keysT[l].rearrange("p a b -> p (a b)"), w=[Rm])
            k.cast('dve', keys_b.ap.rearrange("p a b -> p (a b)"), Rm_f[:, 0:2048], r=[Rm], w=[keys_b])
            Ubv = Ub[l]
            Vbv = Vb[l]
            psb = ps[:, :].bitcast(BF16)
            NT = T // 128
            Rv = Rm.ap.rearrange("p j (h k) -> p j h k", h=8)
            P1v = P1.ap.rearrange("p i (h k) -> p i h k", h=8)
            s_flat = s_sb.ap.rearrange("p h q n -> p (h q n)")
            BANK1 = []

            def route_a(ti):
                t0 = ti * 128
                k.dma(qT.ap, qTs[:, :, t0:t0 + 128], r=[("scr", "qTs")], w=[qT])
                for rd in range(4):
                    for q4 in range(4):
                        hp = rd * 4 + q4
                        k.op('pe', lambda g, hp=hp, q4=q4: g.matmul(ps[:, q4 * 128:(q4 + 1) * 128], lhsT=qT.ap[:, hp, :], rhs=keys_b.ap[:, hp, :],
                                                                    start=True, stop=True), r=[qT, keys_b], w=[('ps', 0)], inc=(q4 == 3))
                    k.op('act', lambda g, rd=rd: g.activation(out=s_flat[:, rd * 512:(rd + 1) * 512], in_=ps[:, 0:512], func=AF.Copy),
                         r=[('ps', 0)], w=[('s', rd), s_sb])
                    yield 0.3
                for hp in range(16):
                    h_, p_ = hp // 2, hp % 2
                    k.op('dve', lambda g, h_=h_, p_=p_: g.max(out=sv.ap[:, h_, p_, 0:8], in_=s_sb.ap[:, h_, p_, :]), r=[s_sb], w=[('sv', hp)])
                    if hp % 2:
                        yield 0.55
                for hp in range(16):
                    h_, p_ = hp // 2, hp % 2
                    k.op('dve', lambda g, h_=h_, p_=p_: g.match_replace(out=work[:, h_, p_, :], in_to_replace=sv.ap[:, h_, p_, 0:8],
                                                                        in_values=s_sb.ap[:, h_, p_, :], imm_value=-1e30),
                         r=[s_sb, ('sv', hp)], w=[('wk', hp), Rm])
                    if hp % 2:
                        yield 0.55
                for hp in range(16):
                    h_, p_ = hp // 2, hp % 2
                    k.op('dve', lambda g, h_=h_, p_=p_: g.max(out=sv.ap[:, h_, p_, 8:16], in_=work[:, h_, p_, :]), r=[('wk', hp)], w=[('sv', hp), sv])
                    if hp % 2:
                        yield 0.55
                k.op('dve', lambda g: g.tensor_tensor(out=cand, in0=sv.ap[:, :, 0, :].unsqueeze(3).to_broadcast([128, 8, 16, 16]),
                                                      in1=sv.ap[:, :, 1, :].unsqueeze(2).to_broadcast([128, 8, 16, 16]), op=ALU.add),
                     r=[sv] + [('sv', i) for i in range(16)], w=[('cand',)])
                yield 0.6
                for h_ in range(8):
                    k.op('dve', lambda g, h_=h_: g.max(out=fv.ap[:, h_, 0:8], in_=cand[:, h_]), r=[('cand',)], w=[('fv', h_)])
                    if h_ % 2:
                        yield 0.55
                for h_ in range(8):
                    k.op('dve', lambda g, h_=h_: g.match_replace(out=candw[:, h_], in_to_replace=fv.ap[:, h_, 0:8], in_values=cand[:, h_], imm_value=-1e30),
                         r=[('cand',), ('fv', h_)], w=[('cw', h_)])
                    if h_ % 2:
                        yield 0.55
                for h_ in range(8):
                    k.op('dve', lambda g, h_=h_: g.max(out=fv.ap[:, h_, 8:16], in_=candw[:, h_]), r=[('cw', h_)], w=[('fv', h_), fv])
                    if h_ % 2:
                        yield 0.55
                fvk = [fv] + [('fv', i) for i in range(8)]
                k.op('dve', lambda g: g.tensor_tensor(out=ef.ap, in0=fv.ap, in1=fv.ap[:, :, 0:1].to_broadcast([128, 8, 16]), op=ALU.subtract), r=fvk, w=[ef])
                k.op('act', lambda g: g.activation(out=ef.ap, in_=ef.ap, func=AF.Exp), r=[ef], w=[ef])
                yield 0.6
                k.op('dve', lambda g: g.tensor_reduce(out=sm.ap[:, :, 0], in_=ef.ap, axis=AX.X, op=ALU.add), r=[ef], w=[sm])
                k.op('dve', lambda g: g.reciprocal(out=sm.ap[:, :, 1], in_=sm.ap[:, :, 0]), r=[sm], w=[sm])
                k.op('dve', lambda g: g.tensor_scalar(out=sm.ap[:, :, 2], in0=fv.ap[:, :, 15], scalar1=-1e-5, scalar2=None, op0=ALU.add), r=fvk, w=[sm])
                yield 0.6
                k.op('dve', lambda g: g.tensor_tensor(out=th.ap, in0=sm.ap[:, :, 2:3].to_broadcast([128, 8, 16]), in1=sv.ap[:, :, 0, :], op=ALU.subtract),
                     r=[sm, sv], w=[th])
                k.op('dve', lambda g: g.tensor_tensor(out=ef.ap, in0=sv.ap[:, :, 0, :], in1=sv.ap[:, :, 0, 0:1].to_broadcast([128, 8, 16]), op=ALU.subtract),
                     r=[sv], w=[ef])
                k.op('act', lambda g: g.activation(out=ef.ap, in_=ef.ap, func=AF.Exp), r=[ef], w=[ef])
                yield 0.6
                k.op('dve', lambda g: g.tensor_tensor(out=cc.ap, in0=ef.ap, in1=sm.ap[:, :, 1:2].to_broadcast([128, 8, 16]), op=ALU.mult), r=[ef, sm], w=[cc])
                k.op('dve', lambda g: g.tensor_tensor(out=e2f.ap, in0=s_sb.ap[:, :, 1, :], in1=sv.ap[:, :, 1, 0:1].to_broadcast([128, 8, 128]), op=ALU.subtract),
                     r=[s_sb, sv], w=[e2f])
                k.op('act', lambda g: g.activation(out=e2.ap, in_=e2f.ap, func=AF.Exp), r=[e2f], w=[e2])
                yield 0.6
                NCH = 8
                CJ = 128 // NCH
                alias_keys = [('cand',)] + [('cw', i) for i in range(8)] + [('wk', i) for i in range(16)]
                for c in range(NCH):
                    js = slice(c * CJ, (c + 1) * CJ)
                    k.op('dve', lambda g, js=js: g.tensor_tensor(out=Rv[:, js], in0=s_sb.ap[:, :, 1, js].rearrange("p h j -> p j h").unsqueeze(3).to_broadcast([128, CJ, 8, 16]),
                                                                 in1=th.ap.unsqueeze(1).to_broadcast([128, CJ, 8, 16]), op=ALU.is_ge),
                         r=[s_sb, th] + alias_keys, w=[('R', c)])
                    k.op('pool', lambda g, js=js: g.tensor_tensor(out=Rv[:, js], in0=Rv[:, js], in1=e2.ap[:, :, js].rearrange("p h j -> p j h").unsqueeze(3).to_broadcast([128, CJ, 8, 16]),
                                                                  op=ALU.mult), r=[('R', c), e2], w=[('R', c)])
                    k.op('pool', lambda g, js=js, c=c: g.tensor_tensor(out=Rv[:, js], in0=Rv[:, js], in1=cc.ap.unsqueeze(1).to_broadcast([128, CJ, 8, 16]), op=ALU.mult),
                         r=[('R', c), cc], w=[('R', c), Rm] if c == NCH - 1 else [('R', c)])
                    yield 4.0
                for c in range(NCH):
                    js = slice(c * CJ, (c + 1) * CJ)
                    k.op('dve', lambda g, js=js: g.tensor_tensor(out=P1v[:, js], in0=s_sb.ap[:, :, 0, js].rearrange("p h i -> p i h").unsqueeze(3).to_broadcast([128, CJ, 8, 16]),
                                                                 in1=sv.ap[:, :, 0, :].unsqueeze(1).to_broadcast([128, CJ, 8, 16]), op=ALU.is_equal),
                         r=[s_sb, sv], w=[P1])
                    yield 4.0

            def route_b(ti):
                Rkeys = [Rm] + [('R', c) for c in range(8)]
                wp = ti % 2
                for hf in range(2):
                    tp = slice(hf * 64, hf * 64 + 64)
                    for (src, dstT, rk) in ((Rm, RT, Rkeys), (P1, P1T, [P1])):
                        for rnd in range(4):
                            banks = [0, 1] if rnd % 2 == 0 else [2, 3]
                            pk = [('ps', bb) for bb in banks] + (BANK1 if rnd % 2 == 0 else [])
                            base = banks[0] * 1024
                            for jj in range(32):
                                j = rnd * 32 + jj
                                k.op('pe', lambda g, j=j, jj=jj, src=src, base=base: g.transpose(psb[:, base + jj * 64:base + (jj + 1) * 64], src.ap[tp, j, :], identb.ap[tp, tp]),
                                     r=rk + [identb], w=pk, inc=(jj == 31))
                            k.cast('act', dstT.ap[:, rnd * 32:(rnd + 1) * 32, :].rearrange("p j t -> p (j t)"), psb[:, base:base + 2048], r=pk, w=[dstT])
                    for rnd in range(8):
                        bb0 = 0 if rnd % 2 == 0 else 2
                        pk = [('ps', bb0), ('ps', bb0 + 1)] + (BANK1 if bb0 == 0 else [])
                        for tt in range(8):
                            tl = rnd * 8 + tt
                            k.op('pe', lambda g, tl=tl, tt=tt, bb0=bb0: g.matmul(ps[:, bb0 * 512 + tt * 128:bb0 * 512 + (tt + 1) * 128], lhsT=P1T.ap[:, :, tl], rhs=RT.ap[:, :, tl],
                                                                                 start=True, stop=True), r=[P1T, RT], w=pk, inc=(tt == 7))
                        k.cast('act' if rnd % 2 == 0 else 'dve', W.ap[:, wp, hf * 64 + rnd * 8:hf * 64 + (rnd + 1) * 8, :].rearrange("p t j -> p (t j)"),
                               ps[:, bb0 * 512:bb0 * 512 + 1024], r=pk, w=[('W', wp)])

            def tail(n):
                t0 = n * 128
                cond = 0 if t0 < 2048 else 1
                G_f = der.ap[:, l, cond, 5, :]
                ob = 4 + 2 * (n % 2)
                k.dma(xt.ap, xres_v[:, :, t0:t0 + 128], r=[("scr", "xres")], w=[xt])
                osb = xsq.ap.rearrange("p a b -> p (a b)")
                k.op('act', lambda g: g.activation(out=osb, in_=ps[:, ob * 512:(ob + 2) * 512], func=AF.Copy), r=[('ps', ob), ('ps', ob + 1)], w=[xsq])
                for dc in range(8):
                    k.op('pe', lambda g, dc=dc: g.transpose(ps[:, 1024 + dc * 128:1024 + (dc + 1) * 128], osb[:, dc * 128:(dc + 1) * 128], identf.ap),
                         r=[xsq, identf], w=[('ps', 2 + dc // 4)])
                for dc in range(8):
                    k.op('dve', lambda g, dc=dc: g.scalar_tensor_tensor(out=xt.ap[:, dc, :], in0=ps[:, 1024 + dc * 128:1024 + (dc + 1) * 128], scalar=G_f[:, dc:dc + 1],
                                                                        in1=xt.ap[:, dc, :], op0=ALU.mult, op1=ALU.add), r=[('ps', 2 + dc // 4), xt, der], w=[xt])
                if not final:
                    k.dma(xres_v[:, :, t0:t0 + 128], xt.ap, r=[xt], w=[("scr", "xres")])
                else:
                    k.op('act', lambda g: g.activation(out=xsq.ap, in_=xt.ap, func=AF.Square), r=[xt], w=[xsq])
                    rms_rstd(lambda c: xsq.ap[:, c, :], 8, 128, 1.0 / 1024.0, rstd, 0, [xsq])
                    k.op('dve', lambda g: g.tensor_tensor(out=xsq.ap, in0=xt.ap, in1=rstd.ap[:, 0:128].unsqueeze(1).to_broadcast([128, 8, 128]), op=ALU.mult),
                         r=[xt, rstd], w=[xsq])
                    k.op('dve', lambda g: g.tensor_tensor(out=xsq.ap, in0=xsq.ap, in1=vecs.ap[:, G_FIN:G_FIN + 8].unsqueeze(2).to_broadcast([128, 8, 128]), op=ALU.mult),
                         r=[xsq, vecs], w=[xsq])
                    k.dma(yT_v[:, :, t0:t0 + 128], xsq.ap, r=[xsq], w=[("o", "y")])

            def half_loop(s_, gen):
                act_t = [n for n in (s_ - 1, s_) if 0 <= n < NT]
                c0 = (s_ % 2) * HALF
                if s_ < NT:
                    k.dma(hT.ap[:, s_ % 2], hTs_v[:, :, s_ * 128:(s_ + 1) * 128], r=[("scr", "hTs")], w=[('hT', s_ % 2)])
                items = [(p_, n) for p_ in range(HALF) for n in act_t]
                L = len(items)

                def load_u(p_):
                    k.dma(ut[p_ % 3].ap.rearrange("p c f -> p (c f)"), Ubv[c0 + p_], r=[("scr", id(Ubv))], w=[ut[p_ % 3]])

                def load_v(p_):
                    k.dma(vt[p_ % 2].ap.rearrange("p j f -> p (j f)"), Vbv[c0 + p_], r=[("scr", id(Vbv))], w=[vt[p_ % 2]], q='act')

                def m1(qi):
                    p_, n = items[qi]
                    u, gg, ba = ut[p_ % 3], gl[qi % 2], 2 + (qi % 2)
                    pk = ('ps', ba)
                    for dc in range(8):
                        k.op('pe', lambda g, dc=dc: g.matmul(ps[:, ba * 512:ba * 512 + JB * 128], lhsT=hT.ap[:, n % 2, dc, :], rhs=u.ap[:, dc, :],
                                                             start=(dc == 0), stop=(dc == 7)), r=[u, ('hT', n % 2)], w=[pk], inc=(dc == 7))
                    k.op('act', lambda g: g.activation(out=gg.ap, in_=ps[:, ba * 512:ba * 512 + JB * 128], func=AF.Gelu_apprx_tanh), r=[pk], w=[gg])

                def tr(qi):
                    p_, n = items[qi]
                    gg, ww = gl[qi % 2], wa[qi % 3]
                    off = (qi % 2) * 1024
                    pkt = ('ps', qi % 2)
                    wk_ = [pkt]
                    for jj in range(JB):
                        k.op('pe', lambda g, jj=jj: g.transpose(psb[:, off + jj * 128:off + (jj + 1) * 128], gg.ap[:, jj * 128:(jj + 1) * 128], identb.ap),
                             r=[gg, identb], w=wk_, inc=(jj == JB - 1))
                    jg = (c0 + p_) * JB
                    k.op('dve', lambda g: g.tensor_tensor(out=ww.ap, in0=psb[:, off:off + JB * 128].rearrange("p (j t) -> p j t", j=JB),
                                                          in1=W.ap[:, n % 2, :, jg:jg + JB].rearrange("p t j -> p j t"), op=ALU.mult),
                         r=[pkt, ('W', n % 2)], w=[ww])

                def m2(qi):
                    p_, n = items[qi]
                    v_, ww = vt[p_ % 2], wa[qi % 3]
                    first = (n == s_) and p_ == 0
                    last = (n == s_ - 1 or NT == 1) and p_ == HALF - 1
                    for jj in range(JB):
                        for hh in range(2):
                            bo = 4 + 2 * (n % 2) + hh
                            k.op('pe', lambda g, jj=jj, hh=hh, bo=bo: g.matmul(
                                ps[:, bo * 512:(bo + 1) * 512], lhsT=ww.ap[:, jj, :], rhs=v_.ap[:, jj, hh * 512:(hh + 1) * 512],
                                start=(first and jj == 0), stop=(last and jj == JB - 1)), r=[v_, ww], w=[('ps', bo)], inc=(jj == JB - 1 and hh == 1))

                budget = 0.0
                spent = 0.0
                load_u(0)
                load_v(0)
                load_u(1)
                load_v(1)
                m1(0)
                for qi in range(L):
                    p_, n = items[qi]
                    tr(qi)
                    if n == act_t[0] and p_ + 2 < HALF:
                        load_u(p_ + 2)
                    if qi + 1 < L:
                        m1(qi + 1)
                    if qi >= 1:
                        m2(qi - 1)
                        pp, pn = items[qi - 1]
                        if pn == act_t[-1] and pp + 2 < HALF:
                            load_v(pp + 2)
                    if gen is not None:
                        budget += 2.9 * 2.0 / len(act_t) / 2.0
                        while spent < budget:
                            cst_ = next(gen, None)
                            if cst_ is None:
                                break
                            spent += cst_
                m2(L - 1)
                if gen is not None:
                    for _ in gen:
                        pass

            k.bg_wait('sp', l * 2)
            k.bg_wait('sp', l * 2 + 1)
            k.bg_wait('act', l * 2 + 1)
            g0 = route_a(0)
            for _ in g0:
                pass
            route_b(0)
            for s_ in range(NT + 1):
                gen = route_a(s_ + 1) if s_ + 1 < NT else None
                half_loop(s_, gen)
                if s_ - 1 >= 0:
                    tail(s_ - 1)
                if s_ + 1 < NT:
                    route_b(s_ + 1)
            k.barrier()
            k.top = mark

        if debug_stage != "A":
            peer(0, False)

        if debug_stage not in ("A", "P0"):
            mark = k.top
            stage = k.alloc("stageQ", [2048])
            wp_b = k.alloc("wp_b", [4, 2, 256], BF16)
            for gq in range(4):
                for kc in range(2):
                    k.load_cast(wp_b, wp_b.ap[:, gq, kc, :], w_pool[gq, kc * 128:(kc + 1) * 128, :], 256, stage)
            PADW = 8
            for si, (t0, S, cond, has_ctx) in enumerate(SEGS):
                markS = k.top
                A_m = der.ap[:, 1, cond, 0, :]
                B_m = der.ap[:, 1, cond, 1, :]
                G_m = der.ap[:, 1, cond, 2, :]
                G = min(512, S)
                NG = S // G
                hp_ = k.alloc("hpad", [8, S + 2 * PADW])
                dT = k.alloc("dT", [8, S], BF16)
                inv = k.alloc("inv", [4, S])
                xg = k.alloc("xgQ", [8, G])
                xsq = k.alloc("xsqQ", [8, G])
                rstd = k.alloc("rstdQ", [G])
                t1 = k.alloc("t1", [S + 2 * PADW])
                t2 = k.alloc("t2", [S + 2 * PADW])
                k.op('pool', lambda g: g.memset(hp_.ap, 0.0), w=[hp_])
                for gq, wdw in enumerate((2, 4, 8, 16)):
                    k.op('pool', lambda g, gq=gq, wdw=wdw: g.memset(inv.ap[:, gq, :], 1.0 / wdw), w=[inv])
                    lo_h, hi_h = wdw // 2, wdw - wdw // 2
                    for t in range(0, lo_h):
                        cnt = min(t + hi_h, S) - max(t - lo_h, 0)
                        k.op('pool', lambda g, gq=gq, t=t, cnt=cnt: g.memset(inv.ap[:, gq, t:t + 1], 1.0 / cnt), w=[inv])
                    for t in range(S - hi_h + 1, S):
                        cnt = min(t + hi_h, S) - max(t - lo_h, 0)
                        k.op('pool', lambda g, gq=gq, t=t, cnt=cnt: g.memset(inv.ap[:, gq, t:t + 1], 1.0 / cnt), w=[inv])
                for gi in range(NG):
                    g0 = gi * G
                    k.dma(xg.ap, xres_v[:, :, t0 + g0:t0 + g0 + G], r=[("scr", "xres")], w=[xg])
                    norm_mod(xg, G, A_m, B_m, lambda dc: hp_.ap[:, dc, PADW + g0:PADW + g0 + G], hp_, xsq, rstd, gi % 8)
                for dc in range(8):
                    gq = dc // 2
                    wdw = (2, 4, 8, 16)[gq]
                    L = S + 2 * PADW
                    src = hp_.ap[:, dc, :]
                    cur, n = src, 1
                    bufs = [t1, t2]
                    bi = 0
                    while n < wdw:
                        o = bufs[bi]
                        bi ^= 1
                        k.op('dve', lambda g, o=o, cur=cur, n=n, L=L: g.tensor_tensor(out=o.ap[:, 0:L - n], in0=cur[:, 0:L - n], in1=cur[:, n:L], op=ALU.add),
                             r=[hp_, t1, t2], w=[o])
                        cur = o.ap
                        n *= 2
                    st = PADW - wdw // 2
                    k.op('dve', lambda g, cur=cur, st=st, gq=gq: g.tensor_tensor(out=t1.ap[:, 0:S] if cur is not t1.ap else t2.ap[:, 0:S], in0=cur[:, st:st + S], in1=inv.ap[:, gq, :], op=ALU.mult),
                         r=[t1, t2, inv], w=[t1, t2])
                    mres = t1.ap if cur is not t1.ap else t2.ap
                    k.op('dve', lambda g, dc=dc, mres=mres: g.tensor_tensor(out=dT.ap[:, dc, :], in0=mres[:, 0:S], in1=hp_.ap[:, dc, PADW:PADW + S], op=ALU.subtract),
                         r=[t1, t2, hp_], w=[dT])
                xo2 = [xg, xg]
                pb2 = 0
                for gi in range(NG):
                    g0 = gi * G
                    xo = xo2[gi % 2]
                    k.dma(xo.ap, xres_v[:, :, t0 + g0:t0 + g0 + G], r=[("scr", "xres")], w=[xo])
                    for oc in range(8):
                        gq, dd = oc // 2, oc % 2
                        b = pb2 = (pb2 + 1) % 8
                        pk = ('ps', b)
                        for kc in range(2):
                            k.op('pe', lambda g, kc=kc, gq=gq, dd=dd, b=b: g.matmul(ps[:, b * 512:b * 512 + G], lhsT=wp_b.ap[:, gq, kc, dd * 128:(dd + 1) * 128],
                                                                                    rhs=dT.ap[:, gq * 2 + kc, g0:g0 + G], start=(kc == 0), stop=(kc == 1)), r=[wp_b, dT], w=[pk])
                        k.op('act', lambda g, b=b, oc=oc: g.activation(out=xsq.ap[:, 0, 0:G], in_=ps[:, b * 512:b * 512 + G], func=AF.Copy,
                                                                       scale=vecs.ap[:, S_POOL + oc:S_POOL + oc + 1]), r=[pk, vecs], w=[xsq])
                        k.op('dve', lambda g, oc=oc, xo=xo: g.scalar_tensor_tensor(out=xo.ap[:, oc, :], in0=xsq.ap[:, 0, 0:G], scalar=G_m[:, oc:oc + 1], in1=xo.ap[:, oc, :],
                                                                                   op0=ALU.mult, op1=ALU.add), r=[xsq, xo, der], w=[xo])
                    k.dma(xres_v[:, :, t0 + g0:t0 + g0 + G], xo.ap, r=[xo], w=[("scr", "xres")])
                k.barrier()
                k.top = markS
            k.barrier()
            k.top = mark
            if debug_stage != "P1":
                peer(1, True)

        if debug_stage is not None:
            mark = k.top
            xg = k.alloc("xdump", [8, 512])
            for gi in range(T // 512):
                k.dma(xg.ap, xres_v[:, :, gi * 512:(gi + 1) * 512], r=[("scr", "xres")], w=[xg])
                k.dma(yT_v[:, :, gi * 512:(gi + 1) * 512], xg.ap, r=[xg], w=[("o", "y")])
        k.barrier()
    return nc


_PROG = {}


def _pack_vec(v):
    v = np.asarray(v, np.float32).reshape(-1)
    return v.reshape(-1, 128).T


def kernel(**inp):
    debug_stage = inp.pop("_debug_stage", None)
    _trace = inp.pop("_trace", False)
    f = lambda a: np.ascontiguousarray(np.asarray(a, dtype=np.float32))
    x_prompt, x_sample = f(inp["x_prompt"]), f(inp["x_sample"])
    c, c_ctx = f(inp["c"]), f(inp["c_ctx"])
    if debug_stage not in _PROG:
        _PROG[debug_stage] = build_program(debug_stage)
    nc = _PROG[debug_stage]
    ident = np.eye(128, dtype=np.float32)
    pmat = np.zeros((64, 64), np.float32)
    for i in range(32):
        pmat[2 * i + 1, 2 * i] = -1.0
        pmat[2 * i, 2 * i + 1] = 1.0
    n_tok = 2048
    rows = np.repeat(np.arange(n_tok // 64, dtype=np.float32), 64)
    cols = np.tile(np.arange(64, dtype=np.float32), n_tok // 64)
    inv_freq = (np.float32(10000.0) ** (-np.arange(0, 32, 2, dtype=np.float32) / np.float32(32))).astype(np.float32)
    ang = np.concatenate([rows[:, None] * inv_freq, cols[:, None] * inv_freq], axis=-1).astype(np.float32)
    cosT = np.ascontiguousarray(np.repeat(np.cos(ang).T, 2, axis=0).astype(np.float32))
    sinT = np.ascontiguousarray(np.repeat(np.sin(ang).T, 2, axis=0).astype(np.float32))
    shared = {
        "ident": ident, "pmat": pmat, "cosT": cosT, "sinT": sinT,
        "w_mod0": f(inp["w_mod_l0"]), "w_mod1": f(inp["w_mod_l1"]),
        "w_in": f(inp["w_in_l0"]), "w_uq": f(inp["w_uq_l0"]), "w_ukv": f(inp["w_ukv_l0"]),
        "w_rg": f(inp["w_rg_l0"]), "w_ig": f(inp["w_ig_l0"]), "w_o": f(inp["w_o_l0"]),
        "w_pool": f(inp["w_pool_l1"]),
        "wq0": f(inp["peer_wq_l0"]), "wq1": f(inp["peer_wq_l1"]),
    }
    peer_in = ((inp["peer_keys_l0"], inp["peer_u_l0"], inp["peer_v_l0"]), (inp["peer_keys_l1"], inp["peer_u_l1"], inp["peer_v_l1"]))
    for l in range(2):
        keys = f(peer_in[l][0])
        shared["keysT%d" % l] = np.ascontiguousarray(keys.reshape(16, 128, 128).transpose(2, 0, 1))
        u = f(peer_in[l][1])
        v = f(peer_in[l][2])
        shared["Up%d" % l] = np.ascontiguousarray(u.reshape(128, 64, 2, 8, 128).transpose(1, 4, 3, 2, 0)).reshape(64, 128, 2048)
        shared["Vp%d" % l] = np.ascontiguousarray(v.reshape(128, 64, 2, 1024).transpose(1, 0, 2, 3)).reshape(64, 128, 2048)
    in_maps = []
    for core in range(NCORES):
        xcat = np.concatenate([x_sample[core], x_prompt[2 * core], x_prompt[2 * core + 1]], axis=0)
        vecs = np.zeros((128, NV), np.float32)
        vecs[:, C_CS:C_CS + 8] = _pack_vec(c[core])
        vecs[:, C_CP:C_CP + 8] = _pack_vec(c_ctx)
        for col, name in ((G_MIX0, "g_mix_l0"), (G_FFN0, "g_ffn_l0"), (G_MIX1, "g_mix_l1"), (G_FFN1, "g_ffn_l1"), (G_FIN, "g_final"), (S_POOL, "s_pool_l1")):
            vecs[:, col:col + 8] = _pack_vec(inp[name])
        vecs[:, B_MOD0:B_MOD0 + 48] = _pack_vec(inp["b_mod_l0"])
        vecs[:, B_MOD1:B_MOD1 + 48] = _pack_vec(inp["b_mod_l1"])
        vecs[:, G_Q:G_Q + 3] = _pack_vec(inp["g_q_l0"])
        vecs[:, G_KV:G_KV + 2] = _pack_vec(inp["g_kv_l0"])
        cw = f(inp["conv_w_l0"])
        for kk in range(4):
            vecs[:, CONV_W + kk * 4:CONV_W + kk * 4 + 4] = _pack_vec(cw[kk])
        vecs[:, CONV_B:CONV_B + 4] = _pack_vec(inp["conv_b_l0"])
        vecs[:, B_RG:B_RG + 8] = _pack_vec(inp["b_rg_l0"])
        vecs[:, B_IG:B_IG + 8] = _pack_vec(inp["b_ig_l0"])
        vecs[:, LAM:LAM + 8] = _pack_vec(inp["lam_l0"])
        vecs[:, H0:H0 + 8] = _pack_vec(f(inp["state_lru_l0"])[core])
        m = dict(shared)
        m["xT"] = np.ascontiguousarray(xcat.T)
        m["vecs"] = vecs
        m["cckvT"] = np.ascontiguousarray(f(inp["cache_ckv_l0"])[core].T)
        m["ckrT"] = np.ascontiguousarray(f(inp["cache_krope_l0"])[core].T)
        in_maps.append(m)
    if _trace:
        res = run_bass_kernel_spmd(nc, in_maps, core_ids=list(range(NCORES)), trace=True)
        print("EXEC_TIME_NS", res.exec_time_ns)
    else:
        res = run_bass_kernel_spmd(nc, in_maps, core_ids=list(range(NCORES)))
    y_prompt = np.zeros((16, 256, 1024), np.float32)
    y_sample = np.zeros((8, 2048, 1024), np.float32)
    new_ckv = np.zeros((16, 256, 256), np.float32)
    new_kr = np.zeros((16, 256, 64), np.float32)
    new_lru = np.zeros((16, 2, 512), np.float32)
    for core in range(NCORES):
        r = res.results[core]
        y = np.asarray(r["yT"]).T
        y_sample[core] = y[0:2048]
        y_prompt[2 * core] = y[2048:2304]
        y_prompt[2 * core + 1] = y[2304:2560]
        ck = np.asarray(r["ckv_o"]).T
        kr = np.asarray(r["kr_o"]).T
        lr = np.asarray(r["lru_o"])
        for bi in range(2):
            new_ckv[2 * core + bi] = ck[bi * 256:(bi + 1) * 256]
            new_kr[2 * core + bi] = kr[bi * 256:(bi + 1) * 256]
            new_lru[2 * core + bi] = lr[:, bi * 8:(bi + 1) * 8].reshape(128, 2, 4).transpose(1, 2, 0).reshape(2, 512)
    return (y_prompt, y_sample, new_ckv, new_kr, new_lru)
```
